# Optimizing a Trainium2 kernel written in Bass

```python
import math
import jax, jax.numpy as jnp
from jax import lax
import numpy as np

D_MODEL = 1024
BATCH = 8
SEQ = 4096
DEPTH = 1

M_HEADS = 4
M_QK = 128
M_V = 256
M_CHUNK = 128
A_HEADS = 8
A_DH = 64
A_DV = 2 * A_DH
Q_BLOCK = 128
ROPE_THETA = 10000.0
D_FF = 2816
CONV_W = 3
EPS = 1e-6

M_QK_W = M_HEADS * M_QK
M_V_W = M_HEADS * M_V
N_GATE = 4 * M_HEADS
A_QK_W = A_HEADS * 2 * A_DH
A_V_W = A_HEADS * A_DV
SPLITS = (M_QK_W, M_QK_W, M_V_W, M_V_W, N_GATE, A_QK_W, A_QK_W, A_V_W, D_MODEL, D_MODEL)
D_IN = sum(SPLITS)

kernel_name = "hybrid_mlstm_diffattn_convffn_encoder"


def rmsnorm(x, g):
    x32 = x.astype(jnp.float32)
    r = x32 * lax.rsqrt(jnp.mean(x32 * x32, axis=-1, keepdims=True) + EPS)
    return (r * g).astype(x.dtype)


def rope_tables(seq, dim):
    pos = jnp.arange(seq, dtype=jnp.float32)
    inv = ROPE_THETA ** (-jnp.arange(0, dim, 2, dtype=jnp.float32) / dim)
    ang = pos[:, None] * inv[None, :]
    return jnp.cos(ang), jnp.sin(ang)


def apply_rope(t, cos, sin):
    t1, t2 = jnp.split(t, 2, axis=-1)
    out = jnp.concatenate([t1 * cos - t2 * sin, t1 * sin + t2 * cos], axis=-1)
    return out.astype(t.dtype)


def mlstm_chunkwise(q, k, v, i_pre, f_pre):
    B, H, S, dk = q.shape
    dv = v.shape[-1]
    L = M_CHUNK
    nc = S // L
    q = q.reshape(B, H, nc, L, dk)
    k = k.reshape(B, H, nc, L, dk)
    v = v.reshape(B, H, nc, L, dv)
    i_pre = i_pre.reshape(B, H, nc, L)
    logf = jax.nn.log_sigmoid(f_pre.reshape(B, H, nc, L))
    b = jnp.cumsum(logf, axis=-1)
    g = b[..., -1]
    a = g[..., None] - b + i_pre
    m_loc = jnp.max(a, axis=-1)
    w = jnp.exp(a - m_loc[..., None])
    kw = k * w[..., None]
    kv = jnp.einsum('bhcld,bhcle->bhcde', kw, v)
    ksum = jnp.sum(kw, axis=3)

    def step(carry, xs):
        C, n, m = carry
        kv_c, ks_c, g_c, ml_c = xs
        m_new = jnp.maximum(g_c + m, ml_c)
        a1 = jnp.exp(g_c + m - m_new)
        a2 = jnp.exp(ml_c - m_new)
        C_new = a1[..., None, None] * C + a2[..., None, None] * kv_c
        n_new = a1[..., None] * n + a2[..., None] * ks_c
        return (C_new, n_new, m_new), (C, n, m)

    init = (jnp.zeros((B, H, dk, dv), jnp.float32), jnp.zeros((B, H, dk), jnp.float32),
            jnp.zeros((B, H), jnp.float32))
    xs = (jnp.moveaxis(kv, 2, 0), jnp.moveaxis(ksum, 2, 0), jnp.moveaxis(g, 2, 0), jnp.moveaxis(m_loc, 2, 0))
    _, (C_prev, n_prev, m_prev) = lax.scan(step, init, xs)
    C_prev = jnp.moveaxis(C_prev, 0, 2)
    n_prev = jnp.moveaxis(n_prev, 0, 2)
    m_prev = jnp.moveaxis(m_prev, 0, 2)

    D = b[..., :, None] - b[..., None, :] + i_pre[..., None, :]
    mask = jnp.tril(jnp.ones((L, L), dtype=bool))
    D = jnp.where(mask, D, -jnp.inf)
    inter_log = b + m_prev[..., None]
    m_t = jnp.maximum(jnp.max(D, axis=-1), inter_log)
    s = jnp.einsum('bhcld,bhcsd->bhcls', q, k) * jnp.exp(D - m_t[..., None])
    inter_w = jnp.exp(inter_log - m_t)
    num = jnp.einsum('bhcls,bhcse->bhcle', s, v) + inter_w[..., None] * jnp.einsum('bhcld,bhcde->bhcle', q, C_prev)
    den = jnp.sum(s, axis=-1) + inter_w * jnp.einsum('bhcld,bhcd->bhcl', q, n_prev)
    h = num / jnp.maximum(jnp.abs(den), jnp.exp(-m_t))[..., None]
    return h.reshape(B, H, S, dv)


def token_mixer(xn, w_in, gate_bias, m_out_norm, q_norm, k_norm, lq1, lk1, lq2, lk2,
                a_out_norm, p_a, p_b, w_o, cos, sin, lam_init):
    B, S, _ = xn.shape
    f32 = jnp.float32
    proj = xn @ w_in
    mq, mk, mv, mo, mg, aq, ak, av, ga, gb = jnp.split(proj, np.cumsum(SPLITS)[:-1].tolist(), axis=-1)

    def heads(t, h, d):
        return t.reshape(B, S, h, d).transpose(0, 2, 1, 3)

    q = heads(mq, M_HEADS, M_QK).astype(f32)
    k = heads(mk, M_HEADS, M_QK).astype(f32) * (M_QK ** -0.5)
    v = heads(mv, M_HEADS, M_V).astype(f32)
    gates = (mg.astype(f32) + gate_bias.astype(f32)).reshape(B, S, 4, M_HEADS).transpose(2, 0, 3, 1)
    h_fwd = mlstm_chunkwise(q, k, v, gates[0], gates[1])
    flip = lambda t: jnp.flip(t, axis=2)
    h_bwd = flip(mlstm_chunkwise(flip(q), flip(k), flip(v), flip(gates[2]), flip(gates[3])))
    hm = (h_fwd + h_bwd).transpose(0, 2, 1, 3)
    hm = rmsnorm(hm, m_out_norm.reshape(M_HEADS, M_V)).reshape(B, S, M_V_W)
    h_a = (hm * jax.nn.sigmoid(mo.astype(f32))).astype(xn.dtype)

    qa = aq.reshape(B, S, A_HEADS, 2, A_DH).transpose(0, 2, 3, 1, 4)
    ka = ak.reshape(B, S, A_HEADS, 2, A_DH).transpose(0, 2, 3, 1, 4)
    qa = apply_rope(rmsnorm(qa, q_norm), cos, sin) * (A_DH ** -0.5)
    ka = apply_rope(rmsnorm(ka, k_norm), cos, sin)
    va = heads(av, A_HEADS, A_DV)
    lam = (jnp.exp(jnp.sum(lq1.astype(f32) * lk1.astype(f32))) -
           jnp.exp(jnp.sum(lq2.astype(f32) * lk2.astype(f32))) + lam_init)
    nb = S // Q_BLOCK
    qb = jnp.moveaxis(qa.reshape(B, A_HEADS, 2, nb, Q_BLOCK, A_DH), 3, 0)

    def block(qi):
        sc = jnp.einsum('bhcqd,bhckd->bhcqk', qi, ka).astype(f32)
        p = jax.nn.softmax(sc, axis=-1)
        wgt = (p[:, :, 0] - lam * p[:, :, 1]).astype(va.dtype)
        return jnp.einsum('bhqk,bhkv->bhqv', wgt, va)

    o = lax.map(block, qb)
    o = o.transpose(1, 0, 3, 2, 4).reshape(B, S, A_HEADS, A_DV)
    o = rmsnorm(o, a_out_norm) * (1.0 - lam_init)
    h_b = o.reshape(B, S, A_V_W).astype(xn.dtype)

    y = jax.nn.sigmoid(ga) * (h_a @ p_a) + jax.nn.sigmoid(gb) * (h_b @ p_b)
    return y @ w_o


def conv_ffn(xn, w_up, conv_w, conv_b, w_down):
    u = xn @ w_up
    u = lax.conv_general_dilated(u, conv_w, window_strides=(1,), padding='SAME',
                                 dimension_numbers=('NWC', 'WIO', 'NWC'),
                                 feature_group_count=u.shape[-1]) + conv_b
    a, g = jnp.split(u, 2, axis=-1)
    return (jax.nn.gelu(g) * a) @ w_down


def setup_inputs(seed: int = 0) -> dict:
    key = jax.random.key(seed)
    ks = jax.random.split(key, 24)
    f32 = jnp.float32

    def nrm(k, shape, scale):
        return jax.random.normal(k, shape, f32) * scale

    fbias = jnp.linspace(3.0, 6.0, M_HEADS, dtype=f32)
    gb_noise = nrm(ks[3], (DEPTH, 4, M_HEADS), 0.1)
    gate_bias = (gb_noise + jnp.stack([jnp.zeros_like(fbias), fbias, jnp.zeros_like(fbias), fbias])[None]).reshape(DEPTH, N_GATE)
    return {
        "x": nrm(ks[0], (BATCH, SEQ, D_MODEL), 1.0),
        "norm1": 1.0 + nrm(ks[1], (DEPTH, D_MODEL), 0.02),
        "w_in": nrm(ks[2], (DEPTH, D_MODEL, D_IN), D_MODEL ** -0.5),
        "gate_bias": gate_bias,
        "m_out_norm": 1.0 + nrm(ks[4], (DEPTH, M_V_W), 0.02),
        "q_norm": 1.0 + nrm(ks[5], (DEPTH, A_DH), 0.02),
        "k_norm": 1.0 + nrm(ks[6], (DEPTH, A_DH), 0.02),
        "lam_q1": nrm(ks[7], (DEPTH, A_DH), 0.1),
        "lam_k1": nrm(ks[8], (DEPTH, A_DH), 0.1),
        "lam_q2": nrm(ks[9], (DEPTH, A_DH), 0.1),
        "lam_k2": nrm(ks[10], (DEPTH, A_DH), 0.1),
        "a_out_norm": 1.0 + nrm(ks[11], (DEPTH, A_DV), 0.02),
        "p_a": nrm(ks[12], (DEPTH, M_V_W, D_MODEL), M_V_W ** -0.5),
        "p_b": nrm(ks[13], (DEPTH, A_V_W, D_MODEL), A_V_W ** -0.5),
        "w_o": nrm(ks[14], (DEPTH, D_MODEL, D_MODEL), D_MODEL ** -0.5),
        "norm2": 1.0 + nrm(ks[15], (DEPTH, D_MODEL), 0.02),
        "w_up": nrm(ks[16], (DEPTH, D_MODEL, 2 * D_FF), D_MODEL ** -0.5),
        "conv_w": nrm(ks[17], (DEPTH, CONV_W, 1, 2 * D_FF), CONV_W ** -0.5),
        "conv_b": nrm(ks[18], (DEPTH, 2 * D_FF), 0.02),
        "w_down": nrm(ks[19], (DEPTH, D_FF, D_MODEL), D_FF ** -0.5),
    }


def reference(x, norm1, w_in, gate_bias, m_out_norm, q_norm, k_norm, lam_q1, lam_k1, lam_q2, lam_k2,
              a_out_norm, p_a, p_b, w_o, norm2, w_up, conv_w, conv_b, w_down):
    S = x.shape[1]
    cos, sin = rope_tables(S, A_DH)
    for l in range(DEPTH):
        lam_init = 0.8 - 0.6 * math.exp(-0.3 * l)
        xn = rmsnorm(x, norm1[l])
        x = x + token_mixer(xn, w_in[l], gate_bias[l], m_out_norm[l], q_norm[l], k_norm[l],
                            lam_q1[l], lam_k1[l], lam_q2[l], lam_k2[l], a_out_norm[l],
                            p_a[l], p_b[l], w_o[l], cos, sin, lam_init)
        xn = rmsnorm(x, norm2[l])
        x = x + conv_ffn(xn, w_up[l], conv_w[l], conv_b[l], w_down[l])
    return x
```

```python
import math
from contextlib import ExitStack

import numpy as np
import concourse.bass as bass
import concourse.mybir as mybir
from concourse.bass_utils import run_bass_kernel_spmd

F32 = mybir.dt.float32
BF16 = mybir.dt.bfloat16
AF = mybir.ActivationFunctionType
ALU = mybir.AluOpType
AX = mybir.AxisListType

T = 4096
D = 1024
NT = 32
KC = 8
DIN = 8208
OFF_MQ, OFF_MK, OFF_MV, OFF_MO, OFF_MG = 0, 512, 1024, 2048, 3072
OFF_AQ, OFF_AK, OFF_AV, OFF_GA, OFF_GB = 3088, 4112, 5136, 6160, 7184
DFF = 2816
NJ = 22
EPS = 1e-6
LAM_INIT = 0.8 - 0.6 * math.exp(-0.3 * 0)
ARENA_BYTES = 206 * 1024

ENGS = ("pe", "act", "dve", "pool", "sp")


class Op:
    __slots__ = ("eng", "fn", "pidx", "pos", "sig", "dk", "ordn", "waits", "cnt", "cost", "deps", "users",
                 "nd", "rt", "fin", "group", "lastpos", "dmacnt", "nbytes")

    def __init__(self, eng, fn, dk, cost):
        self.eng = eng
        self.fn = fn
        self.dk = dk
        self.cost = cost
        self.sig = False
        self.ordn = 0
        self.waits = []
        self.cnt = 0
        self.deps = []
        self.users = []
        self.nd = 0
        self.rt = 0.0
        self.fin = 0.0
        self.pos = -1
        self.group = None
        self.nbytes = 0


USE_DRAIN = False
MAX_DMA_INFLIGHT = 6
NDUM = 0
PSUM_EXCL = True
SEM_LAT = 300.0
DMA_LAT = 2000.0
DMA_BW = 160.0


class Sched:
    def __init__(self):
        self.all = []
        self.lastw = {}
        self.readers = {}
        self.dma_n = {}
        self.dma_last = {}
        self.cur_bar = None
        self.group = []
        self.order = None
        self.last_psr = {}
        self.dma_hist = []

    def add(self, eng, fn, r=(), w=(), dk=None, cost=100.0, nbytes=0):
        op = Op(eng, fn, dk, cost)
        op.pidx = len(self.all)
        op.nbytes = nbytes
        deps = {}
        psr = False

        def dep(d, raw):
            if d is op:
                return
            if id(d) in deps:
                if raw:
                    deps[id(d)] = (d, True)
            else:
                deps[id(d)] = (d, raw)

        for k in r:
            d = self.lastw.get(k)
            if d is not None:
                dep(d, True)
            if PSUM_EXCL and k[:2] == "ps" and eng in ("act", "dve"):
                psr = True
        if psr:
            other = "dve" if eng == "act" else "act"
            d = self.last_psr.get(other)
            if d is not None:
                dep(d, True)
            self.last_psr[eng] = op
        for k in w:
            d = self.lastw.get(k)
            if d is not None:
                dep(d, False)
            for d2 in self.readers.get(k, ()):
                dep(d2, False)
        if dk is not None:
            d = self.dma_last.get(dk)
            if d is not None:
                dep(d, True)
            self.dma_hist.append(op)
            if len(self.dma_hist) > MAX_DMA_INFLIGHT:
                dep(self.dma_hist[-1 - MAX_DMA_INFLIGHT], True)
            self.dma_n[dk] = self.dma_n.get(dk, 0) + 1
            op.ordn = self.dma_n[dk]
            self.dma_last[dk] = op
        if self.cur_bar is not None:
            dep(self.cur_bar, True)
        op.deps = list(deps.values())
        for d, _ in op.deps:
            d.users.append(op)
        self.all.append(op)
        self.group.append(op)
        for k in r:
            self.readers.setdefault(k, []).append(op)
        for k in w:
            self.lastw[k] = op
            self.readers[k] = []
        return op

    def barrier(self):
        b = Op("bar", None, None, 0.0)
        b.pidx = len(self.all)
        b.group = self.group
        b.deps = [(o, True) for o in self.group]
        if self.cur_bar is not None:
            b.deps.append((self.cur_bar, True))
        for d, _ in b.deps:
            d.users.append(b)
        self.all.append(b)
        self.group = []
        self.cur_bar = b
        self.lastw = {}
        self.readers = {}

    def schedule(self):
        import heapq
        engs = ENGS + ("bar",)
        order = {e: [] for e in engs}
        readyq = {e: [] for e in engs}
        busy = {e: False for e in engs}
        ev = []
        seq = 0
        dma_free = 0.0
        for o in self.all:
            o.nd = len(o.deps)
            o.rt = 0.0
            if o.nd == 0:
                heapq.heappush(ev, (0.0, seq, 0, o))
                seq += 1

        def start(e, t):
            nonlocal seq, dma_free
            o = heapq.heappop(readyq[e])[1]
            busy[e] = True
            o.rt = t
            o.pos = len(order[e])
            order[e].append(o)
            if o.dk is not None:
                tend = t + 60.0
                dma_free = max(dma_free, t) + o.nbytes / DMA_BW
                o.fin = dma_free + DMA_LAT
            else:
                tend = t + o.cost
                o.fin = tend
            heapq.heappush(ev, (tend, seq, 1, e))
            seq += 1
            for u in o.users:
                u.nd -= 1
                lat = 0.0 if (u.eng == o.eng and o.dk is None) else SEM_LAT
                if o.fin + lat > u.rt:
                    u.rt = o.fin + lat
                if u.nd == 0:
                    heapq.heappush(ev, (u.rt, seq, 0, u))
                    seq += 1

        while ev:
            t, _, kind, x = heapq.heappop(ev)
            if kind == 0:
                e = x.eng
                heapq.heappush(readyq[e], (x.pidx, x))
                if not busy[e]:
                    start(e, t)
            else:
                busy[x] = False
                if readyq[x]:
                    start(x, t)
        assert sum(len(v) for v in order.values()) == len(self.all), "scheduler: dependency cycle"
        self.order = order
        self.est_ns = max(o.fin for o in self.all)
        last = {}
        dcnt = {}
        for b in order["bar"]:
            for o in b.group:
                if o.dk is not None:
                    dcnt[o.dk] = max(dcnt.get(o.dk, 0), o.ordn)
                elif o.fn is not None:
                    p = last.get(o.eng)
                    if p is None or o.pos > p.pos:
                        last[o.eng] = o
            b.lastpos = dict(last)
            b.dmacnt = dict(dcnt)
        import bisect
        kn = {e: {} for e in ENGS}
        hist = {e: {} for e in ENGS}
        last_drain = {e: -1 for e in ENGS}

        def learn(e, p, key, val):
            if kn[e].get(key, 0) < val:
                kn[e][key] = val
                h = hist[e].setdefault(key, ([], []))
                h[0].append(p)
                h[1].append(val)

        def want(op, key, val, dop):
            e = op.eng
            if kn[e].get(key, 0) >= val:
                return
            if dop is not None:
                dop.sig = True
                op.waits.append(dop)
            else:
                op.waits.append((key[1], val))
            learn(e, op.pos, key, val)
            if dop is not None:
                F = dop.eng
                for k2, h in hist[F].items():
                    i = bisect.bisect_right(h[0], dop.pos) - 1
                    if i >= 0:
                        learn(e, op.pos, k2, h[1][i])

        seq_ops = sorted((o for o in self.all if o.eng != "bar"), key=lambda o: (o.rt, o.pidx))
        for op in seq_ops:
            e = op.eng
            for d, raw in op.deps:
                if d.eng == "bar":
                    for F, lo in d.lastpos.items():
                        if F != e:
                            want(op, ("eng", F), lo.pos + 1, lo)
                    for dk_, n_ in d.dmacnt.items():
                        want(op, ("dma", dk_), n_, None)
                elif d.dk is not None:
                    want(op, ("dma", d.dk), d.ordn, None)
                elif d.eng == e and op.dk is None:
                    if e != "pe" and raw:
                        if USE_DRAIN:
                            if d.pos > last_drain[e]:
                                op.waits.append("DRAIN")
                                last_drain[e] = op.pos
                        else:
                            want(op, ("eng", e), d.pos + 1, d)
                else:
                    want(op, ("eng", d.eng), d.pos + 1, d)
        for e in ENGS:
            c = 0
            for o in order[e]:
                if o.dk is None and o.sig:
                    c += 1
                o.cnt = c

    def emit(self, eng, e, esem, dsem):
        for o in self.order[eng]:
            for wt in o.waits:
                if wt == "DRAIN":
                    e.drain()
                elif isinstance(wt, tuple):
                    e.wait_ge(dsem[wt[0]], 16 * wt[1])
                else:
                    e.wait_ge(esem[wt.eng], wt.cnt)
            if o.fn is None:
                continue
            ins = o.fn(e)
            if o.dk is not None:
                ins.then_inc(dsem[o.dk], 16)
            elif o.sig:
                ins.then_inc(esem[eng], 1)


class Arena:
    def __init__(self, t, nbytes, base=0):
        self.t = t
        self.nbytes = nbytes
        self.off = base

    def alloc(self, shape, dt):
        n = 1
        for s in shape[1:]:
            n *= s
        nb = n * (4 if dt == F32 else 2)
        nb = (nb + 31) // 32 * 32
        o = self.off
        assert o + nb <= self.nbytes, f"arena overflow {o}+{nb}>{self.nbytes}"
        self.off = o + nb
        a = self.t[:, o // 4:(o + nb) // 4]
        if dt != F32:
            a = a.bitcast(dt)
        a = a[:, 0:n]
        if len(shape) == 3:
            a = a.rearrange("p (a b) -> p a b", a=shape[1])
        elif len(shape) == 4:
            a = a.rearrange("p (a b c) -> p a b c", a=shape[1], b=shape[2])
        return a


def build(debug=False, stop_after=None, skip=(), cstage=99, var=0):
    nc = bass.Bass("TRN2", target_bir_lowering=False)

    def din(name, shape, dt=F32):
        return nc.dram_tensor(name, list(shape), dt, kind="ExternalInput").ap()

    x = din("x", [T, D])
    w_in = din("w_in", [D, DIN])
    p_a = din("p_a", [D, D])
    p_b = din("p_b", [D, D])
    w_o = din("w_o", [D, D])
    w_up = din("w_up", [D, 2 * DFF])
    w_down = din("w_down", [DFF, D])
    norm1 = din("norm1", [1, D])
    norm2 = din("norm2", [1, D])
    m_out_norm = din("m_out_norm", [1, D])
    gate_bias = din("gate_bias", [1, 16])
    q_norm = din("q_norm", [1, 64])
    k_norm = din("k_norm", [1, 64])
    lam4 = din("lam4", [1, 256])
    a_out_norm = din("a_out_norm", [1, 128])
    conv_wp = din("conv_wp", [128, 44 * 3])
    conv_bp = din("conv_bp", [128, 44])
    cst = din("cst", [128, 4 * 128])
    rope = din("rope", [128, 2 * 32 * 64])
    skind = "ExternalOutput" if debug else "Internal"
    haT = nc.dram_tensor("haT", [D, T], BF16, kind=skind).ap()
    hbT = nc.dram_tensor("hbT", [D, T], BF16, kind=skind).ap()
    yTs = nc.dram_tensor("yTs", [D, T], BF16, kind=skind).ap()
    x1s = nc.dram_tensor("x1s", [T, D], F32, kind=skind).ap()
    xn2T = nc.dram_tensor("xn2T", [D, T], BF16, kind=skind).ap()
    out = nc.dram_tensor("out", [T, D], F32, kind="ExternalOutput").ap()

    w_in_v = w_in.rearrange("(c p) n -> p c n", p=128)
    haT_v = haT.rearrange("(f p) t -> p f t", p=128)
    hbT_v = hbT.rearrange("(f p) t -> p f t", p=128)
    yT_v = yTs.rearrange("(f p) t -> p f t", p=128)
    xn2T_v = xn2T.rearrange("(f p) t -> p f t", p=128)

    S = Sched()
    ctx = ExitStack()
    arena_t = ctx.enter_context(nc.sbuf_tensor("arena", [128, ARENA_BYTES // 4], F32))
    psall = ctx.enter_context(nc.psum_tensor("psall", [128, 4096], F32))
    PS = [psall[:, b * 512:(b + 1) * 512] for b in range(8)]
    AR = Arena(arena_t, ARENA_BYTES)

    def fsz(ap):
        n = 1
        for d_ in ap.shape[1:]:
            n *= d_
        return n

    def mm(o, lhsT, rhs, start, stop, r=(), w=(), sgc=False, cost=None):
        n = fsz(rhs)
        c = max(64, n) / 3.2 + 15.0
        if lhsT.dtype == F32:
            c *= 4
        if cost is not None:
            c = cost
        if sgc:
            return S.add("pe", lambda e: e.matmul(o, lhsT=lhsT, rhs=rhs, start=start, stop=stop, skip_group_check=True),
                         r, w, cost=c)
        return S.add("pe", lambda e: e.matmul(o, lhsT=lhsT, rhs=rhs, start=start, stop=stop), r, w, cost=c)

    def tr(o, in_, ident, r=(), w=()):
        return S.add("pe", lambda e: e.transpose(o, in_, ident), r, w, cost=110.0)

    def act(o, in_, func, r=(), w=(), bias=None, scale=None, accum=None):
        kw = {}
        c = fsz(in_) / 1.15 + 110.0
        if bias is not None:
            kw["bias"] = bias
            if not isinstance(bias, float):
                c += 90.0
        if scale is not None:
            kw["scale"] = scale
            if not isinstance(scale, float):
                c += 90.0
        if accum is not None:
            kw["accum_out"] = accum
            c += 90.0
        return S.add("act", lambda e: e.activation(out=o, in_=in_, func=func, **kw), r, w, cost=c)

    def ecost(eng, n):
        return (2.2 * n + 120.0) if eng == "pool" else (n / 0.7 + 100.0)

    def ts(eng, o, in0, s1, op0, r=(), w=(), s2=None, op1=None):
        c = ecost(eng, fsz(o))
        if op1 is None:
            return S.add(eng, lambda e: e.tensor_scalar(out=o, in0=in0, scalar1=s1, scalar2=None, op0=op0), r, w, cost=c)
        return S.add(eng, lambda e: e.tensor_scalar(out=o, in0=in0, scalar1=s1, scalar2=s2, op0=op0, op1=op1), r, w, cost=c)

    def tt(eng, o, in0, in1, op, r=(), w=()):
        return S.add(eng, lambda e: e.tensor_tensor(out=o, in0=in0, in1=in1, op=op), r, w, cost=ecost(eng, fsz(o)))

    def stt(o, in0, sc, in1, op0, op1, r=(), w=()):
        return S.add("dve", lambda e: e.scalar_tensor_tensor(out=o, in0=in0, scalar=sc, in1=in1, op0=op0, op1=op1), r, w,
                     cost=ecost("dve", fsz(o)))

    def red(o, in_, op, r=(), w=(), negate=None):
        c = ecost("dve", fsz(in_))
        if negate:
            return S.add("dve", lambda e: e.tensor_reduce(out=o, in_=in_, axis=AX.X, op=op, negate=True), r, w, cost=c)
        return S.add("dve", lambda e: e.tensor_reduce(out=o, in_=in_, axis=AX.X, op=op), r, w, cost=c)

    def recip(o, in_, r=(), w=()):
        return S.add("dve", lambda e: e.reciprocal(out=o, in_=in_), r, w, cost=2.0 * fsz(o) + 70.0)

    def cp(eng, o, in_, r=(), w=()):
        if eng == "act":
            return S.add(eng, lambda e: e.activation(out=o, in_=in_, func=AF.Copy), r, w, cost=fsz(o) / 1.15 + 110.0)
        return S.add(eng, lambda e: e.tensor_copy(out=o, in_=in_), r, w, cost=ecost(eng, fsz(o)))

    def mset(eng, o, val, r=(), w=()):
        return S.add(eng, lambda e: e.memset(o, val), r, w, cost=ecost(eng, fsz(o)))

    def dma(o, in_, dk, r=(), w=()):
        nb = fsz(o) * o.shape[0] * (4 if o.dtype == F32 else 2)
        return S.add("sp", lambda e: e.dma_start(out=o, in_=in_), r, w, dk=dk, nbytes=nb)

    def psb(b, n0, n1):
        return PS[b][:, :].bitcast(BF16)[:, n0:n1]

    cstt = AR.alloc([128, 4, 128], F32)
    identf = cstt[:, 0, :]
    onesf = cstt[:, 1, :]
    U = [cstt[:, 2, :], cstt[:, 3, :]]
    identb = AR.alloc([128, 128], BF16)
    smallv = AR.alloc([128, 64], F32)
    base_consts = AR.off
    xnT = AR.alloc([128, KC, T], BF16)
    base_persist = AR.off

    dma(cstt, cst.rearrange("p (a b) -> p a b", a=4), "cst", w=["cst"])
    cp("dve", identb, identf, r=["cst"], w=["identb"])

    xt = [AR.alloc([128, D], F32) for _ in range(2)]
    g1bc = AR.alloc([128, D], F32)
    junk = AR.alloc([128, D], BF16)
    xs = [AR.alloc([128, D], BF16) for _ in range(2)]
    ssq = AR.alloc([128, 3 * NT], F32)
    dma(g1bc, norm1.partition_broadcast(128), "g1bc", w=["g1bc"])
    for i in range(NT):
        b = i % 2
        dma(xt[b], x[i * 128:(i + 1) * 128, :], f"xt{b}", w=[f"xt{b}"])
        act(junk, xt[b], AF.Square, r=[f"xt{b}"], w=["junk", f"ss{i}"], accum=ssq[:, i:i + 1])
        act(ssq[:, NT + i:NT + i + 1], ssq[:, i:i + 1], AF.Ln, r=[f"ss{i}"], w=[f"ln{i}"], bias=EPS, scale=1.0 / D)
        act(ssq[:, 2 * NT + i:2 * NT + i + 1], ssq[:, NT + i:NT + i + 1], AF.Exp, r=[f"ln{i}"], w=[f"rs{i}"], scale=-0.5)
        stt(xs[b], xt[b], ssq[:, 2 * NT + i:2 * NT + i + 1], g1bc, ALU.mult, ALU.mult,
            r=[f"xt{b}", f"rs{i}", "g1bc"], w=[f"xs{b}"])
        for c in range(KC):
            tr(psb(b, c * 128, (c + 1) * 128), xs[b][:, c * 128:(c + 1) * 128], identb,
               r=[f"xs{b}", "identb"], w=[f"ps{b}"])
        cp("dve", xnT[:, :, i * 128:(i + 1) * 128],
           psb(b, 0, 1024).rearrange("p (c t) -> p c t", c=KC), r=[f"ps{b}"], w=[])
    S.barrier()
    if stop_after == "A":
        return finish(nc, S, ctx, out, debug_dump=(xnT, haT_v))

    AR.off = base_persist
    cs2 = AR.alloc([128, 2, 32, 64], F32)
    wst = AR.alloc([128, KC, 384], F32)
    wbf = [AR.alloc([128, KC, 384], BF16) for _ in range(2)]
    qkT = [AR.alloc([128, 2, T], BF16) for _ in range(2)]
    vaug = [AR.alloc([128, NT, 130], BF16) for _ in range(2)]
    pjs = [AR.alloc([128, 384], F32) for _ in range(3)]
    sqb = [AR.alloc([128, 256], F32) for _ in range(3)]
    xnq = [AR.alloc([128, 256], F32) for _ in range(3)]
    rA = [AR.alloc([128, 256], F32) for _ in range(3)]
    rB = [AR.alloc([128, 256], F32) for _ in range(3)]
    qkb = [AR.alloc([128, 256], BF16) for _ in range(3)]
    st4 = [AR.alloc([128, 12], F32) for _ in range(3)]
    pT = [AR.alloc([128, 1024], BF16) for _ in range(3)]
    gqk = AR.alloc([128, 256], F32)
    aon = AR.alloc([128, 128], F32)
    lamt = AR.alloc([128, 256], F32)
    lamp = AR.alloc([128, 128], F32)
    accs = [AR.alloc([128, 9, 129], F32) for _ in range(2)]
    od4 = [AR.alloc([128, 4, 128], F32) for _ in range(2)]
    t24 = AR.alloc([128, 4, 128], F32)
    sq4 = AR.alloc([128, 4, 128], F32)
    ob4 = [AR.alloc([128, 4, 128], BF16) for _ in range(2)]
    e8 = [AR.alloc([128, 24], F32) for _ in range(2)]
    hbst = [AR.alloc([128, 512], BF16) for _ in range(2)]

    dma(cs2, rope.rearrange("p (a i d) -> p a i d", a=2, i=32), "cs2", w=["cs2"])
    for g in range(4):
        dma(gqk[:, g * 64:(g + 1) * 64], (q_norm if g < 2 else k_norm).partition_broadcast(128), "gqk", w=[f"gqk{g}"])
    ts("dve", gqk[:, 0:128], gqk[:, 0:128], 0.125, ALU.mult, r=["gqk0", "gqk1"], w=["gqk0", "gqk1"])
    dma(aon, a_out_norm.partition_broadcast(128), "aon", w=["aon"])
    ts("dve", aon, aon, 1.0 - LAM_INIT, ALU.mult, r=["aon"], w=["aon"])
    dma(lamt, lam4.partition_broadcast(128), "lamt", w=["lamt"])
    tt("dve", lamp[:, 0:64], lamt[:, 0:64], lamt[:, 64:128], ALU.mult, r=["lamt"], w=["lamp"])
    tt("dve", lamp[:, 64:128], lamt[:, 128:192], lamt[:, 192:256], ALU.mult, r=["lamt"], w=["lamp"])
    red(smallv[:, 0:2], lamp.rearrange("p (a b) -> p a b", a=2), ALU.add, r=["lamp"], w=["lam0"])
    act(smallv[:, 4:6], smallv[:, 0:2], AF.Exp, r=["lam0"], w=["lam1"])
    tt("dve", smallv[:, 2:3], smallv[:, 4:5], smallv[:, 5:6], ALU.subtract, r=["lam1"], w=["lam2"])
    ts("dve", smallv[:, 3:4], smallv[:, 2:3], LAM_INIT, ALU.add, r=["lam2"], w=["neglam"], s2=-1.0, op1=ALU.mult)
    neglam = smallv[:, 3:4]
    for sl in range(2):
        mset("dve", vaug[sl][:, :, 128:130], 1.0, w=[f"vaug1_{sl}"])

    def load_head_w(h):
        sl = h % 2
        for n, off in enumerate((OFF_AQ, OFF_AK, OFF_AV)):
            dma(wst[:, :, n * 128:(n + 1) * 128], w_in_v[:, :, off + h * 128:off + (h + 1) * 128],
                "wst", w=[f"wst_{n}"])
        cp("pool", wbf[sl], wst, r=[f"wst_{n}" for n in range(3)], w=[f"wbf{sl}"])

    def b2_tile(h, i, pb=7):
        sl = h % 2
        b = i % 3
        pj = PS[pb]
        for c in range(KC):
            mm(pj[:, 0:384], xnT[:, c, i * 128:(i + 1) * 128], wbf[sl][:, c, :], c == 0, c == KC - 1,
               r=[f"wbf{sl}"], w=[f"ps{pb}"])
        cp("act", pjs[b], pj[:, 0:384], r=[f"ps{pb}"], w=[f"pjs{b}"])
        cp("pool", vaug[sl][:, i, 0:128], pjs[b][:, 256:384], r=[f"pjs{b}"], w=[f"vaug{sl}_{i}"])
        tt("pool", sqb[b], pjs[b][:, 0:256], pjs[b][:, 0:256], ALU.mult, r=[f"pjs{b}"], w=[f"sqb{b}"])
        red(st4[b][:, 0:4], sqb[b].rearrange("p (g d) -> p g d", g=4), ALU.add, r=[f"sqb{b}"], w=[f"st4a{b}"])
        act(st4[b][:, 4:8], st4[b][:, 0:4], AF.Ln, r=[f"st4a{b}"], w=[f"st4b{b}"], bias=EPS, scale=1.0 / 64)
        act(st4[b][:, 8:12], st4[b][:, 4:8], AF.Exp, r=[f"st4b{b}"], w=[f"st4c{b}"], scale=-0.5)
        for g in range(4):
            stt(xnq[b][:, g * 64:(g + 1) * 64], pjs[b][:, g * 64:(g + 1) * 64], st4[b][:, 8 + g:9 + g],
                gqk[:, g * 64:(g + 1) * 64], ALU.mult, ALU.mult,
                r=[f"pjs{b}", f"st4c{b}", f"gqk{g}"], w=[f"xnq{b}"])
        x3 = xnq[b].rearrange("p (g d) -> p g d", g=4)
        a3 = rA[b].rearrange("p (g d) -> p g d", g=4)
        b3 = rB[b].rearrange("p (g d) -> p g d", g=4)
        cosb = cs2[:, 0, i, :].unsqueeze(1).broadcast_to([128, 4, 64])
        sin_lo = cs2[:, 1, i, 0:32].unsqueeze(1).broadcast_to([128, 4, 32])
        sin_hi = cs2[:, 1, i, 32:64].unsqueeze(1).broadcast_to([128, 4, 32])
        tt("dve", a3, x3, cosb, ALU.mult, r=[f"xnq{b}", "cs2"], w=[f"rA{b}"])
        tt("pool", b3[:, :, 0:32], x3[:, :, 32:64], sin_lo, ALU.mult, r=[f"xnq{b}", "cs2"], w=[f"rB{b}"])
        tt("pool", b3[:, :, 32:64], x3[:, :, 0:32], sin_hi, ALU.mult, r=[f"xnq{b}", "cs2"], w=[f"rB{b}"])
        tt("dve", qkb[b], rA[b], rB[b], ALU.add, r=[f"rA{b}", f"rB{b}"], w=[f"qkb{b}"])

    def b2_tile_p2(h, i, pb=7):
        sl = h % 2
        b = i % 3
        tr(psb(pb, 768, 896), qkb[b][:, 0:128], identb, r=[f"qkb{b}"], w=[f"ps{pb}"])
        tr(psb(pb, 896, 1024), qkb[b][:, 128:256], identb, r=[f"qkb{b}"], w=[f"ps{pb}"])
        cp("act", qkT[sl][:, :, i * 128:(i + 1) * 128], psb(pb, 768, 1024).rearrange("p (a t) -> p a t", a=2),
           r=[f"ps{pb}"], w=[f"qkT{sl}"])

    acc3 = psall[:, 4 * 512:7 * 512].rearrange("p (b n) -> p b n", b=3)

    def b3_epilogue(h, g):
        k2 = g % 2
        a9 = accs[k2]
        a3 = a9.rearrange("p a n -> p (a n)").rearrange("p (b n) -> p b n", b=3)
        e_ = e8[k2]
        cp("act", a3, acc3[:, :, 0:387], r=["ps4", "ps5", "ps6"], w=[f"accs{k2}"])
        recip(e_[:, 0:8], a9[:, 0:8, 128], r=[f"accs{k2}"], w=[f"e8a{k2}"])
        ts("dve", e_[:, 8:12], e_[:, 4:8], neglam, ALU.mult, r=[f"e8a{k2}", "neglam"], w=[f"e8b{k2}"])
        o_ = od4[k2]
        tt("pool", o_, a9[:, 0:4, 0:128], e_[:, 0:4].unsqueeze(2).broadcast_to([128, 4, 128]), ALU.mult,
           r=[f"accs{k2}", f"e8a{k2}"], w=[f"od4{k2}"])
        tt("pool", t24, a9[:, 4:8, 0:128], e_[:, 8:12].unsqueeze(2).broadcast_to([128, 4, 128]), ALU.mult,
           r=[f"accs{k2}", f"e8b{k2}"], w=["t24"])
        tt("dve", o_, o_, t24, ALU.add, r=[f"od4{k2}", "t24"], w=[f"od4{k2}"])
        tt("pool", sq4, o_, o_, ALU.mult, r=[f"od4{k2}"], w=["sq4"])
        red(e_[:, 12:16], sq4, ALU.add, r=["sq4"], w=[f"e8c{k2}"])
        act(e_[:, 12:16], e_[:, 12:16], AF.Ln, r=[f"e8c{k2}"], w=[f"e8c{k2}"], bias=EPS, scale=1.0 / 128)
        act(e_[:, 16:20], e_[:, 12:16], AF.Exp, r=[f"e8c{k2}"], w=[f"e8d{k2}"], scale=-0.5)
        tt("dve", o_, o_, e_[:, 16:20].unsqueeze(2).broadcast_to([128, 4, 128]), ALU.mult,
           r=[f"od4{k2}", f"e8d{k2}"], w=[f"od4{k2}"])
        tt("pool", ob4[k2], o_, aon.unsqueeze(1).broadcast_to([128, 4, 128]), ALU.mult,
           r=[f"od4{k2}", "aon"], w=[f"ob4{k2}"])

    def b3_epilogue_p2(h, g):
        k2 = g % 2
        for qt in range(4):
            tr(psb(7, qt * 128, (qt + 1) * 128), ob4[k2][:, qt, :], identb, r=[f"ob4{k2}"], w=["ps7"])
        cp("act", hbst[k2], psb(7, 0, 512), r=["ps7"], w=[f"hbst{k2}"])
        dma(hbT_v[:, h, g * 512:(g + 1) * 512], hbst[k2], f"hbst{k2}", r=[f"hbst{k2}"], w=[f"hbT{h}_{g}"])

    def b3_head(h, inter):
        sl = h % 2
        steps = [(g, j) for g in range(8) for j in range(NT)]
        ns = len(steps)

        def qk(s):
            g, j = steps[s]
            pb_ = 2 * (s % 2)
            for c in range(2):
                mm(PS[pb_ + c], qkT[sl][c * 64:(c + 1) * 64, 1, j * 128:(j + 1) * 128],
                   qkT[sl][c * 64:(c + 1) * 64, 0, g * 512:(g + 1) * 512], True, True,
                   r=[f"qkT{sl}"], w=[f"ps{pb_ + c}"], cost=170.0)
            act(pT[s % 3], psall[:, pb_ * 512:(pb_ + 2) * 512], AF.Exp, r=[f"ps{pb_}", f"ps{pb_ + 1}"],
                w=[f"pT{s % 3}"])

        pending = []
        qk(0)
        for s in range(ns):
            if s + 1 < ns:
                qk(s + 1)
            g, j = steps[s]
            for a in range(8):
                c, qt = divmod(a, 4)
                bank = 4 + a // 3
                col = (a % 3) * 129
                mm(PS[bank][:, col:col + 129], pT[s % 3][:, c * 512 + qt * 128:c * 512 + (qt + 1) * 128],
                   vaug[sl][:, j, 0:129], j == 0 and a % 3 == 0, j == NT - 1,
                   r=[f"pT{s % 3}", f"vaug{sl}_{j}", f"vaug1_{sl}"], w=[f"ps{bank}"], sgc=True)
            for _dm in range(NDUM):
                mm(PS[6][:, 258:387], pT[s % 3][:, 896:1024], vaug[sl][:, j, 0:129], False, j == NT - 1,
                   r=[f"pT{s % 3}", f"vaug{sl}_{j}", f"vaug1_{sl}"], w=["ps6"], sgc=True)
            for it in list(pending):
                if it[0] <= s:
                    pending.remove(it)
                    it[1]()
            if j == NT - 1:
                b3_epilogue(h, g)
                pending.append([s + 7, (lambda gg: (lambda: b3_epilogue_p2(h, gg)))(g)])
            if s % 8 == 2 and inter:
                p1, p2 = inter.pop(0)
                p1()
                if p2 is not None:
                    pending.append([s + 5, p2])
        for it in pending:
            it[1]()
        while inter:
            p1, p2 = inter.pop(0)
            p1()
            if p2 is not None:
                p2()

    nheads = 0 if "B" in skip else 8
    if nheads:
        load_head_w(0)
        load_head_w(1)
        for i in range(NT):
            b2_tile(0, i, pb=i % 8)
            if i >= 2:
                b2_tile_p2(0, i - 2, pb=(i - 2) % 8)
        for i in range(NT - 2, NT):
            b2_tile_p2(0, i, pb=i % 8)
    for h in range(nheads):
        inter = []
        if h + 1 < nheads:
            inter = [((lambda hh, ii: (lambda: b2_tile(hh, ii)))(h + 1, i),
                      (lambda hh, ii: (lambda: b2_tile_p2(hh, ii)))(h + 1, i)) for i in range(NT)]
            if h + 2 < nheads:
                inter.insert(NT // 2, ((lambda hh: (lambda: load_head_w(hh)))(h + 2), None))
        b3_head(h, inter)
    S.barrier()
    if stop_after == "B":
        return finish(nc, S, ctx, out)

    AR.off = base_persist
    wgst = AR.alloc([128, KC, 16], F32)
    wgb = AR.alloc([128, KC, 16], BF16)
    G = AR.alloc([128, NT, 16], F32)
    gb16 = AR.alloc([128, 16], F32)

    def arr():
        return AR.alloc([128, 2, 128], F32)

    SPl, NB, GTn, E_, EMAX, MN, MP, A1, A2, W1, W2, UP, IW, ED, T1, T2 = [arr() for _ in range(16)]
    emT = AR.alloc([128, 2], F32)
    dg1 = AR.alloc([128, 128], F32)
    dg = [dg1, dg1]
    zero4 = AR.alloc([128, 4], F32)
    mst = AR.alloc([128, KC, 128], F32)
    wmD = [AR.alloc([128, KC, 768], BF16) for _ in range(2)]
    qT = AR.alloc([128, T], BF16)
    kT = AR.alloc([128, T], BF16)
    ktok = AR.alloc([128, NT, 128], BF16)
    vaug2 = AR.alloc([128, NT, 258], BF16)
    hfwd = AR.alloc([128, NT, 256], F32)
    CnD = [AR.alloc([128, 264], F32) for _ in range(2)]
    CbD = [[AR.alloc([128, 264], BF16) for _ in range(2)] for _ in range(2)]
    STD = [[AR.alloc([128, 128], BF16) for _ in range(3)] for _ in range(2)]
    kwD = [[AR.alloc([128, 128], BF16) for _ in range(2)] for _ in range(2)]
    hm = [AR.alloc([128, 256], F32) for _ in range(2)]
    sg = [AR.alloc([128, 256], F32) for _ in range(2)]
    hn = [AR.alloc([128, 256], F32) for _ in range(2)]
    hab = [AR.alloc([128, 256], BF16) for _ in range(2)]
    hast = [AR.alloc([128, 256], BF16) for _ in range(2)]
    mnbc = AR.alloc([128, D], F32)
    junk2 = AR.alloc([128, 256], BF16)
    dnn = [AR.alloc([128, 8], F32) for _ in range(4)]

    def v4(a):
        return a.rearrange("p d (c h) -> p d c h", c=NT)

    Gv = G.rearrange("p i (t h) -> p i t h", t=4)
    dma(wgst, w_in_v[:, :, OFF_MG:OFF_MG + 16], "wgst", w=["wgst"])
    cp("pool", wgb, wgst, r=["wgst"], w=["wgb"])
    dma(gb16, gate_bias.partition_broadcast(128), "gb16", w=["gb16"])
    dma(mnbc, m_out_norm.partition_broadcast(128), "mnbc", w=["mnbc"])
    mset("pool", vaug2[:, :, 256:258], 1.0, w=["vaug2one"])
    mset("pool", zero4, 0.0, w=["zero4"])
    for i in range(NT):
        for c in range(KC):
            mm(PS[0][:, i * 16:(i + 1) * 16], xnT[:, c, i * 128:(i + 1) * 128], wgb[:, c, :], c == 0, c == KC - 1,
               r=["wgb"], w=["ps0"])
    tt("dve", G, PS[0][:, :].rearrange("p (i k) -> p i k", i=NT), gb16.unsqueeze(1).broadcast_to([128, NT, 16]),
       ALU.add, r=["ps0", "gb16"], w=["G"])
    for d in range(2):
        act(v4(SPl)[:, d], Gv[:, :, 2 * d + 1, :], AF.Exp, r=["G"], w=[f"SPl{d}"], scale=-1.0)
        act(SPl[:, d], SPl[:, d], AF.Ln, r=[f"SPl{d}"], w=[f"SPl{d}"], bias=1.0, scale=1.0)
        mm(PS[1][:, d * 128:(d + 1) * 128], U[d], SPl[:, d], True, True, r=[f"SPl{d}"], w=["ps1"])
        mm(PS[2][:, d * 128:(d + 1) * 128], onesf, SPl[:, d], True, True, r=[f"SPl{d}"], w=["ps2"])
    cp("dve", NB, PS[1][:, 0:256].rearrange("p (d k) -> p d k", d=2), r=["ps1"], w=["NB"])
    cp("dve", GTn, PS[2][:, 0:256].rearrange("p (d k) -> p d k", d=2), r=["ps2"], w=["GTn"])
    for d in range(2):
        tt("dve", v4(E_)[:, d], v4(NB)[:, d], Gv[:, :, 2 * d, :], ALU.add, r=["NB", "G"], w=[f"E{d}"])
        tr(PS[3][:, d * 128:(d + 1) * 128], E_[:, d], identf, r=[f"E{d}"], w=["ps3"])
        S.add("dve", (lambda dd: (lambda e: e.tensor_reduce(out=emT[:, dd:dd + 1], in_=PS[3][:, dd * 128:(dd + 1) * 128],
                                                            axis=AX.X, op=ALU.max)))(d), ["ps3"], [f"emT{d}"])
        ts("dve", dg[d], identf, emT[:, d:d + 1], ALU.mult, r=[f"emT{d}"], w=["dg"])
        mm(PS[4][:, d * 128:(d + 1) * 128], onesf, dg[d], True, True, r=["dg"], w=["ps4"])
    cp("dve", EMAX, PS[4][:, 0:256].rearrange("p (d k) -> p d k", d=2), r=["ps4"], w=["EMAX"])
    if cstage == 0:
        return finish(nc, S, ctx, out)
    for d in range(2):
        order = list(range(NT)) if d == 0 else list(range(NT - 1, -1, -1))
        prev = zero4
        pk = "zero4"
        for c in order:
            sl4 = slice(c * 4, (c + 1) * 4)
            tt("dve", T1[:, d, sl4], prev, EMAX[:, d, sl4], ALU.max, r=[pk, "EMAX"], w=[f"T1s{d}_{c}"])
            tt("dve", MN[:, d, sl4], T1[:, d, sl4], GTn[:, d, sl4], ALU.subtract, r=[f"T1s{d}_{c}", "GTn"], w=[f"MN{d}_{c}"])
            prev = MN[:, d, sl4]
            pk = f"MN{d}_{c}"
    mnk = [f"MN{d}_{c}" for d in range(2) for c in range(NT)]
    cp("dve", MP[:, 0, 4:128], MN[:, 0, 0:124], r=mnk, w=["MP"])
    mset("dve", MP[:, 0, 0:4], 0.0, w=["MP"])
    cp("dve", MP[:, 1, 0:124], MN[:, 1, 4:128], r=mnk, w=["MP"])
    mset("dve", MP[:, 1, 124:128], 0.0, w=["MP"])
    tt("dve", T1, MP, MN, ALU.subtract, r=["MP"] + mnk, w=["T1"])
    tt("dve", T1, T1, GTn, ALU.subtract, r=["T1", "GTn"], w=["T1"])
    act(A1, T1, AF.Exp, r=["T1"], w=["A1"])
    tt("dve", T2, EMAX, MN, ALU.subtract, r=["EMAX"] + mnk, w=["T2"])
    tt("dve", T2, T2, GTn, ALU.subtract, r=["T2", "GTn"], w=["T2"])
    act(A2, T2, AF.Exp, r=["T2"], w=["A2"])
    tt("dve", T1, E_, EMAX, ALU.subtract, r=["E0", "E1", "EMAX", "A1"], w=["T1"])
    act(W1, T1, AF.Exp, r=["T1"], w=["W1"])
    tt("dve", UP, EMAX, MP, ALU.max, r=["EMAX", "MP"], w=["UP"])
    tt("dve", T2, E_, UP, ALU.subtract, r=["E0", "E1", "UP", "A2"], w=["T2"])
    act(W2, T2, AF.Exp, r=["T2"], w=["W2"])
    tt("dve", T1, MP, UP, ALU.subtract, r=["MP", "UP", "W1"], w=["T1"])
    act(IW, T1, AF.Exp, r=["T1"], w=["IW"])
    tt("dve", T2, NB, UP, ALU.subtract, r=["NB", "UP", "W2"], w=["T2"])
    act(ED, T2, AF.Exp, r=["T2"], w=["ED"])

    if cstage == 1:
        return finish(nc, S, ctx, out)
    KSC = 128.0 ** -0.5
    def load_mw(h):
        wm_ = wmD[h % 2]
        q_ = h % 2
        subs = ((OFF_MQ + h * 128, 0, 0), (OFF_MK + h * 128, 128, 0),
                (OFF_MV + h * 256, 256, 1), (OFF_MV + h * 256 + 128, 384, 1),
                (OFF_MO + h * 256, 512, 2), (OFF_MO + h * 256 + 128, 640, 2))
        for off, dc, part in subs:
            dma(mst, w_in_v[:, :, off:off + 128], "mst", w=["mst"])
            cp("pool", wm_[:, :, dc:dc + 128], mst, r=["mst"], w=[f"wm{part}_{q_}"])

    nh_c = 4 if "C" not in skip else 0
    if nh_c:
        load_mw(0)
    for h in range(nh_c):
        wm = wmD[h % 2]
        wq = h % 2
        if h + 1 < nh_c:
            load_mw(h + 1)
        if cstage == 10:
            return finish(nc, S, ctx, out)
        n = 0
        for tb in range(8):
            for which, dst, scl in ((0, qT, 1.0), (1, kT, KSC)):
                bk = n % 2
                n += 1
                for c in range(KC):
                    mm(PS[bk][:, :], wm[:, c, which * 128:(which + 1) * 128], xnT[:, c, tb * 512:(tb + 1) * 512],
                       c == 0, c == KC - 1, r=[f"wm0_{wq}"], w=[f"ps{bk}"])
                act(dst[:, tb * 512:(tb + 1) * 512], PS[bk][:, :], AF.Identity, r=[f"ps{bk}"],
                    w=[("qT" if which == 0 else "kT")], scale=scl)
        if cstage == 11:
            return finish(nc, S, ctx, out)
        for i in range(NT):
            bk = 2 + i % 2
            for c in range(KC):
                mm(PS[bk][:, 0:384], xnT[:, c, i * 128:(i + 1) * 128], wm[:, c, 128:512], c == 0, c == KC - 1,
                   r=[f"wm0_{wq}", f"wm1_{wq}"], w=[f"ps{bk}"])
            cp("act", vaug2[:, i, 0:256], PS[bk][:, 128:384], r=[f"ps{bk}"], w=[f"v2_{i}"])
            ts("dve", ktok[:, i, :], PS[bk][:, 0:128], KSC, ALU.mult, r=[f"ps{bk}"], w=[f"ktok{i}"])
        if cstage == 2:
            return finish(nc, S, ctx, out)
        orders = [list(range(NT)), list(range(NT - 1, -1, -1))]
        for d in range(2):
            mset("pool", CnD[d], 0.0, w=[f"Cn{d}"])
            mset("pool", CbD[d][0], 0.0, w=[f"Cb{d}_0"])

        def pre(d, idx):
            c = orders[d][idx]
            bk = 4 + d
            mm(PS[bk][:, 0:128], kT[:, c * 128:(c + 1) * 128], qT[:, c * 128:(c + 1) * 128], True, True,
               r=["qT", "kT"], w=[f"ps{bk}"])
            stt(STD[d][idx % 3], PS[bk][:, 0:128], W2[:, d, c * 4 + h:c * 4 + h + 1], U[d], ALU.mult, ALU.mult,
                r=[f"ps{bk}", "W2"], w=[f"ST{d}_{idx % 3}"])

        def chunk(d, idx, ncomb):
            c = orders[d][idx]
            hb_ = 6 + d
            k2 = idx % 2
            dk4 = (2 * idx + d) % 4
            dn = dnn[dk4]
            col = c * 4 + h
            Cn_ = CnD[d]
            mm(PS[hb_][:, 0:257], STD[d][idx % 3], vaug2[:, c, 0:257], True, False,
               r=[f"ST{d}_{idx % 3}", f"v2_{c}", "vaug2one"], w=[f"ps{hb_}"])
            mm(PS[hb_][:, 0:257], qT[:, c * 128:(c + 1) * 128], CbD[d][k2][:, 0:257], False, True,
               r=["qT", f"Cb{d}_{k2}"], w=[f"ps{hb_}"])
            if idx + 1 < NT:
                cn_ = orders[d][idx + 1]
                kw_ = kwD[d][idx % 2]
                ts("pool", kw_, ktok[:, c, :], W1[:, d, col:col + 1], ALU.mult,
                   r=[f"ktok{c}", "W1"], w=[f"kw{d}_{idx % 2}"], s2=1.0, op1=ALU.mult)
                mm(PS[1][:, 0:257], kw_, vaug2[:, c, 0:257], True, True,
                   r=[f"kw{d}_{idx % 2}", f"v2_{c}", "vaug2one"], w=["ps1"])
                ts("dve", Cn_[:, 0:257], Cn_[:, 0:257], A1[:, d, col:col + 1], ALU.mult, r=[f"Cn{d}", "A1"], w=[f"Cn{d}"])
                stt(Cn_[:, 0:257], PS[1][:, 0:257], A2[:, d, col:col + 1], Cn_[:, 0:257], ALU.mult, ALU.add,
                    r=["ps1", "A2", f"Cn{d}"], w=[f"Cn{d}"])
                k3 = (idx + 1) % 2
                ts("pool", CbD[d][k3][:, 0:257], Cn_[:, 0:257], IW[:, d, cn_ * 4 + h:cn_ * 4 + h + 1], ALU.mult,
                   r=[f"Cn{d}", "IW"], w=[f"Cb{d}_{k3}"], s2=1.0, op1=ALU.mult)
            ts("dve", dn[:, 5:6], PS[hb_][:, 256:257], -1.0, ALU.mult, r=[f"ps{hb_}", "ED"], w=[f"dn{dk4}z"],
               s2=ED[:, d, col:col + 1], op1=ALU.max)
            stt(dn[:, 0:1], PS[hb_][:, 256:257], 1.0, dn[:, 5:6], ALU.mult, ALU.max,
                r=[f"ps{hb_}", f"dn{dk4}z"], w=[f"dn{dk4}a"])
            recip(dn[:, 1:2], dn[:, 0:1], r=[f"dn{dk4}a"], w=[f"dn{dk4}b"])
            if idx < NT // 2:
                ts("dve", hfwd[:, c, :], PS[hb_][:, 0:256], dn[:, 1:2], ALU.mult, r=[f"ps{hb_}", f"dn{dk4}b"], w=[f"hf{c}"])
                return
            k2 = ncomb % 2
            stt(hm[k2], PS[hb_][:, 0:256], dn[:, 1:2], hfwd[:, c, :], ALU.mult, ALU.add,
                r=[f"ps{hb_}", f"dn{dk4}b", f"hf{c}"], w=[f"hm{k2}"])
            act(junk2, hm[k2], AF.Square, r=[f"hm{k2}"], w=["junk2", f"dn{dk4}c"], accum=dn[:, 2:3])
            act(dn[:, 3:4], dn[:, 2:3], AF.Ln, r=[f"dn{dk4}c"], w=[f"dn{dk4}d"], bias=EPS, scale=1.0 / 256)
            act(dn[:, 4:5], dn[:, 3:4], AF.Exp, r=[f"dn{dk4}d"], w=[f"dn{dk4}e"], scale=-0.5)
            mb = 2 + ncomb % 2
            for cc in range(KC):
                mm(PS[mb][:, 0:256], xnT[:, cc, c * 128:(c + 1) * 128], wm[:, cc, 512:768], cc == 0, cc == KC - 1,
                   r=[f"wm2_{wq}"], w=[f"ps{mb}"])
            cp("dve", sg[k2], PS[mb][:, 0:256], r=[f"ps{mb}"], w=[f"sg{k2}"])
            act(sg[k2], sg[k2], AF.Exp, r=[f"sg{k2}"], w=[f"sg{k2}"], scale=-1.0)
            act(sg[k2], sg[k2], AF.Ln, r=[f"sg{k2}"], w=[f"sg{k2}"], bias=1.0, scale=1.0)
            act(sg[k2], sg[k2], AF.Exp, r=[f"sg{k2}"], w=[f"sg{k2}"], scale=-1.0)
            stt(hn[k2], hm[k2], dn[:, 4:5], mnbc[:, h * 256:(h + 1) * 256], ALU.mult, ALU.mult,
                r=[f"hm{k2}", f"dn{dk4}e", "mnbc"], w=[f"hn{k2}"])
            tt("pool", hab[k2], hn[k2], sg[k2], ALU.mult, r=[f"hn{k2}", f"sg{k2}"], w=[f"hab{k2}"])
            tr(psb(0, 0, 128), hab[k2][:, 0:128], identb, r=[f"hab{k2}"], w=["ps0"])
            tr(psb(0, 128, 256), hab[k2][:, 128:256], identb, r=[f"hab{k2}"], w=["ps0"])
            cp("dve", hast[k2], psb(0, 0, 256), r=["ps0"], w=[f"hast{k2}"])
            dma(haT_v[:, 2 * h:2 * h + 2, c * 128:(c + 1) * 128], hast[k2].rearrange("p (a t) -> p a t", a=2),
                f"hast{k2}", r=[f"hast{k2}"], w=[f"haT{h}_{c}"])

        pre(0, 0)
        pre(1, 0)
        ncomb = 0
        for idx in range(NT):
            for d in range(2):
                if idx + 1 < NT:
                    pre(d, idx + 1)
                chunk(d, idx, ncomb)
                if idx >= NT // 2:
                    ncomb += 1
    S.barrier()
    if stop_after == "C":
        return finish(nc, S, ctx, out)

    AR.off = base_persist
    wst1 = [AR.alloc([128, KC, 256], F32) for _ in range(2)]
    wmat = [AR.alloc([128, KC, D], BF16) for _ in range(4)]
    hAB = [[AR.alloc([128, KC, 256], BF16) for _ in range(2)] for _ in range(2)]
    eg = [AR.alloc([128, 512], F32) for _ in range(2)]
    y1 = [AR.alloc([128, 512], F32) for _ in range(2)]
    yTb = [AR.alloc([128, KC, 256], BF16) for _ in range(2)]
    srcs = [p_a.rearrange("(c p) n -> p c n", p=128), p_b.rearrange("(c p) n -> p c n", p=128),
            w_in_v[:, :, OFF_GA:OFF_GA + D], w_in_v[:, :, OFF_GB:OFF_GB + D]]
    npc = 0
    for pc in range(4):
        for wi in range(4):
            k = npc % 2
            npc += 1
            dma(wst1[k], srcs[wi][:, :, pc * 256:(pc + 1) * 256], f"wst1_{k}", w=[f"wst1_{k}"])
            cp("pool", wmat[wi][:, :, pc * 256:(pc + 1) * 256], wst1[k], r=[f"wst1_{k}"], w=[f"wmat{wi}_{pc}"])
    for b in range(16):
        sl = b % 2
        t0 = b * 256
        dma(hAB[0][sl], haT_v[:, :, t0:t0 + 256], f"hA{sl}", w=[f"hA{sl}"])
        dma(hAB[1][sl], hbT_v[:, :, t0:t0 + 256], f"hB{sl}", w=[f"hB{sl}"])
        for m in range(8):
            yb_ = m % 2
            gk = 2 + m % 2
            for half, (wi, rk) in enumerate(((0, f"hA{sl}"), (1, f"hB{sl}"))):
                for c in range(KC):
                    mm(PS[yb_][:, half * 256:(half + 1) * 256], wmat[wi][:, c, m * 128:(m + 1) * 128],
                       hAB[half][sl][:, c, :], c == 0, c == KC - 1, r=[f"wmat{wi}_{m // 2}", rk], w=[f"ps{yb_}"])
            for half, wi in enumerate((2, 3)):
                for c in range(KC):
                    mm(PS[gk][:, half * 256:(half + 1) * 256], wmat[wi][:, c, m * 128:(m + 1) * 128],
                       xnT[:, c, t0:t0 + 256], c == 0, c == KC - 1, r=[f"wmat{wi}_{m // 2}"], w=[f"ps{gk}"])
            act(eg[m % 2], PS[gk][:, :], AF.Exp, r=[f"ps{gk}"], w=[f"eg{m % 2}"], scale=-1.0)
            act(eg[m % 2], eg[m % 2], AF.Ln, r=[f"eg{m % 2}"], w=[f"eg{m % 2}"], bias=1.0, scale=1.0)
            act(eg[m % 2], eg[m % 2], AF.Exp, r=[f"eg{m % 2}"], w=[f"eg{m % 2}"], scale=-1.0)
            tt("dve", y1[m % 2], PS[yb_][:, :], eg[m % 2], ALU.mult, r=[f"ps{yb_}", f"eg{m % 2}"], w=[f"y1{m % 2}"])
            tt("pool", yTb[sl][:, m, :], y1[m % 2][:, 0:256], y1[m % 2][:, 256:512], ALU.add,
               r=[f"y1{m % 2}"], w=[f"yTb{sl}_{m}"])
        dma(yT_v[:, :, t0:t0 + 256], yTb[sl], f"yTb{sl}", r=[f"yTb{sl}_{m}" for m in range(8)], w=[f"yTs{b}"])
    S.barrier()
    if stop_after == "D1":
        return finish(nc, S, ctx, out)

    AR.off = base_consts
    wup = AR.alloc([128, KC, 2 * DFF], BF16)
    wdn = AR.alloc([128, NJ, D], BF16)
    base_e = AR.off
    wst2 = [AR.alloc([128, KC, 256], F32) for _ in range(2)]
    wo = AR.alloc([128, KC, D], BF16)
    yTt = [AR.alloc([128, KC, 128], BF16) for _ in range(2)]
    xt2 = [AR.alloc([128, D], F32) for _ in range(2)]
    x1t = [AR.alloc([128, D], F32) for _ in range(2)]
    g2bc = AR.alloc([128, D], F32)
    xn2b = [AR.alloc([128, D], BF16) for _ in range(2)]
    xn2st = [AR.alloc([128, D], BF16) for _ in range(2)]
    junk3 = AR.alloc([128, D], BF16)
    st2 = [AR.alloc([128, 4], F32) for _ in range(2)]
    end_d2 = AR.off
    wo_src = w_o.rearrange("(c p) n -> p c n", p=128)
    wup_src = w_up.rearrange("(c p) n -> p c n", p=128)
    wdn_src = w_down.rearrange("(j p) n -> p j n", p=128)
    npc = 0
    for pc in range(4):
        k = npc % 2
        npc += 1
        dma(wst2[k], wo_src[:, :, pc * 256:(pc + 1) * 256], f"wst2_{k}", w=[f"wst2_{k}"])
        cp("pool", wo[:, :, pc * 256:(pc + 1) * 256], wst2[k], r=[f"wst2_{k}"], w=["wo"])
    dma(g2bc, norm2.partition_broadcast(128), "g2bc", w=["g2bc"])
    ffn_pieces = [("up", pc) for pc in range(22)] + [("dn", pc) for pc in range(11)]

    def load_ffn_piece(kind, pc):
        nonlocal npc
        k = npc % 2
        npc += 1
        if kind == "up":
            dma(wst2[k], wup_src[:, :, pc * 256:(pc + 1) * 256], f"wst2_{k}", w=[f"wst2_{k}"])
            cp("pool", wup[:, :, pc * 256:(pc + 1) * 256], wst2[k], r=[f"wst2_{k}"], w=["wup"])
        else:
            dma(wst2[k].rearrange("p (a b) n -> p a (b n)", a=2), wdn_src[:, 2 * pc:2 * pc + 2, :], f"wst2_{k}",
                w=[f"wst2_{k}"])
            cp("pool", wdn[:, 2 * pc:2 * pc + 2, :], wst2[k].rearrange("p (a b) n -> p a (b n)", a=2),
               r=[f"wst2_{k}"], w=["wdn"])

    for i in range(NT):
        b = i % 2
        dma(yTt[b], yT_v[:, :, i * 128:(i + 1) * 128], f"yTt{b}", w=[f"yTt{b}"])
        dma(xt2[b], x[i * 128:(i + 1) * 128, :], f"xt2{b}", w=[f"xt2{b}"])
        if i < len(ffn_pieces):
            load_ffn_piece(*ffn_pieces[i])
        for n in range(2):
            bk = 2 * b + n
            for m in range(KC):
                mm(PS[bk][:, :], yTt[b][:, m, :], wo[:, m, n * 512:(n + 1) * 512], m == 0, m == KC - 1,
                   r=[f"yTt{b}", "wo"], w=[f"ps{bk}"])
            tt("dve", x1t[b][:, n * 512:(n + 1) * 512], PS[bk][:, :], xt2[b][:, n * 512:(n + 1) * 512], ALU.add,
               r=[f"ps{bk}", f"xt2{b}"], w=[f"x1t{b}_{n}"])
        dma(x1s[i * 128:(i + 1) * 128, :], x1t[b], f"x1t{b}", r=[f"x1t{b}_0", f"x1t{b}_1"], w=[f"x1s{i}"])
        act(junk3, x1t[b], AF.Square, r=[f"x1t{b}_0", f"x1t{b}_1"], w=["junk3", f"st2a{b}"], accum=st2[b][:, 0:1])
        act(st2[b][:, 1:2], st2[b][:, 0:1], AF.Ln, r=[f"st2a{b}"], w=[f"st2b{b}"], bias=EPS, scale=1.0 / D)
        act(st2[b][:, 2:3], st2[b][:, 1:2], AF.Exp, r=[f"st2b{b}"], w=[f"st2c{b}"], scale=-0.5)
        stt(xn2b[b], x1t[b], st2[b][:, 2:3], g2bc, ALU.mult, ALU.mult,
            r=[f"x1t{b}_0", f"x1t{b}_1", f"st2c{b}", "g2bc"], w=[f"xn2b{b}"])
        for c in range(KC):
            tr(psb(4 + b, c * 128, (c + 1) * 128), xn2b[b][:, c * 128:(c + 1) * 128], identb,
               r=[f"xn2b{b}"], w=[f"ps{4 + b}"])
        cp("dve", xn2st[b], psb(4 + b, 0, 1024), r=[f"ps{4 + b}"], w=[f"xn2st{b}"])
        dma(xn2T_v[:, :, i * 128:(i + 1) * 128], xn2st[b].rearrange("p (c t) -> p c t", c=KC), f"xn2st{b}",
            r=[f"xn2st{b}"], w=[f"xn2T{i}"])
    for kind, pc in ffn_pieces[NT:]:
        load_ffn_piece(kind, pc)
    S.barrier()
    if stop_after == "D2":
        return finish(nc, S, ctx, out)

    AR.off = base_e
    cw = AR.alloc([128, 44, 3], F32)
    cb = AR.alloc([128, 44], F32)
    xw = [AR.alloc([128, KC, 258], BF16) for _ in range(2)]
    actT = [AR.alloc([128, NJ, 256], BF16) for _ in range(2)]
    NB_E = 4
    ca = [AR.alloc([128, 256], F32) for _ in range(NB_E)]
    cg = [AR.alloc([128, 256], F32) for _ in range(NB_E)]
    x2 = [AR.alloc([128, 256], F32) for _ in range(NB_E)]
    zz = [AR.alloc([128, 256], F32) for _ in range(NB_E)]
    ez = [AR.alloc([128, 256], F32) for _ in range(NB_E)]
    x1e = [AR.alloc([128, D], F32) for _ in range(2)]
    ucp = [AR.alloc([128, 260], F32) for _ in range(2 * NB_E)]
    dma(cw, conv_wp.rearrange("p (j k) -> p j k", k=3), "cw", w=["cw"])
    dma(cb, conv_bp, "cb", w=["cb"])
    pend_down = []
    for b in range(16):
        sl = b % 2
        t0 = b * 256
        if b == 0:
            mset("pool", xw[sl][:, :, 0:2], 0.0, w=[f"xw{sl}", f"xw{sl}h"])
            dma(xw[sl][:, :, 1:258], xn2T_v[:, :, 0:257], f"xw{sl}", w=[f"xw{sl}"])
        elif b == 15:
            mset("pool", xw[sl][:, :, 256:258], 0.0, w=[f"xw{sl}", f"xw{sl}h"])
            dma(xw[sl][:, :, 0:257], xn2T_v[:, :, t0 - 1:T], f"xw{sl}", w=[f"xw{sl}"])
        else:
            dma(xw[sl], xn2T_v[:, :, t0 - 1:t0 + 257], f"xw{sl}", w=[f"xw{sl}", f"xw{sl}h"])
        for j in range(NJ):
            if j == 8 and pend_down:
                pend_down.pop(0)()
            s = j % NB_E
            for bank, ch, dst, dk_ in ((2 * (j % 2), j, ca[s], f"ca{s}"), (2 * (j % 2) + 1, NJ + j, cg[s], f"cg{s}")):
                for c in range(KC):
                    mm(PS[bank][:, 0:258], wup[:, c, ch * 128:(ch + 1) * 128], xw[sl][:, c, :], c == 0, c == KC - 1,
                       r=[f"xw{sl}", f"xw{sl}h"], w=[f"ps{bank}"])
                ui = 2 * s + (bank % 2)
                uc = ucp[ui][:, 0:258]
                act(uc, PS[bank][:, 0:258], AF.Copy, r=[f"ps{bank}"], w=[f"uc{ui}"])
                act(dst, uc[:, 1:257], AF.Identity, r=[f"uc{ui}", "cw", "cb"], w=[dk_],
                    scale=cw[:, ch, 1:2], bias=cb[:, ch:ch + 1])
                stt(dst, uc[:, 0:256], cw[:, ch, 0:1], dst, ALU.mult, ALU.add, r=[f"uc{ui}", dk_], w=[dk_])
                stt(dst, uc[:, 2:258], cw[:, ch, 2:3], dst, ALU.mult, ALU.add, r=[f"uc{ui}", dk_], w=[dk_])
            act(ez[s], cg[s], AF.Gelu_apprx_tanh, r=[f"cg{s}"], w=[f"ez{s}"])
            tt("dve", actT[sl][:, j, :], ez[s], ca[s], ALU.mult, r=[f"ez{s}", f"ca{s}"], w=[f"actT{sl}"])
        def down_proj(b=b, sl=sl):
            for t2 in range(2):
                tile_i = b * 2 + t2
                dma(x1e[t2], x1s[tile_i * 128:(tile_i + 1) * 128, :], f"x1e{t2}", w=[f"x1e{t2}"])
                for n in range(2):
                    bk = 4 + 2 * t2 + n
                    for j in range(NJ):
                        mm(PS[bk][:, :], actT[sl][:, j, t2 * 128:(t2 + 1) * 128], wdn[:, j, n * 512:(n + 1) * 512],
                           j == 0, j == NJ - 1, r=[f"actT{sl}"], w=[f"ps{bk}"])
                    tt("dve", x1e[t2][:, n * 512:(n + 1) * 512], PS[bk][:, :], x1e[t2][:, n * 512:(n + 1) * 512],
                       ALU.add, r=[f"ps{bk}", f"x1e{t2}"], w=[f"x1e{t2}"])
                dma(out[tile_i * 128:(tile_i + 1) * 128, :], x1e[t2], f"x1e{t2}", r=[f"x1e{t2}"], w=[f"out{tile_i}"])

        pend_down.append(down_proj)
    for f_ in pend_down:
        f_()
    return finish(nc, S, ctx, out)


def finish(nc, S, ctx, out, debug_dump=None):
    if debug_dump is not None:
        S.barrier()
        src, dst = debug_dump
        S.add("sp", lambda e: e.dma_start(out=dst, in_=src), (), (), dk="dbg", nbytes=1 << 23)
    S.barrier()
    for e_ in ENGS:
        S.add(e_, None, (), ())
    S.schedule()
    esem = {e: ctx.enter_context(nc.semaphore(f"se_{e}")) for e in ENGS if e != "sp"}
    dsem = {k: ctx.enter_context(nc.semaphore(f"sd_{k}")) for k in S.dma_n}
    with nc.Block() as block:
        @block.sync
        def _(e):
            S.emit("sp", e, esem, dsem)

        @block.scalar
        def _(e):
            S.emit("act", e, esem, dsem)

        @block.vector
        def _(e):
            S.emit("dve", e, esem, dsem)

        @block.gpsimd
        def _(e):
            S.emit("pool", e, esem, dsem)

        @block.tensor
        def _(e):
            S.emit("pe", e, esem, dsem)
    ctx.close()
    nc._est_ns = S.est_ns
    return nc


def host_consts():
    ident = np.eye(128, dtype=np.float32)
    ones = np.ones((128, 128), np.float32)
    s = np.arange(128)
    ufwd = (s[:, None] <= s[None, :]).astype(np.float32)
    ubwd = (s[:, None] >= s[None, :]).astype(np.float32)
    cst = np.concatenate([ident, ones, ufwd, ubwd], axis=1)
    pos = np.arange(T, dtype=np.float32)
    inv = (10000.0 ** (-np.arange(0, 64, 2, dtype=np.float32) / 64)).astype(np.float32)
    ang = pos[:, None] * inv[None, :]
    cos = np.cos(ang).astype(np.float32)
    sin = np.sin(ang).astype(np.float32)
    cos2 = np.concatenate([cos, cos], axis=1).reshape(32, 128, 64).transpose(1, 0, 2)
    sin2 = np.concatenate([-sin, sin], axis=1).reshape(32, 128, 64).transpose(1, 0, 2)
    rope = np.stack([cos2, sin2], axis=1).reshape(128, -1)
    return np.ascontiguousarray(cst), np.ascontiguousarray(rope.astype(np.float32))


def make_in_maps(inp, cores):
    cst, rope = host_consts()
    f = lambda a: np.ascontiguousarray(np.asarray(a, dtype=np.float32))
    shared = {
        "w_in": f(inp["w_in"][0]), "p_a": f(inp["p_a"][0]), "p_b": f(inp["p_b"][0]), "w_o": f(inp["w_o"][0]),
        "w_up": f(inp["w_up"][0]), "w_down": f(inp["w_down"][0]),
        "norm1": f(inp["norm1"]), "norm2": f(inp["norm2"]), "m_out_norm": f(inp["m_out_norm"]),
        "gate_bias": f(inp["gate_bias"]), "q_norm": f(inp["q_norm"]), "k_norm": f(inp["k_norm"]),
        "lam4": f(np.concatenate([inp["lam_q1"], inp["lam_k1"], inp["lam_q2"], inp["lam_k2"]], axis=1)),
        "a_out_norm": f(inp["a_out_norm"]),
        "conv_wp": f(np.asarray(inp["conv_w"])[0, :, 0, :].reshape(3, 44, 128).transpose(2, 1, 0).reshape(128, 132)),
        "conv_bp": f(np.asarray(inp["conv_b"])[0].reshape(44, 128).T),
        "cst": cst, "rope": rope,
    }
    maps = []
    for b in cores:
        m = dict(shared)
        m["x"] = f(inp["x"][b])
        maps.append(m)
    return maps


_NC = None


def kernel(**inputs):
    global _NC
    if _NC is None:
        _NC = build()
    maps = make_in_maps(inputs, list(range(8)))
    res = run_bass_kernel_spmd(_NC, maps, core_ids=list(range(8)))
    return np.stack([np.asarray(r["out"]) for r in res.results], axis=0).astype(np.float32)
```

```python
import math
from contextlib import ExitStack

import numpy as np
import concourse.bass as bass
import concourse.mybir as mybir
from concourse.bass_utils import run_bass_kernel_spmd

F32 = mybir.dt.float32
BF16 = mybir.dt.bfloat16
AF = mybir.ActivationFunctionType
ALU = mybir.AluOpType
AX = mybir.AxisListType

T = 4096
D = 1024
NT = 32
KC = 8
DIN = 8208
OFF_MQ, OFF_MK, OFF_MV, OFF_MO, OFF_MG = 0, 512, 1024, 2048, 3072
OFF_AQ, OFF_AK, OFF_AV, OFF_GA, OFF_GB = 3088, 4112, 5136, 6160, 7184
DFF = 2816
NJ = 22
EPS = 1e-6
LAM_INIT = 0.8 - 0.6 * math.exp(-0.3 * 0)
ARENA_BYTES = 206 * 1024

ENGS = ("pe", "act", "dve", "pool", "sp")


class Op:
    __slots__ = ("eng", "fn", "pidx", "pos", "sig", "dk", "ordn", "waits", "cnt", "cost", "deps", "users",
                 "nd", "rt", "fin", "group", "lastpos", "dmacnt", "nbytes")

    def __init__(self, eng, fn, dk, cost):
        self.eng = eng
        self.fn = fn
        self.dk = dk
        self.cost = cost
        self.sig = False
        self.ordn = 0
        self.waits = []
        self.cnt = 0
        self.deps = []
        self.users = []
        self.nd = 0
        self.rt = 0.0
        self.fin = 0.0
        self.pos = -1
        self.group = None
        self.nbytes = 0


USE_DRAIN = False
MAX_DMA_INFLIGHT = 6
NDUM = 0
PSUM_EXCL = True
SEM_LAT = 300.0
DMA_LAT = 2000.0
DMA_BW = 160.0


class Sched:
    def __init__(self):
        self.all = []
        self.lastw = {}
        self.readers = {}
        self.dma_n = {}
        self.dma_last = {}
        self.cur_bar = None
        self.group = []
        self.order = None
        self.last_psr = {}
        self.dma_hist = []

    def add(self, eng, fn, r=(), w=(), dk=None, cost=100.0, nbytes=0):
        op = Op(eng, fn, dk, cost)
        op.pidx = len(self.all)
        op.nbytes = nbytes
        deps = {}
        psr = False

        def dep(d, raw):
            if d is op:
                return
            if id(d) in deps:
                if raw:
                    deps[id(d)] = (d, True)
            else:
                deps[id(d)] = (d, raw)

        for k in r:
            d = self.lastw.get(k)
            if d is not None:
                dep(d, True)
            if PSUM_EXCL and k[:2] == "ps" and eng in ("act", "dve"):
                psr = True
        if psr:
            other = "dve" if eng == "act" else "act"
            d = self.last_psr.get(other)
            if d is not None:
                dep(d, True)
            self.last_psr[eng] = op
        for k in w:
            d = self.lastw.get(k)
            if d is not None:
                dep(d, False)
            for d2 in self.readers.get(k, ()):
                dep(d2, False)
        if dk is not None:
            d = self.dma_last.get(dk)
            if d is not None:
                dep(d, True)
            self.dma_hist.append(op)
            if len(self.dma_hist) > MAX_DMA_INFLIGHT:
                dep(self.dma_hist[-1 - MAX_DMA_INFLIGHT], True)
            self.dma_n[dk] = self.dma_n.get(dk, 0) + 1
            op.ordn = self.dma_n[dk]
            self.dma_last[dk] = op
        if self.cur_bar is not None:
            dep(self.cur_bar, True)
        op.deps = list(deps.values())
        for d, _ in op.deps:
            d.users.append(op)
        self.all.append(op)
        self.group.append(op)
        for k in r:
            self.readers.setdefault(k, []).append(op)
        for k in w:
            self.lastw[k] = op
            self.readers[k] = []
        return op

    def barrier(self):
        b = Op("bar", None, None, 0.0)
        b.pidx = len(self.all)
        b.group = self.group
        b.deps = [(o, True) for o in self.group]
        if self.cur_bar is not None:
            b.deps.append((self.cur_bar, True))
        for d, _ in b.deps:
            d.users.append(b)
        self.all.append(b)
        self.group = []
        self.cur_bar = b
        self.lastw = {}
        self.readers = {}

    def schedule(self):
        import heapq
        engs = ENGS + ("bar",)
        order = {e: [] for e in engs}
        readyq = {e: [] for e in engs}
        busy = {e: False for e in engs}
        ev = []
        seq = 0
        dma_free = 0.0
        for o in self.all:
            o.nd = len(o.deps)
            o.rt = 0.0
            if o.nd == 0:
                heapq.heappush(ev, (0.0, seq, 0, o))
                seq += 1

        def start(e, t):
            nonlocal seq, dma_free
            o = heapq.heappop(readyq[e])[1]
            busy[e] = True
            o.rt = t
            o.pos = len(order[e])
            order[e].append(o)
            if o.dk is not None:
                tend = t + 60.0
                dma_free = max(dma_free, t) + o.nbytes / DMA_BW
                o.fin = dma_free + DMA_LAT
            else:
                tend = t + o.cost
                o.fin = tend
            heapq.heappush(ev, (tend, seq, 1, e))
            seq += 1
            for u in o.users:
                u.nd -= 1
                lat = 0.0 if (u.eng == o.eng and o.dk is None) else SEM_LAT
                if o.fin + lat > u.rt:
                    u.rt = o.fin + lat
                if u.nd == 0:
                    heapq.heappush(ev, (u.rt, seq, 0, u))
                    seq += 1

        while ev:
            t, _, kind, x = heapq.heappop(ev)
            if kind == 0:
                e = x.eng
                heapq.heappush(readyq[e], (x.pidx, x))
                if not busy[e]:
                    start(e, t)
            else:
                busy[x] = False
                if readyq[x]:
                    start(x, t)
        assert sum(len(v) for v in order.values()) == len(self.all), "scheduler: dependency cycle"
        self.order = order
        self.est_ns = max(o.fin for o in self.all)
        last = {}
        dcnt = {}
        for b in order["bar"]:
            for o in b.group:
                if o.dk is not None:
                    dcnt[o.dk] = max(dcnt.get(o.dk, 0), o.ordn)
                elif o.fn is not None:
                    p = last.get(o.eng)
                    if p is None or o.pos > p.pos:
                        last[o.eng] = o
            b.lastpos = dict(last)
            b.dmacnt = dict(dcnt)
        import bisect
        kn = {e: {} for e in ENGS}
        hist = {e: {} for e in ENGS}
        last_drain = {e: -1 for e in ENGS}

        def learn(e, p, key, val):
            if kn[e].get(key, 0) < val:
                kn[e][key] = val
                h = hist[e].setdefault(key, ([], []))
                h[0].append(p)
                h[1].append(val)

        def want(op, key, val, dop):
            e = op.eng
            if kn[e].get(key, 0) >= val:
                return
            if dop is not None:
                dop.sig = True
                op.waits.append(dop)
            else:
                op.waits.append((key[1], val))
            learn(e, op.pos, key, val)
            if dop is not None:
                F = dop.eng
                for k2, h in hist[F].items():
                    i = bisect.bisect_right(h[0], dop.pos) - 1
                    if i >= 0:
                        learn(e, op.pos, k2, h[1][i])

        seq_ops = sorted((o for o in self.all if o.eng != "bar"), key=lambda o: (o.rt, o.pidx))
        for op in seq_ops:
            e = op.eng
            for d, raw in op.deps:
                if d.eng == "bar":
                    for F, lo in d.lastpos.items():
                        if F != e:
                            want(op, ("eng", F), lo.pos + 1, lo)
                    for dk_, n_ in d.dmacnt.items():
                        want(op, ("dma", dk_), n_, None)
                elif d.dk is not None:
                    want(op, ("dma", d.dk), d.ordn, None)
                elif d.eng == e and op.dk is None:
                    if e != "pe" and raw:
                        if USE_DRAIN:
                            if d.pos > last_drain[e]:
                                op.waits.append("DRAIN")
                                last_drain[e] = op.pos
                        else:
                            want(op, ("eng", e), d.pos + 1, d)
                else:
                    want(op, ("eng", d.eng), d.pos + 1, d)
        for e in ENGS:
            c = 0
            for o in order[e]:
                if o.dk is None and o.sig:
                    c += 1
                o.cnt = c

    def emit(self, eng, e, esem, dsem):
        for o in self.order[eng]:
            for wt in o.waits:
                if wt == "DRAIN":
                    e.drain()
                elif isinstance(wt, tuple):
                    e.wait_ge(dsem[wt[0]], 16 * wt[1])
                else:
                    e.wait_ge(esem[wt.eng], wt.cnt)
            if o.fn is None:
                continue
            ins = o.fn(e)
            if o.dk is not None:
                ins.then_inc(dsem[o.dk], 16)
            elif o.sig:
                ins.then_inc(esem[eng], 1)


class Arena:
    def __init__(self, t, nbytes, base=0):
        self.t = t
        self.nbytes = nbytes
        self.off = base

    def alloc(self, shape, dt):
        n = 1
        for s in shape[1:]:
            n *= s
        nb = n * (4 if dt == F32 else 2)
        nb = (nb + 31) // 32 * 32
        o = self.off
        assert o + nb <= self.nbytes, f"arena overflow {o}+{nb}>{self.nbytes}"
        self.off = o + nb
        a = self.t[:, o // 4:(o + nb) // 4]
        if dt != F32:
            a = a.bitcast(dt)
        a = a[:, 0:n]
        if len(shape) == 3:
            a = a.rearrange("p (a b) -> p a b", a=shape[1])
        elif len(shape) == 4:
            a = a.rearrange("p (a b c) -> p a b c", a=shape[1], b=shape[2])
        return a


def build(debug=False, stop_after=None, skip=(), cstage=99, var=0):
    nc = bass.Bass("TRN2", target_bir_lowering=False)

    def din(name, shape, dt=F32):
        return nc.dram_tensor(name, list(shape), dt, kind="ExternalInput").ap()

    x = din("x", [T, D])
    w_in = din("w_in", [D, DIN])
    p_a = din("p_a", [D, D])
    p_b = din("p_b", [D, D])
    w_o = din("w_o", [D, D])
    w_up = din("w_up", [D, 2 * DFF])
    w_down = din("w_down", [DFF, D])
    norm1 = din("norm1", [1, D])
    norm2 = din("norm2", [1, D])
    m_out_norm = din("m_out_norm", [1, D])
    gate_bias = din("gate_bias", [1, 16])
    q_norm = din("q_norm", [1, 64])
    k_norm = din("k_norm", [1, 64])
    lam4 = din("lam4", [1, 256])
    a_out_norm = din("a_out_norm", [1, 128])
    conv_wp = din("conv_wp", [128, 44 * 3])
    conv_bp = din("conv_bp", [128, 44])
    cst = din("cst", [128, 4 * 128])
    rope = din("rope", [128, 2 * 32 * 64])
    skind = "ExternalOutput" if debug else "Internal"
    haT = nc.dram_tensor("haT", [D, T], BF16, kind=skind).ap()
    hbT = nc.dram_tensor("hbT", [D, T], BF16, kind=skind).ap()
    yTs = nc.dram_tensor("yTs", [D, T], BF16, kind=skind).ap()
    x1s = nc.dram_tensor("x1s", [T, D], F32, kind=skind).ap()
    xn2T = nc.dram_tensor("xn2T", [D, T], BF16, kind=skind).ap()
    out = nc.dram_tensor("out", [T, D], F32, kind="ExternalOutput").ap()

    w_in_v = w_in.rearrange("(c p) n -> p c n", p=128)
    haT_v = haT.rearrange("(f p) t -> p f t", p=128)
    hbT_v = hbT.rearrange("(f p) t -> p f t", p=128)
    yT_v = yTs.rearrange("(f p) t -> p f t", p=128)
    xn2T_v = xn2T.rearrange("(f p) t -> p f t", p=128)

    S = Sched()
    ctx = ExitStack()
    arena_t = ctx.enter_context(nc.sbuf_tensor("arena", [128, ARENA_BYTES // 4], F32))
    psall = ctx.enter_context(nc.psum_tensor("psall", [128, 4096], F32))
    PS = [psall[:, b * 512:(b + 1) * 512] for b in range(8)]
    AR = Arena(arena_t, ARENA_BYTES)

    def fsz(ap):
        n = 1
        for d_ in ap.shape[1:]:
            n *= d_
        return n

    def mm(o, lhsT, rhs, start, stop, r=(), w=(), sgc=False, cost=None):
        n = fsz(rhs)
        c = max(64, n) / 3.2 + 15.0
        if lhsT.dtype == F32:
            c *= 4
        if cost is not None:
            c = cost
        if sgc:
            return S.add("pe", lambda e: e.matmul(o, lhsT=lhsT, rhs=rhs, start=start, stop=stop, skip_group_check=True),
                         r, w, cost=c)
        return S.add("pe", lambda e: e.matmul(o, lhsT=lhsT, rhs=rhs, start=start, stop=stop), r, w, cost=c)

    def tr(o, in_, ident, r=(), w=()):
        return S.add("pe", lambda e: e.transpose(o, in_, ident), r, w, cost=110.0)

    def act(o, in_, func, r=(), w=(), bias=None, scale=None, accum=None):
        kw = {}
        c = fsz(in_) / 1.15 + 110.0
        if bias is not None:
            kw["bias"] = bias
            if not isinstance(bias, float):
                c += 90.0
        if scale is not None:
            kw["scale"] = scale
            if not isinstance(scale, float):
                c += 90.0
        if accum is not None:
            kw["accum_out"] = accum
            c += 90.0
        return S.add("act", lambda e: e.activation(out=o, in_=in_, func=func, **kw), r, w, cost=c)

    def ecost(eng, n):
        return (2.2 * n + 120.0) if eng == "pool" else (n / 1.4 + 50.0)

    def ts(eng, o, in0, s1, op0, r=(), w=(), s2=None, op1=None):
        c = ecost(eng, fsz(o))
        if op1 is None:
            return S.add(eng, lambda e: e.tensor_scalar(out=o, in0=in0, scalar1=s1, scalar2=None, op0=op0), r, w, cost=c)
        return S.add(eng, lambda e: e.tensor_scalar(out=o, in0=in0, scalar1=s1, scalar2=s2, op0=op0, op1=op1), r, w, cost=c)

    def tt(eng, o, in0, in1, op, r=(), w=()):
        return S.add(eng, lambda e: e.tensor_tensor(out=o, in0=in0, in1=in1, op=op), r, w, cost=ecost(eng, fsz(o)))

    def stt(o, in0, sc, in1, op0, op1, r=(), w=()):
        return S.add("dve", lambda e: e.scalar_tensor_tensor(out=o, in0=in0, scalar=sc, in1=in1, op0=op0, op1=op1), r, w,
                     cost=ecost("dve", fsz(o)))

    def red(o, in_, op, r=(), w=(), negate=None):
        c = ecost("dve", fsz(in_))
        if negate:
            return S.add("dve", lambda e: e.tensor_reduce(out=o, in_=in_, axis=AX.X, op=op, negate=True), r, w, cost=c)
        return S.add("dve", lambda e: e.tensor_reduce(out=o, in_=in_, axis=AX.X, op=op), r, w, cost=c)

    def recip(o, in_, r=(), w=()):
        return S.add("dve", lambda e: e.reciprocal(out=o, in_=in_), r, w, cost=2.0 * fsz(o) + 70.0)

    def cp(eng, o, in_, r=(), w=()):
        if eng == "act":
            return S.add(eng, lambda e: e.activation(out=o, in_=in_, func=AF.Copy), r, w, cost=fsz(o) / 1.15 + 110.0)
        return S.add(eng, lambda e: e.tensor_copy(out=o, in_=in_), r, w, cost=ecost(eng, fsz(o)))

    def mset(eng, o, val, r=(), w=()):
        return S.add(eng, lambda e: e.memset(o, val), r, w, cost=ecost(eng, fsz(o)))

    def dma(o, in_, dk, r=(), w=()):
        nb = fsz(o) * o.shape[0] * (4 if o.dtype == F32 else 2)
        return S.add("sp", lambda e: e.dma_start(out=o, in_=in_), r, w, dk=dk, nbytes=nb)

    def psb(b, n0, n1):
        return PS[b][:, :].bitcast(BF16)[:, n0:n1]

    cstt = AR.alloc([128, 4, 128], F32)
    identf = cstt[:, 0, :]
    onesf = cstt[:, 1, :]
    U = [cstt[:, 2, :], cstt[:, 3, :]]
    identb = AR.alloc([128, 128], BF16)
    smallv = AR.alloc([128, 64], F32)
    base_consts = AR.off
    xnT = AR.alloc([128, KC, T], BF16)
    base_persist = AR.off

    dma(cstt, cst.rearrange("p (a b) -> p a b", a=4), "cst", w=["cst"])
    cp("dve", identb, identf, r=["cst"], w=["identb"])

    xt = [AR.alloc([128, D], F32) for _ in range(2)]
    g1bc = AR.alloc([128, D], F32)
    junk = AR.alloc([128, D], BF16)
    xs = [AR.alloc([128, D], BF16) for _ in range(2)]
    ssq = AR.alloc([128, 3 * NT], F32)
    dma(g1bc, norm1.partition_broadcast(128), "g1bc", w=["g1bc"])
    for i in range(NT):
        b = i % 2
        dma(xt[b], x[i * 128:(i + 1) * 128, :], f"xt{b}", w=[f"xt{b}"])
        act(junk, xt[b], AF.Square, r=[f"xt{b}"], w=["junk", f"ss{i}"], accum=ssq[:, i:i + 1])
        act(ssq[:, NT + i:NT + i + 1], ssq[:, i:i + 1], AF.Ln, r=[f"ss{i}"], w=[f"ln{i}"], bias=EPS, scale=1.0 / D)
        act(ssq[:, 2 * NT + i:2 * NT + i + 1], ssq[:, NT + i:NT + i + 1], AF.Exp, r=[f"ln{i}"], w=[f"rs{i}"], scale=-0.5)
        stt(xs[b], xt[b], ssq[:, 2 * NT + i:2 * NT + i + 1], g1bc, ALU.mult, ALU.mult,
            r=[f"xt{b}", f"rs{i}", "g1bc"], w=[f"xs{b}"])
        for c in range(KC):
            tr(psb(b, c * 128, (c + 1) * 128), xs[b][:, c * 128:(c + 1) * 128], identb,
               r=[f"xs{b}", "identb"], w=[f"ps{b}"])
        cp("dve", xnT[:, :, i * 128:(i + 1) * 128],
           psb(b, 0, 1024).rearrange("p (c t) -> p c t", c=KC), r=[f"ps{b}"], w=[])
    S.barrier()
    if stop_after == "A":
        return finish(nc, S, ctx, out, debug_dump=(xnT, haT_v))

    AR.off = base_persist
    cs2 = AR.alloc([128, 2, 32, 64], F32)
    wst = AR.alloc([128, KC, 384], F32)
    wbf = [AR.alloc([128, KC, 384], BF16) for _ in range(2)]
    qkT = [AR.alloc([128, 2, T], BF16) for _ in range(2)]
    vaug = [AR.alloc([128, NT, 130], BF16) for _ in range(2)]
    pjs = [AR.alloc([128, 384], F32) for _ in range(3)]
    sqb = [AR.alloc([128, 256], F32) for _ in range(3)]
    xnq = [AR.alloc([128, 256], F32) for _ in range(3)]
    rA = [AR.alloc([128, 256], F32) for _ in range(3)]
    rB = [AR.alloc([128, 256], F32) for _ in range(3)]
    qkb = [AR.alloc([128, 256], BF16) for _ in range(3)]
    st4 = [AR.alloc([128, 12], F32) for _ in range(3)]
    pT = [AR.alloc([128, 1024], BF16) for _ in range(3)]
    gqk = AR.alloc([128, 256], F32)
    aon = AR.alloc([128, 128], F32)
    lamt = AR.alloc([128, 256], F32)
    lamp = AR.alloc([128, 128], F32)
    accs = [AR.alloc([128, 9, 129], F32) for _ in range(2)]
    od4 = [AR.alloc([128, 4, 128], F32) for _ in range(2)]
    t24 = AR.alloc([128, 4, 128], F32)
    sq4 = AR.alloc([128, 4, 128], F32)
    ob4 = [AR.alloc([128, 4, 128], BF16) for _ in range(2)]
    e8 = [AR.alloc([128, 24], F32) for _ in range(2)]
    hbst = [AR.alloc([128, 512], BF16) for _ in range(2)]

    dma(cs2, rope.rearrange("p (a i d) -> p a i d", a=2, i=32), "cs2", w=["cs2"])
    for g in range(4):
        dma(gqk[:, g * 64:(g + 1) * 64], (q_norm if g < 2 else k_norm).partition_broadcast(128), "gqk", w=[f"gqk{g}"])
    ts("dve", gqk[:, 0:128], gqk[:, 0:128], 0.125, ALU.mult, r=["gqk0", "gqk1"], w=["gqk0", "gqk1"])
    dma(aon, a_out_norm.partition_broadcast(128), "aon", w=["aon"])
    ts("dve", aon, aon, 1.0 - LAM_INIT, ALU.mult, r=["aon"], w=["aon"])
    dma(lamt, lam4.partition_broadcast(128), "lamt", w=["lamt"])
    tt("dve", lamp[:, 0:64], lamt[:, 0:64], lamt[:, 64:128], ALU.mult, r=["lamt"], w=["lamp"])
    tt("dve", lamp[:, 64:128], lamt[:, 128:192], lamt[:, 192:256], ALU.mult, r=["lamt"], w=["lamp"])
    red(smallv[:, 0:2], lamp.rearrange("p (a b) -> p a b", a=2), ALU.add, r=["lamp"], w=["lam0"])
    act(smallv[:, 4:6], smallv[:, 0:2], AF.Exp, r=["lam0"], w=["lam1"])
    tt("dve", smallv[:, 2:3], smallv[:, 4:5], smallv[:, 5:6], ALU.subtract, r=["lam1"], w=["lam2"])
    ts("dve", smallv[:, 3:4], smallv[:, 2:3], LAM_INIT, ALU.add, r=["lam2"], w=["neglam"], s2=-1.0, op1=ALU.mult)
    neglam = smallv[:, 3:4]
    for sl in range(2):
        mset("dve", vaug[sl][:, :, 128:130], 1.0, w=[f"vaug1_{sl}"])

    def load_head_w(h):
        sl = h % 2
        for n, off in enumerate((OFF_AQ, OFF_AK, OFF_AV)):
            dma(wst[:, :, n * 128:(n + 1) * 128], w_in_v[:, :, off + h * 128:off + (h + 1) * 128],
                "wst", w=[f"wst_{n}"])
        cp("pool", wbf[sl], wst, r=[f"wst_{n}" for n in range(3)], w=[f"wbf{sl}"])

    def b2_tile(h, i, pb=7):
        sl = h % 2
        b = i % 3
        pj = PS[pb]
        for c in range(KC):
            mm(pj[:, 0:384], xnT[:, c, i * 128:(i + 1) * 128], wbf[sl][:, c, :], c == 0, c == KC - 1,
               r=[f"wbf{sl}"], w=[f"ps{pb}"])
        cp("act", pjs[b], pj[:, 0:384], r=[f"ps{pb}"], w=[f"pjs{b}"])
        cp("pool", vaug[sl][:, i, 0:128], pjs[b][:, 256:384], r=[f"pjs{b}"], w=[f"vaug{sl}_{i}"])
        tt("pool", sqb[b], pjs[b][:, 0:256], pjs[b][:, 0:256], ALU.mult, r=[f"pjs{b}"], w=[f"sqb{b}"])
        red(st4[b][:, 0:4], sqb[b].rearrange("p (g d) -> p g d", g=4), ALU.add, r=[f"sqb{b}"], w=[f"st4a{b}"])
        act(st4[b][:, 4:8], st4[b][:, 0:4], AF.Ln, r=[f"st4a{b}"], w=[f"st4b{b}"], bias=EPS, scale=1.0 / 64)
        act(st4[b][:, 8:12], st4[b][:, 4:8], AF.Exp, r=[f"st4b{b}"], w=[f"st4c{b}"], scale=-0.5)
        for g in range(4):
            stt(xnq[b][:, g * 64:(g + 1) * 64], pjs[b][:, g * 64:(g + 1) * 64], st4[b][:, 8 + g:9 + g],
                gqk[:, g * 64:(g + 1) * 64], ALU.mult, ALU.mult,
                r=[f"pjs{b}", f"st4c{b}", f"gqk{g}"], w=[f"xnq{b}"])
        x3 = xnq[b].rearrange("p (g d) -> p g d", g=4)
        a3 = rA[b].rearrange("p (g d) -> p g d", g=4)
        b3 = rB[b].rearrange("p (g d) -> p g d", g=4)
        cosb = cs2[:, 0, i, :].unsqueeze(1).broadcast_to([128, 4, 64])
        sin_lo = cs2[:, 1, i, 0:32].unsqueeze(1).broadcast_to([128, 4, 32])
        sin_hi = cs2[:, 1, i, 32:64].unsqueeze(1).broadcast_to([128, 4, 32])
        tt("dve", a3, x3, cosb, ALU.mult, r=[f"xnq{b}", "cs2"], w=[f"rA{b}"])
        tt("pool", b3[:, :, 0:32], x3[:, :, 32:64], sin_lo, ALU.mult, r=[f"xnq{b}", "cs2"], w=[f"rB{b}"])
        tt("pool", b3[:, :, 32:64], x3[:, :, 0:32], sin_hi, ALU.mult, r=[f"xnq{b}", "cs2"], w=[f"rB{b}"])
        tt("dve", qkb[b], rA[b], rB[b], ALU.add, r=[f"rA{b}", f"rB{b}"], w=[f"qkb{b}"])

    def b2_tile_p2(h, i, pb=7):
        sl = h % 2
        b = i % 3
        tr(psb(pb, 768, 896), qkb[b][:, 0:128], identb, r=[f"qkb{b}"], w=[f"ps{pb}"])
        tr(psb(pb, 896, 1024), qkb[b][:, 128:256], identb, r=[f"qkb{b}"], w=[f"ps{pb}"])
        cp("act", qkT[sl][:, :, i * 128:(i + 1) * 128], psb(pb, 768, 1024).rearrange("p (a t) -> p a t", a=2),
           r=[f"ps{pb}"], w=[f"qkT{sl}"])

    acc3 = psall[:, 4 * 512:7 * 512].rearrange("p (b n) -> p b n", b=3)

    def b3_epilogue(h, g):
        k2 = g % 2
        a9 = accs[k2]
        a3 = a9.rearrange("p a n -> p (a n)").rearrange("p (b n) -> p b n", b=3)
        e_ = e8[k2]
        cp("act", a3, acc3[:, :, 0:387], r=["ps4", "ps5", "ps6"], w=[f"accs{k2}"])
        recip(e_[:, 0:8], a9[:, 0:8, 128], r=[f"accs{k2}"], w=[f"e8a{k2}"])
        ts("dve", e_[:, 8:12], e_[:, 4:8], neglam, ALU.mult, r=[f"e8a{k2}", "neglam"], w=[f"e8b{k2}"])
        o_ = od4[k2]
        tt("pool", o_, a9[:, 0:4, 0:128], e_[:, 0:4].unsqueeze(2).broadcast_to([128, 4, 128]), ALU.mult,
           r=[f"accs{k2}", f"e8a{k2}"], w=[f"od4{k2}"])
        tt("pool", t24, a9[:, 4:8, 0:128], e_[:, 8:12].unsqueeze(2).broadcast_to([128, 4, 128]), ALU.mult,
           r=[f"accs{k2}", f"e8b{k2}"], w=["t24"])
        tt("dve", o_, o_, t24, ALU.add, r=[f"od4{k2}", "t24"], w=[f"od4{k2}"])
        tt("pool", sq4, o_, o_, ALU.mult, r=[f"od4{k2}"], w=["sq4"])
        red(e_[:, 12:16], sq4, ALU.add, r=["sq4"], w=[f"e8c{k2}"])
        act(e_[:, 12:16], e_[:, 12:16], AF.Ln, r=[f"e8c{k2}"], w=[f"e8c{k2}"], bias=EPS, scale=1.0 / 128)
        act(e_[:, 16:20], e_[:, 12:16], AF.Exp, r=[f"e8c{k2}"], w=[f"e8d{k2}"], scale=-0.5)
        tt("dve", o_, o_, e_[:, 16:20].unsqueeze(2).broadcast_to([128, 4, 128]), ALU.mult,
           r=[f"od4{k2}", f"e8d{k2}"], w=[f"od4{k2}"])
        tt("pool", ob4[k2], o_, aon.unsqueeze(1).broadcast_to([128, 4, 128]), ALU.mult,
           r=[f"od4{k2}", "aon"], w=[f"ob4{k2}"])

    def b3_epilogue_p2(h, g):
        k2 = g % 2
        for qt in range(4):
            tr(psb(7, qt * 128, (qt + 1) * 128), ob4[k2][:, qt, :], identb, r=[f"ob4{k2}"], w=["ps7"])
        cp("act", hbst[k2], psb(7, 0, 512), r=["ps7"], w=[f"hbst{k2}"])
        dma(hbT_v[:, h, g * 512:(g + 1) * 512], hbst[k2], f"hbst{k2}", r=[f"hbst{k2}"], w=[f"hbT{h}_{g}"])

    def b3_head(h, inter):
        sl = h % 2
        steps = [(g, j) for g in range(8) for j in range(NT)]
        ns = len(steps)

        def qk(s):
            g, j = steps[s]
            pb_ = 2 * (s % 2)
            for c in range(2):
                mm(PS[pb_ + c], qkT[sl][c * 64:(c + 1) * 64, 1, j * 128:(j + 1) * 128],
                   qkT[sl][c * 64:(c + 1) * 64, 0, g * 512:(g + 1) * 512], True, True,
                   r=[f"qkT{sl}"], w=[f"ps{pb_ + c}"], cost=170.0)
            act(pT[s % 3], psall[:, pb_ * 512:(pb_ + 2) * 512], AF.Exp, r=[f"ps{pb_}", f"ps{pb_ + 1}"],
                w=[f"pT{s % 3}"])

        pending = []
        qk(0)
        for s in range(ns):
            if s + 1 < ns:
                qk(s + 1)
            g, j = steps[s]
            for a in range(8):
                c, qt = divmod(a, 4)
                bank = 4 + a // 3
                col = (a % 3) * 129
                mm(PS[bank][:, col:col + 129], pT[s % 3][:, c * 512 + qt * 128:c * 512 + (qt + 1) * 128],
                   vaug[sl][:, j, 0:129], j == 0 and a % 3 == 0, j == NT - 1,
                   r=[f"pT{s % 3}", f"vaug{sl}_{j}", f"vaug1_{sl}"], w=[f"ps{bank}"], sgc=True)
            for _dm in range(NDUM):
                mm(PS[6][:, 258:387], pT[s % 3][:, 896:1024], vaug[sl][:, j, 0:129], False, j == NT - 1,
                   r=[f"pT{s % 3}", f"vaug{sl}_{j}", f"vaug1_{sl}"], w=["ps6"], sgc=True)
            for it in list(pending):
                if it[0] <= s:
                    pending.remove(it)
                    it[1]()
            if j == NT - 1:
                b3_epilogue(h, g)
                pending.append([s + 7, (lambda gg: (lambda: b3_epilogue_p2(h, gg)))(g)])
            if s % 8 == 2 and inter:
                p1, p2 = inter.pop(0)
                p1()
                if p2 is not None:
                    pending.append([s + 5, p2])
        for it in pending:
            it[1]()
        while inter:
            p1, p2 = inter.pop(0)
            p1()
            if p2 is not None:
                p2()

    nheads = 0 if "B" in skip else 8
    if nheads:
        load_head_w(0)
        load_head_w(1)
        for i in range(NT):
            b2_tile(0, i, pb=i % 8)
            if i >= 2:
                b2_tile_p2(0, i - 2, pb=(i - 2) % 8)
        for i in range(NT - 2, NT):
            b2_tile_p2(0, i, pb=i % 8)
    for h in range(nheads):
        inter = []
        if h + 1 < nheads:
            inter = [((lambda hh, ii: (lambda: b2_tile(hh, ii)))(h + 1, i),
                      (lambda hh, ii: (lambda: b2_tile_p2(hh, ii)))(h + 1, i)) for i in range(NT)]
            if h + 2 < nheads:
                inter.insert(NT // 2, ((lambda hh: (lambda: load_head_w(hh)))(h + 2), None))
        b3_head(h, inter)
    S.barrier()
    if stop_after == "B":
        return finish(nc, S, ctx, out)

    AR.off = base_persist
    wgst = AR.alloc([128, KC, 16], F32)
    wgb = AR.alloc([128, KC, 16], BF16)
    G = AR.alloc([128, NT, 16], F32)
    gb16 = AR.alloc([128, 16], F32)

    def arr():
        return AR.alloc([128, 2, 128], F32)

    SPl, NB, GTn, E_, EMAX, MN, MP, A1, A2, W1, W2, UP, IW, ED, T1, T2 = [arr() for _ in range(16)]
    emT = AR.alloc([128, 2], F32)
    dg1 = AR.alloc([128, 128], F32)
    dg = [dg1, dg1]
    zero4 = AR.alloc([128, 4], F32)
    mst = AR.alloc([128, KC, 128], F32)
    wmD = [AR.alloc([128, KC, 768], BF16) for _ in range(2)]
    qT = AR.alloc([128, T], BF16)
    kT = AR.alloc([128, T], BF16)
    ktok = AR.alloc([128, NT, 128], BF16)
    vaug2 = AR.alloc([128, NT, 258], BF16)
    hfwd = AR.alloc([128, NT, 256], F32)
    CnD = [AR.alloc([128, 264], F32) for _ in range(2)]
    CbD = [[AR.alloc([128, 264], BF16) for _ in range(2)] for _ in range(2)]
    STD = [[AR.alloc([128, 128], BF16) for _ in range(3)] for _ in range(2)]
    kwD = [[AR.alloc([128, 128], BF16) for _ in range(2)] for _ in range(2)]
    hm = [AR.alloc([128, 256], F32) for _ in range(2)]
    sg = [AR.alloc([128, 256], F32) for _ in range(2)]
    hn = [AR.alloc([128, 256], F32) for _ in range(2)]
    hab = [AR.alloc([128, 256], BF16) for _ in range(2)]
    hast = [AR.alloc([128, 256], BF16) for _ in range(2)]
    mnbc = AR.alloc([128, D], F32)
    junk2 = AR.alloc([128, 256], BF16)
    dnn = [AR.alloc([128, 8], F32) for _ in range(4)]

    def v4(a):
        return a.rearrange("p d (c h) -> p d c h", c=NT)

    Gv = G.rearrange("p i (t h) -> p i t h", t=4)
    dma(wgst, w_in_v[:, :, OFF_MG:OFF_MG + 16], "wgst", w=["wgst"])
    cp("pool", wgb, wgst, r=["wgst"], w=["wgb"])
    dma(gb16, gate_bias.partition_broadcast(128), "gb16", w=["gb16"])
    dma(mnbc, m_out_norm.partition_broadcast(128), "mnbc", w=["mnbc"])
    mset("pool", vaug2[:, :, 256:258], 1.0, w=["vaug2one"])
    mset("pool", zero4, 0.0, w=["zero4"])
    for i in range(NT):
        for c in range(KC):
            mm(PS[0][:, i * 16:(i + 1) * 16], xnT[:, c, i * 128:(i + 1) * 128], wgb[:, c, :], c == 0, c == KC - 1,
               r=["wgb"], w=["ps0"])
    tt("dve", G, PS[0][:, :].rearrange("p (i k) -> p i k", i=NT), gb16.unsqueeze(1).broadcast_to([128, NT, 16]),
       ALU.add, r=["ps0", "gb16"], w=["G"])
    for d in range(2):
        act(v4(SPl)[:, d], Gv[:, :, 2 * d + 1, :], AF.Exp, r=["G"], w=[f"SPl{d}"], scale=-1.0)
        act(SPl[:, d], SPl[:, d], AF.Ln, r=[f"SPl{d}"], w=[f"SPl{d}"], bias=1.0, scale=1.0)
        mm(PS[1][:, d * 128:(d + 1) * 128], U[d], SPl[:, d], True, True, r=[f"SPl{d}"], w=["ps1"])
        mm(PS[2][:, d * 128:(d + 1) * 128], onesf, SPl[:, d], True, True, r=[f"SPl{d}"], w=["ps2"])
    cp("dve", NB, PS[1][:, 0:256].rearrange("p (d k) -> p d k", d=2), r=["ps1"], w=["NB"])
    cp("dve", GTn, PS[2][:, 0:256].rearrange("p (d k) -> p d k", d=2), r=["ps2"], w=["GTn"])
    for d in range(2):
        tt("dve", v4(E_)[:, d], v4(NB)[:, d], Gv[:, :, 2 * d, :], ALU.add, r=["NB", "G"], w=[f"E{d}"])
        tr(PS[3][:, d * 128:(d + 1) * 128], E_[:, d], identf, r=[f"E{d}"], w=["ps3"])
        S.add("dve", (lambda dd: (lambda e: e.tensor_reduce(out=emT[:, dd:dd + 1], in_=PS[3][:, dd * 128:(dd + 1) * 128],
                                                            axis=AX.X, op=ALU.max)))(d), ["ps3"], [f"emT{d}"])
        ts("dve", dg[d], identf, emT[:, d:d + 1], ALU.mult, r=[f"emT{d}"], w=["dg"])
        mm(PS[4][:, d * 128:(d + 1) * 128], onesf, dg[d], True, True, r=["dg"], w=["ps4"])
    cp("dve", EMAX, PS[4][:, 0:256].rearrange("p (d k) -> p d k", d=2), r=["ps4"], w=["EMAX"])
    if cstage == 0:
        return finish(nc, S, ctx, out)
    for d in range(2):
        order = list(range(NT)) if d == 0 else list(range(NT - 1, -1, -1))
        prev = zero4
        pk = "zero4"
        for c in order:
            sl4 = slice(c * 4, (c + 1) * 4)
            tt("dve", T1[:, d, sl4], prev, EMAX[:, d, sl4], ALU.max, r=[pk, "EMAX"], w=[f"T1s{d}_{c}"])
            tt("dve", MN[:, d, sl4], T1[:, d, sl4], GTn[:, d, sl4], ALU.subtract, r=[f"T1s{d}_{c}", "GTn"], w=[f"MN{d}_{c}"])
            prev = MN[:, d, sl4]
            pk = f"MN{d}_{c}"
    mnk = [f"MN{d}_{c}" for d in range(2) for c in range(NT)]
    cp("dve", MP[:, 0, 4:128], MN[:, 0, 0:124], r=mnk, w=["MP"])
    mset("dve", MP[:, 0, 0:4], 0.0, w=["MP"])
    cp("dve", MP[:, 1, 0:124], MN[:, 1, 4:128], r=mnk, w=["MP"])
    mset("dve", MP[:, 1, 124:128], 0.0, w=["MP"])
    tt("dve", T1, MP, MN, ALU.subtract, r=["MP"] + mnk, w=["T1"])
    tt("dve", T1, T1, GTn, ALU.subtract, r=["T1", "GTn"], w=["T1"])
    act(A1, T1, AF.Exp, r=["T1"], w=["A1"])
    tt("dve", T2, EMAX, MN, ALU.subtract, r=["EMAX"] + mnk, w=["T2"])
    tt("dve", T2, T2, GTn, ALU.subtract, r=["T2", "GTn"], w=["T2"])
    act(A2, T2, AF.Exp, r=["T2"], w=["A2"])
    tt("dve", T1, E_, EMAX, ALU.subtract, r=["E0", "E1", "EMAX", "A1"], w=["T1"])
    act(W1, T1, AF.Exp, r=["T1"], w=["W1"])
    tt("dve", UP, EMAX, MP, ALU.max, r=["EMAX", "MP"], w=["UP"])
    tt("dve", T2, E_, UP, ALU.subtract, r=["E0", "E1", "UP", "A2"], w=["T2"])
    act(W2, T2, AF.Exp, r=["T2"], w=["W2"])
    tt("dve", T1, MP, UP, ALU.subtract, r=["MP", "UP", "W1"], w=["T1"])
    act(IW, T1, AF.Exp, r=["T1"], w=["IW"])
    tt("dve", T2, NB, UP, ALU.subtract, r=["NB", "UP", "W2"], w=["T2"])
    act(ED, T2, AF.Exp, r=["T2"], w=["ED"])

    if cstage == 1:
        return finish(nc, S, ctx, out)
    KSC = 128.0 ** -0.5
    def load_mw(h):
        wm_ = wmD[h % 2]
        q_ = h % 2
        subs = ((OFF_MQ + h * 128, 0, 0), (OFF_MK + h * 128, 128, 0),
                (OFF_MV + h * 256, 256, 1), (OFF_MV + h * 256 + 128, 384, 1),
                (OFF_MO + h * 256, 512, 2), (OFF_MO + h * 256 + 128, 640, 2))
        for off, dc, part in subs:
            dma(mst, w_in_v[:, :, off:off + 128], "mst", w=["mst"])
            cp("pool", wm_[:, :, dc:dc + 128], mst, r=["mst"], w=[f"wm{part}_{q_}"])

    nh_c = 4 if "C" not in skip else 0
    if nh_c:
        load_mw(0)
    for h in range(nh_c):
        wm = wmD[h % 2]
        wq = h % 2
        if h + 1 < nh_c:
            load_mw(h + 1)
        if cstage == 10:
            return finish(nc, S, ctx, out)
        n = 0
        for tb in range(8):
            for which, dst, scl in ((0, qT, 1.0), (1, kT, KSC)):
                bk = n % 2
                n += 1
                for c in range(KC):
                    mm(PS[bk][:, :], wm[:, c, which * 128:(which + 1) * 128], xnT[:, c, tb * 512:(tb + 1) * 512],
                       c == 0, c == KC - 1, r=[f"wm0_{wq}"], w=[f"ps{bk}"])
                act(dst[:, tb * 512:(tb + 1) * 512], PS[bk][:, :], AF.Identity, r=[f"ps{bk}"],
                    w=[("qT" if which == 0 else "kT")], scale=scl)
        if cstage == 11:
            return finish(nc, S, ctx, out)
        for i in range(NT):
            bk = 2 + i % 2
            for c in range(KC):
                mm(PS[bk][:, 0:384], xnT[:, c, i * 128:(i + 1) * 128], wm[:, c, 128:512], c == 0, c == KC - 1,
                   r=[f"wm0_{wq}", f"wm1_{wq}"], w=[f"ps{bk}"])
            cp("act", vaug2[:, i, 0:256], PS[bk][:, 128:384], r=[f"ps{bk}"], w=[f"v2_{i}"])
            ts("dve", ktok[:, i, :], PS[bk][:, 0:128], KSC, ALU.mult, r=[f"ps{bk}"], w=[f"ktok{i}"])
        if cstage == 2:
            return finish(nc, S, ctx, out)
        orders = [list(range(NT)), list(range(NT - 1, -1, -1))]
        for d in range(2):
            mset("pool", CnD[d], 0.0, w=[f"Cn{d}"])
            mset("pool", CbD[d][0], 0.0, w=[f"Cb{d}_0"])

        def pre(d, idx):
            c = orders[d][idx]
            bk = 4 + d
            mm(PS[bk][:, 0:128], kT[:, c * 128:(c + 1) * 128], qT[:, c * 128:(c + 1) * 128], True, True,
               r=["qT", "kT"], w=[f"ps{bk}"])
            stt(STD[d][idx % 3], PS[bk][:, 0:128], W2[:, d, c * 4 + h:c * 4 + h + 1], U[d], ALU.mult, ALU.mult,
                r=[f"ps{bk}", "W2"], w=[f"ST{d}_{idx % 3}"])

        def chunk(d, idx, ncomb):
            c = orders[d][idx]
            hb_ = 6 + d
            k2 = idx % 2
            dk4 = (2 * idx + d) % 4
            dn = dnn[dk4]
            col = c * 4 + h
            Cn_ = CnD[d]
            mm(PS[hb_][:, 0:257], STD[d][idx % 3], vaug2[:, c, 0:257], True, False,
               r=[f"ST{d}_{idx % 3}", f"v2_{c}", "vaug2one"], w=[f"ps{hb_}"])
            mm(PS[hb_][:, 0:257], qT[:, c * 128:(c + 1) * 128], CbD[d][k2][:, 0:257], False, True,
               r=["qT", f"Cb{d}_{k2}"], w=[f"ps{hb_}"])
            if idx + 1 < NT:
                cn_ = orders[d][idx + 1]
                kw_ = kwD[d][idx % 2]
                ts("pool", kw_, ktok[:, c, :], W1[:, d, col:col + 1], ALU.mult,
                   r=[f"ktok{c}", "W1"], w=[f"kw{d}_{idx % 2}"], s2=1.0, op1=ALU.mult)
                mm(PS[1][:, 0:257], kw_, vaug2[:, c, 0:257], True, True,
                   r=[f"kw{d}_{idx % 2}", f"v2_{c}", "vaug2one"], w=["ps1"])
                ts("dve", Cn_[:, 0:257], Cn_[:, 0:257], A1[:, d, col:col + 1], ALU.mult, r=[f"Cn{d}", "A1"], w=[f"Cn{d}"])
                stt(Cn_[:, 0:257], PS[1][:, 0:257], A2[:, d, col:col + 1], Cn_[:, 0:257], ALU.mult, ALU.add,
                    r=["ps1", "A2", f"Cn{d}"], w=[f"Cn{d}"])
                k3 = (idx + 1) % 2
                ts("pool", CbD[d][k3][:, 0:257], Cn_[:, 0:257], IW[:, d, cn_ * 4 + h:cn_ * 4 + h + 1], ALU.mult,
                   r=[f"Cn{d}", "IW"], w=[f"Cb{d}_{k3}"], s2=1.0, op1=ALU.mult)
            ts("dve", dn[:, 5:6], PS[hb_][:, 256:257], -1.0, ALU.mult, r=[f"ps{hb_}", "ED"], w=[f"dn{dk4}z"],
               s2=ED[:, d, col:col + 1], op1=ALU.max)
            stt(dn[:, 0:1], PS[hb_][:, 256:257], 1.0, dn[:, 5:6], ALU.mult, ALU.max,
                r=[f"ps{hb_}", f"dn{dk4}z"], w=[f"dn{dk4}a"])
            recip(dn[:, 1:2], dn[:, 0:1], r=[f"dn{dk4}a"], w=[f"dn{dk4}b"])
            if idx < NT // 2:
                ts("dve", hfwd[:, c, :], PS[hb_][:, 0:256], dn[:, 1:2], ALU.mult, r=[f"ps{hb_}", f"dn{dk4}b"], w=[f"hf{c}"])
                return
            k2 = ncomb % 2
            stt(hm[k2], PS[hb_][:, 0:256], dn[:, 1:2], hfwd[:, c, :], ALU.mult, ALU.add,
                r=[f"ps{hb_}", f"dn{dk4}b", f"hf{c}"], w=[f"hm{k2}"])
            act(junk2, hm[k2], AF.Square, r=[f"hm{k2}"], w=["junk2", f"dn{dk4}c"], accum=dn[:, 2:3])
            act(dn[:, 3:4], dn[:, 2:3], AF.Ln, r=[f"dn{dk4}c"], w=[f"dn{dk4}d"], bias=EPS, scale=1.0 / 256)
            act(dn[:, 4:5], dn[:, 3:4], AF.Exp, r=[f"dn{dk4}d"], w=[f"dn{dk4}e"], scale=-0.5)
            mb = 2 + ncomb % 2
            for cc in range(KC):
                mm(PS[mb][:, 0:256], xnT[:, cc, c * 128:(c + 1) * 128], wm[:, cc, 512:768], cc == 0, cc == KC - 1,
                   r=[f"wm2_{wq}"], w=[f"ps{mb}"])
            cp("dve", sg[k2], PS[mb][:, 0:256], r=[f"ps{mb}"], w=[f"sg{k2}"])
            act(sg[k2], sg[k2], AF.Exp, r=[f"sg{k2}"], w=[f"sg{k2}"], scale=-1.0)
            act(sg[k2], sg[k2], AF.Ln, r=[f"sg{k2}"], w=[f"sg{k2}"], bias=1.0, scale=1.0)
            act(sg[k2], sg[k2], AF.Exp, r=[f"sg{k2}"], w=[f"sg{k2}"], scale=-1.0)
            stt(hn[k2], hm[k2], dn[:, 4:5], mnbc[:, h * 256:(h + 1) * 256], ALU.mult, ALU.mult,
                r=[f"hm{k2}", f"dn{dk4}e", "mnbc"], w=[f"hn{k2}"])
            tt("pool", hab[k2], hn[k2], sg[k2], ALU.mult, r=[f"hn{k2}", f"sg{k2}"], w=[f"hab{k2}"])
            tr(psb(0, 0, 128), hab[k2][:, 0:128], identb, r=[f"hab{k2}"], w=["ps0"])
            tr(psb(0, 128, 256), hab[k2][:, 128:256], identb, r=[f"hab{k2}"], w=["ps0"])
            cp("dve", hast[k2], psb(0, 0, 256), r=["ps0"], w=[f"hast{k2}"])
            dma(haT_v[:, 2 * h:2 * h + 2, c * 128:(c + 1) * 128], hast[k2].rearrange("p (a t) -> p a t", a=2),
                f"hast{k2}", r=[f"hast{k2}"], w=[f"haT{h}_{c}"])

        pre(0, 0)
        pre(1, 0)
        ncomb = 0
        for idx in range(NT):
            for d in range(2):
                if idx + 1 < NT:
                    pre(d, idx + 1)
                chunk(d, idx, ncomb)
                if idx >= NT // 2:
                    ncomb += 1
    S.barrier()
    if stop_after == "C":
        return finish(nc, S, ctx, out)

    AR.off = base_persist
    wst1 = [AR.alloc([128, KC, 256], F32) for _ in range(2)]
    wmat = [AR.alloc([128, KC, D], BF16) for _ in range(4)]
    hAB = [[AR.alloc([128, KC, 256], BF16) for _ in range(2)] for _ in range(2)]
    eg = [AR.alloc([128, 512], F32) for _ in range(2)]
    y1 = [AR.alloc([128, 512], F32) for _ in range(2)]
    yTb = [AR.alloc([128, KC, 256], BF16) for _ in range(2)]
    srcs = [p_a.rearrange("(c p) n -> p c n", p=128), p_b.rearrange("(c p) n -> p c n", p=128),
            w_in_v[:, :, OFF_GA:OFF_GA + D], w_in_v[:, :, OFF_GB:OFF_GB + D]]
    npc = 0
    for pc in range(4):
        for wi in range(4):
            k = npc % 2
            npc += 1
            dma(wst1[k], srcs[wi][:, :, pc * 256:(pc + 1) * 256], f"wst1_{k}", w=[f"wst1_{k}"])
            cp("pool", wmat[wi][:, :, pc * 256:(pc + 1) * 256], wst1[k], r=[f"wst1_{k}"], w=[f"wmat{wi}_{pc}"])
    for b in range(16):
        sl = b % 2
        t0 = b * 256
        dma(hAB[0][sl], haT_v[:, :, t0:t0 + 256], f"hA{sl}", w=[f"hA{sl}"])
        dma(hAB[1][sl], hbT_v[:, :, t0:t0 + 256], f"hB{sl}", w=[f"hB{sl}"])
        for m in range(8):
            yb_ = m % 2
            gk = 2 + m % 2
            for half, (wi, rk) in enumerate(((0, f"hA{sl}"), (1, f"hB{sl}"))):
                for c in range(KC):
                    mm(PS[yb_][:, half * 256:(half + 1) * 256], wmat[wi][:, c, m * 128:(m + 1) * 128],
                       hAB[half][sl][:, c, :], c == 0, c == KC - 1, r=[f"wmat{wi}_{m // 2}", rk], w=[f"ps{yb_}"])
            for half, wi in enumerate((2, 3)):
                for c in range(KC):
                    mm(PS[gk][:, half * 256:(half + 1) * 256], wmat[wi][:, c, m * 128:(m + 1) * 128],
                       xnT[:, c, t0:t0 + 256], c == 0, c == KC - 1, r=[f"wmat{wi}_{m // 2}"], w=[f"ps{gk}"])
            act(eg[m % 2], PS[gk][:, :], AF.Exp, r=[f"ps{gk}"], w=[f"eg{m % 2}"], scale=-1.0)
            act(eg[m % 2], eg[m % 2], AF.Ln, r=[f"eg{m % 2}"], w=[f"eg{m % 2}"], bias=1.0, scale=1.0)
            act(eg[m % 2], eg[m % 2], AF.Exp, r=[f"eg{m % 2}"], w=[f"eg{m % 2}"], scale=-1.0)
            tt("dve", y1[m % 2], PS[yb_][:, :], eg[m % 2], ALU.mult, r=[f"ps{yb_}", f"eg{m % 2}"], w=[f"y1{m % 2}"])
            tt("pool", yTb[sl][:, m, :], y1[m % 2][:, 0:256], y1[m % 2][:, 256:512], ALU.add,
               r=[f"y1{m % 2}"], w=[f"yTb{sl}_{m}"])
        dma(yT_v[:, :, t0:t0 + 256], yTb[sl], f"yTb{sl}", r=[f"yTb{sl}_{m}" for m in range(8)], w=[f"yTs{b}"])
    S.barrier()
    if stop_after == "D1":
        return finish(nc, S, ctx, out)

    AR.off = base_consts
    wup = AR.alloc([128, KC, 2 * DFF], BF16)
    wdn = AR.alloc([128, NJ, D], BF16)
    base_e = AR.off
    wst2 = [AR.alloc([128, KC, 256], F32) for _ in range(2)]
    wo = AR.alloc([128, KC, D], BF16)
    yTt = [AR.alloc([128, KC, 128], BF16) for _ in range(2)]
    xt2 = [AR.alloc([128, D], F32) for _ in range(2)]
    x1t = [AR.alloc([128, D], F32) for _ in range(2)]
    g2bc = AR.alloc([128, D], F32)
    xn2b = [AR.alloc([128, D], BF16) for _ in range(2)]
    xn2st = [AR.alloc([128, D], BF16) for _ in range(2)]
    junk3 = AR.alloc([128, D], BF16)
    st2 = [AR.alloc([128, 4], F32) for _ in range(2)]
    end_d2 = AR.off
    wo_src = w_o.rearrange("(c p) n -> p c n", p=128)
    wup_src = w_up.rearrange("(c p) n -> p c n", p=128)
    wdn_src = w_down.rearrange("(j p) n -> p j n", p=128)
    npc = 0
    for pc in range(4):
        k = npc % 2
        npc += 1
        dma(wst2[k], wo_src[:, :, pc * 256:(pc + 1) * 256], f"wst2_{k}", w=[f"wst2_{k}"])
        cp("pool", wo[:, :, pc * 256:(pc + 1) * 256], wst2[k], r=[f"wst2_{k}"], w=["wo"])
    dma(g2bc, norm2.partition_broadcast(128), "g2bc", w=["g2bc"])
    ffn_pieces = [("up", pc) for pc in range(22)] + [("dn", pc) for pc in range(11)]

    def load_ffn_piece(kind, pc):
        nonlocal npc
        k = npc % 2
        npc += 1
        if kind == "up":
            dma(wst2[k], wup_src[:, :, pc * 256:(pc + 1) * 256], f"wst2_{k}", w=[f"wst2_{k}"])
            cp("pool", wup[:, :, pc * 256:(pc + 1) * 256], wst2[k], r=[f"wst2_{k}"], w=["wup"])
        else:
            dma(wst2[k].rearrange("p (a b) n -> p a (b n)", a=2), wdn_src[:, 2 * pc:2 * pc + 2, :], f"wst2_{k}",
                w=[f"wst2_{k}"])
            cp("pool", wdn[:, 2 * pc:2 * pc + 2, :], wst2[k].rearrange("p (a b) n -> p a (b n)", a=2),
               r=[f"wst2_{k}"], w=["wdn"])

    for i in range(NT):
        b = i % 2
        dma(yTt[b], yT_v[:, :, i * 128:(i + 1) * 128], f"yTt{b}", w=[f"yTt{b}"])
        dma(xt2[b], x[i * 128:(i + 1) * 128, :], f"xt2{b}", w=[f"xt2{b}"])
        if i < len(ffn_pieces):
            load_ffn_piece(*ffn_pieces[i])
        for n in range(2):
            bk = 2 * b + n
            for m in range(KC):
                mm(PS[bk][:, :], yTt[b][:, m, :], wo[:, m, n * 512:(n + 1) * 512], m == 0, m == KC - 1,
                   r=[f"yTt{b}", "wo"], w=[f"ps{bk}"])
            tt("dve", x1t[b][:, n * 512:(n + 1) * 512], PS[bk][:, :], xt2[b][:, n * 512:(n + 1) * 512], ALU.add,
               r=[f"ps{bk}", f"xt2{b}"], w=[f"x1t{b}_{n}"])
        dma(x1s[i * 128:(i + 1) * 128, :], x1t[b], f"x1t{b}", r=[f"x1t{b}_0", f"x1t{b}_1"], w=[f"x1s{i}"])
        act(junk3, x1t[b], AF.Square, r=[f"x1t{b}_0", f"x1t{b}_1"], w=["junk3", f"st2a{b}"], accum=st2[b][:, 0:1])
        act(st2[b][:, 1:2], st2[b][:, 0:1], AF.Ln, r=[f"st2a{b}"], w=[f"st2b{b}"], bias=EPS, scale=1.0 / D)
        act(st2[b][:, 2:3], st2[b][:, 1:2], AF.Exp, r=[f"st2b{b}"], w=[f"st2c{b}"], scale=-0.5)
        stt(xn2b[b], x1t[b], st2[b][:, 2:3], g2bc, ALU.mult, ALU.mult,
            r=[f"x1t{b}_0", f"x1t{b}_1", f"st2c{b}", "g2bc"], w=[f"xn2b{b}"])
        for c in range(KC):
            tr(psb(4 + b, c * 128, (c + 1) * 128), xn2b[b][:, c * 128:(c + 1) * 128], identb,
               r=[f"xn2b{b}"], w=[f"ps{4 + b}"])
        cp("dve", xn2st[b], psb(4 + b, 0, 1024), r=[f"ps{4 + b}"], w=[f"xn2st{b}"])
        dma(xn2T_v[:, :, i * 128:(i + 1) * 128], xn2st[b].rearrange("p (c t) -> p c t", c=KC), f"xn2st{b}",
            r=[f"xn2st{b}"], w=[f"xn2T{i}"])
    for kind, pc in ffn_pieces[NT:]:
        load_ffn_piece(kind, pc)
    S.barrier()
    if stop_after == "D2":
        return finish(nc, S, ctx, out)

    AR.off = base_e
    cw = AR.alloc([128, 44, 3], F32)
    cb = AR.alloc([128, 44], F32)
    xw = [AR.alloc([128, KC, 258], BF16) for _ in range(2)]
    actT = [AR.alloc([128, NJ, 256], BF16) for _ in range(2)]
    NB_E = 4
    ca = [AR.alloc([128, 256], F32) for _ in range(NB_E)]
    cg = [AR.alloc([128, 256], F32) for _ in range(NB_E)]
    x2 = [AR.alloc([128, 256], F32) for _ in range(NB_E)]
    zz = [AR.alloc([128, 256], F32) for _ in range(NB_E)]
    ez = [AR.alloc([128, 256], F32) for _ in range(NB_E)]
    x1e = [AR.alloc([128, D], F32) for _ in range(2)]
    ucp = [AR.alloc([128, 260], F32) for _ in range(2 * NB_E)]
    dma(cw, conv_wp.rearrange("p (j k) -> p j k", k=3), "cw", w=["cw"])
    dma(cb, conv_bp, "cb", w=["cb"])
    pend_down = []
    for b in range(16):
        sl = b % 2
        t0 = b * 256
        if b == 0:
            mset("pool", xw[sl][:, :, 0:2], 0.0, w=[f"xw{sl}", f"xw{sl}h"])
            dma(xw[sl][:, :, 1:258], xn2T_v[:, :, 0:257], f"xw{sl}", w=[f"xw{sl}"])
        elif b == 15:
            mset("pool", xw[sl][:, :, 256:258], 0.0, w=[f"xw{sl}", f"xw{sl}h"])
            dma(xw[sl][:, :, 0:257], xn2T_v[:, :, t0 - 1:T], f"xw{sl}", w=[f"xw{sl}"])
        else:
            dma(xw[sl], xn2T_v[:, :, t0 - 1:t0 + 257], f"xw{sl}", w=[f"xw{sl}", f"xw{sl}h"])
        for j in range(NJ):
            if j == 8 and pend_down:
                pend_down.pop(0)()
            s = j % NB_E
            for bank, ch, dst, dk_ in ((2 * (j % 2), j, ca[s], f"ca{s}"), (2 * (j % 2) + 1, NJ + j, cg[s], f"cg{s}")):
                for c in range(KC):
                    mm(PS[bank][:, 0:258], wup[:, c, ch * 128:(ch + 1) * 128], xw[sl][:, c, :], c == 0, c == KC - 1,
                       r=[f"xw{sl}", f"xw{sl}h"], w=[f"ps{bank}"])
                ui = 2 * s + (bank % 2)
                uc = ucp[ui][:, 0:258]
                act(uc, PS[bank][:, 0:258], AF.Copy, r=[f"ps{bank}"], w=[f"uc{ui}"])
                act(dst, uc[:, 1:257], AF.Identity, r=[f"uc{ui}", "cw", "cb"], w=[dk_],
                    scale=cw[:, ch, 1:2], bias=cb[:, ch:ch + 1])
                stt(dst, uc[:, 0:256], cw[:, ch, 0:1], dst, ALU.mult, ALU.add, r=[f"uc{ui}", dk_], w=[dk_])
                stt(dst, uc[:, 2:258], cw[:, ch, 2:3], dst, ALU.mult, ALU.add, r=[f"uc{ui}", dk_], w=[dk_])
            act(ez[s], cg[s], AF.Gelu_apprx_tanh, r=[f"cg{s}"], w=[f"ez{s}"])
            tt("dve", actT[sl][:, j, :], ez[s], ca[s], ALU.mult, r=[f"ez{s}", f"ca{s}"], w=[f"actT{sl}"])
        def down_proj(b=b, sl=sl):
            for t2 in range(2):
                tile_i = b * 2 + t2
                dma(x1e[t2], x1s[tile_i * 128:(tile_i + 1) * 128, :], f"x1e{t2}", w=[f"x1e{t2}"])
                for n in range(2):
                    bk = 4 + 2 * t2 + n
                    for j in range(NJ):
                        mm(PS[bk][:, :], actT[sl][:, j, t2 * 128:(t2 + 1) * 128], wdn[:, j, n * 512:(n + 1) * 512],
                           j == 0, j == NJ - 1, r=[f"actT{sl}"], w=[f"ps{bk}"])
                    tt("dve", x1e[t2][:, n * 512:(n + 1) * 512], PS[bk][:, :], x1e[t2][:, n * 512:(n + 1) * 512],
                       ALU.add, r=[f"ps{bk}", f"x1e{t2}"], w=[f"x1e{t2}"])
                dma(out[tile_i * 128:(tile_i + 1) * 128, :], x1e[t2], f"x1e{t2}", r=[f"x1e{t2}"], w=[f"out{tile_i}"])

        pend_down.append(down_proj)
    for f_ in pend_down:
        f_()
    return finish(nc, S, ctx, out)


def finish(nc, S, ctx, out, debug_dump=None):
    if debug_dump is not None:
        S.barrier()
        src, dst = debug_dump
        S.add("sp", lambda e: e.dma_start(out=dst, in_=src), (), (), dk="dbg", nbytes=1 << 23)
    S.barrier()
    for e_ in ENGS:
        S.add(e_, None, (), ())
    S.schedule()
    esem = {e: ctx.enter_context(nc.semaphore(f"se_{e}")) for e in ENGS if e != "sp"}
    dsem = {k: ctx.enter_context(nc.semaphore(f"sd_{k}")) for k in S.dma_n}
    with nc.Block() as block:
        @block.sync
        def _(e):
            S.emit("sp", e, esem, dsem)

        @block.scalar
        def _(e):
            S.emit("act", e, esem, dsem)

        @block.vector
        def _(e):
            S.emit("dve", e, esem, dsem)

        @block.gpsimd
        def _(e):
            S.emit("pool", e, esem, dsem)

        @block.tensor
        def _(e):
            S.emit("pe", e, esem, dsem)
    ctx.close()
    nc._est_ns = S.est_ns
    return nc


def host_consts():
    ident = np.eye(128, dtype=np.float32)
    ones = np.ones((128, 128), np.float32)
    s = np.arange(128)
    ufwd = (s[:, None] <= s[None, :]).astype(np.float32)
    ubwd = (s[:, None] >= s[None, :]).astype(np.float32)
    cst = np.concatenate([ident, ones, ufwd, ubwd], axis=1)
    pos = np.arange(T, dtype=np.float32)
    inv = (10000.0 ** (-np.arange(0, 64, 2, dtype=np.float32) / 64)).astype(np.float32)
    ang = pos[:, None] * inv[None, :]
    cos = np.cos(ang).astype(np.float32)
    sin = np.sin(ang).astype(np.float32)
    cos2 = np.concatenate([cos, cos], axis=1).reshape(32, 128, 64).transpose(1, 0, 2)
    sin2 = np.concatenate([-sin, sin], axis=1).reshape(32, 128, 64).transpose(1, 0, 2)
    rope = np.stack([cos2, sin2], axis=1).reshape(128, -1)
    return np.ascontiguousarray(cst), np.ascontiguousarray(rope.astype(np.float32))


def make_in_maps(inp, cores):
    cst, rope = host_consts()
    f = lambda a: np.ascontiguousarray(np.asarray(a, dtype=np.float32))
    shared = {
        "w_in": f(inp["w_in"][0]), "p_a": f(inp["p_a"][0]), "p_b": f(inp["p_b"][0]), "w_o": f(inp["w_o"][0]),
        "w_up": f(inp["w_up"][0]), "w_down": f(inp["w_down"][0]),
        "norm1": f(inp["norm1"]), "norm2": f(inp["norm2"]), "m_out_norm": f(inp["m_out_norm"]),
        "gate_bias": f(inp["gate_bias"]), "q_norm": f(inp["q_norm"]), "k_norm": f(inp["k_norm"]),
        "lam4": f(np.concatenate([inp["lam_q1"], inp["lam_k1"], inp["lam_q2"], inp["lam_k2"]], axis=1)),
        "a_out_norm": f(inp["a_out_norm"]),
        "conv_wp": f(np.asarray(inp["conv_w"])[0, :, 0, :].reshape(3, 44, 128).transpose(2, 1, 0).reshape(128, 132)),
        "conv_bp": f(np.asarray(inp["conv_b"])[0].reshape(44, 128).T),
        "cst": cst, "rope": rope,
    }
    maps = []
    for b in cores:
        m = dict(shared)
        m["x"] = f(inp["x"][b])
        maps.append(m)
    return maps


_NC = None


def kernel(**inputs):
    global _NC
    if _NC is None:
        _NC = build()
    maps = make_in_maps(inputs, list(range(8)))
    res = run_bass_kernel_spmd(_NC, maps, core_ids=list(range(8)))
    return np.stack([np.asarray(r["out"]) for r in res.results], axis=0).astype(np.float32)
```

```python
import math
from contextlib import ExitStack

import numpy as np
import concourse.bass as bass
import concourse.mybir as mybir
from concourse.bass_utils import run_bass_kernel_spmd

F32 = mybir.dt.float32
BF16 = mybir.dt.bfloat16
AF = mybir.ActivationFunctionType
ALU = mybir.AluOpType
AX = mybir.AxisListType

T = 4096
D = 1024
NT = 32
KC = 8
DIN = 8208
OFF_MQ, OFF_MK, OFF_MV, OFF_MO, OFF_MG = 0, 512, 1024, 2048, 3072
OFF_AQ, OFF_AK, OFF_AV, OFF_GA, OFF_GB = 3088, 4112, 5136, 6160, 7184
DFF = 2816
NJ = 22
EPS = 1e-6
LAM_INIT = 0.8 - 0.6 * math.exp(-0.3 * 0)
ARENA_BYTES = 206 * 1024

ENGS = ("pe", "act", "dve", "pool", "sp")


class Op:
    __slots__ = ("eng", "fn", "pidx", "pos", "sig", "dk", "ordn", "waits", "cnt", "cost", "deps", "users",
                 "nd", "rt", "fin", "group", "lastpos", "dmacnt", "nbytes")

    def __init__(self, eng, fn, dk, cost):
        self.eng = eng
        self.fn = fn
        self.dk = dk
        self.cost = cost
        self.sig = False
        self.ordn = 0
        self.waits = []
        self.cnt = 0
        self.deps = []
        self.users = []
        self.nd = 0
        self.rt = 0.0
        self.fin = 0.0
        self.pos = -1
        self.group = None
        self.nbytes = 0


USE_DRAIN = False
MAX_DMA_INFLIGHT = 6
NDUM = 0
PSUM_EXCL = True
SEM_LAT = 300.0
DMA_LAT = 2000.0
DMA_BW = 160.0


class Sched:
    def __init__(self):
        self.all = []
        self.lastw = {}
        self.readers = {}
        self.dma_n = {}
        self.dma_last = {}
        self.cur_bar = None
        self.group = []
        self.order = None
        self.last_psr = {}
        self.dma_hist = []

    def add(self, eng, fn, r=(), w=(), dk=None, cost=100.0, nbytes=0):
        op = Op(eng, fn, dk, cost)
        op.pidx = len(self.all)
        op.nbytes = nbytes
        deps = {}
        psr = False

        def dep(d, raw):
            if d is op:
                return
            if id(d) in deps:
                if raw:
                    deps[id(d)] = (d, True)
            else:
                deps[id(d)] = (d, raw)

        for k in r:
            d = self.lastw.get(k)
            if d is not None:
                dep(d, True)
            if PSUM_EXCL and k[:2] == "ps" and eng in ("act", "dve"):
                psr = True
        if psr:
            other = "dve" if eng == "act" else "act"
            d = self.last_psr.get(other)
            if d is not None:
                dep(d, True)
            self.last_psr[eng] = op
        for k in w:
            d = self.lastw.get(k)
            if d is not None:
                dep(d, False)
            for d2 in self.readers.get(k, ()):
                dep(d2, False)
        if dk is not None:
            d = self.dma_last.get(dk)
            if d is not None:
                dep(d, True)
            self.dma_hist.append(op)
            if len(self.dma_hist) > MAX_DMA_INFLIGHT:
                dep(self.dma_hist[-1 - MAX_DMA_INFLIGHT], True)
            self.dma_n[dk] = self.dma_n.get(dk, 0) + 1
            op.ordn = self.dma_n[dk]
            self.dma_last[dk] = op
        if self.cur_bar is not None:
            dep(self.cur_bar, True)
        op.deps = list(deps.values())
        for d, _ in op.deps:
            d.users.append(op)
        self.all.append(op)
        self.group.append(op)
        for k in r:
            self.readers.setdefault(k, []).append(op)
        for k in w:
            self.lastw[k] = op
            self.readers[k] = []
        return op

    def barrier(self):
        b = Op("bar", None, None, 0.0)
        b.pidx = len(self.all)
        b.group = self.group
        b.deps = [(o, True) for o in self.group]
        if self.cur_bar is not None:
            b.deps.append((self.cur_bar, True))
        for d, _ in b.deps:
            d.users.append(b)
        self.all.append(b)
        self.group = []
        self.cur_bar = b
        self.lastw = {}
        self.readers = {}

    def schedule(self):
        import heapq
        engs = ENGS + ("bar",)
        order = {e: [] for e in engs}
        readyq = {e: [] for e in engs}
        busy = {e: False for e in engs}
        ev = []
        seq = 0
        dma_free = 0.0
        for o in self.all:
            o.nd = len(o.deps)
            o.rt = 0.0
            if o.nd == 0:
                heapq.heappush(ev, (0.0, seq, 0, o))
                seq += 1

        def start(e, t):
            nonlocal seq, dma_free
            o = heapq.heappop(readyq[e])[1]
            busy[e] = True
            o.rt = t
            o.pos = len(order[e])
            order[e].append(o)
            if o.dk is not None:
                tend = t + 60.0
                dma_free = max(dma_free, t) + o.nbytes / DMA_BW
                o.fin = dma_free + DMA_LAT
            else:
                tend = t + o.cost
                o.fin = tend
            heapq.heappush(ev, (tend, seq, 1, e))
            seq += 1
            for u in o.users:
                u.nd -= 1
                lat = 0.0 if (u.eng == o.eng and o.dk is None) else SEM_LAT
                if o.fin + lat > u.rt:
                    u.rt = o.fin + lat
                if u.nd == 0:
                    heapq.heappush(ev, (u.rt, seq, 0, u))
                    seq += 1

        while ev:
            t, _, kind, x = heapq.heappop(ev)
            if kind == 0:
                e = x.eng
                heapq.heappush(readyq[e], (x.pidx, x))
                if not busy[e]:
                    start(e, t)
            else:
                busy[x] = False
                if readyq[x]:
                    start(x, t)
        assert sum(len(v) for v in order.values()) == len(self.all), "scheduler: dependency cycle"
        self.order = order
        self.est_ns = max(o.fin for o in self.all)
        last = {}
        dcnt = {}
        for b in order["bar"]:
            for o in b.group:
                if o.dk is not None:
                    dcnt[o.dk] = max(dcnt.get(o.dk, 0), o.ordn)
                elif o.fn is not None:
                    p = last.get(o.eng)
                    if p is None or o.pos > p.pos:
                        last[o.eng] = o
            b.lastpos = dict(last)
            b.dmacnt = dict(dcnt)
        import bisect
        kn = {e: {} for e in ENGS}
        hist = {e: {} for e in ENGS}
        last_drain = {e: -1 for e in ENGS}

        def learn(e, p, key, val):
            if kn[e].get(key, 0) < val:
                kn[e][key] = val
                h = hist[e].setdefault(key, ([], []))
                h[0].append(p)
                h[1].append(val)

        def want(op, key, val, dop):
            e = op.eng
            if kn[e].get(key, 0) >= val:
                return
            if dop is not None:
                dop.sig = True
                op.waits.append(dop)
            else:
                op.waits.append((key[1], val))
            learn(e, op.pos, key, val)
            if dop is not None:
                F = dop.eng
                for k2, h in hist[F].items():
                    i = bisect.bisect_right(h[0], dop.pos) - 1
                    if i >= 0:
                        learn(e, op.pos, k2, h[1][i])

        seq_ops = sorted((o for o in self.all if o.eng != "bar"), key=lambda o: (o.rt, o.pidx))
        for op in seq_ops:
            e = op.eng
            for d, raw in op.deps:
                if d.eng == "bar":
                    for F, lo in d.lastpos.items():
                        if F != e:
                            want(op, ("eng", F), lo.pos + 1, lo)
                    for dk_, n_ in d.dmacnt.items():
                        want(op, ("dma", dk_), n_, None)
                elif d.dk is not None:
                    want(op, ("dma", d.dk), d.ordn, None)
                elif d.eng == e and op.dk is None:
                    if e != "pe" and raw:
                        if USE_DRAIN:
                            if d.pos > last_drain[e]:
                                op.waits.append("DRAIN")
                                last_drain[e] = op.pos
                        else:
                            want(op, ("eng", e), d.pos + 1, d)
                else:
                    want(op, ("eng", d.eng), d.pos + 1, d)
        for e in ENGS:
            c = 0
            for o in order[e]:
                if o.dk is None and o.sig:
                    c += 1
                o.cnt = c

    def emit(self, eng, e, esem, dsem):
        for o in self.order[eng]:
            for wt in o.waits:
                if wt == "DRAIN":
                    e.drain()
                elif isinstance(wt, tuple):
                    e.wait_ge(dsem[wt[0]], 16 * wt[1])
                else:
                    e.wait_ge(esem[wt.eng], wt.cnt)
            if o.fn is None:
                continue
            ins = o.fn(e)
            if o.dk is not None:
                ins.then_inc(dsem[o.dk], 16)
            elif o.sig:
                ins.then_inc(esem[eng], 1)


class Arena:
    def __init__(self, t, nbytes, base=0):
        self.t = t
        self.nbytes = nbytes
        self.off = base

    def alloc(self, shape, dt):
        n = 1
        for s in shape[1:]:
            n *= s
        nb = n * (4 if dt == F32 else 2)
        nb = (nb + 31) // 32 * 32
        o = self.off
        assert o + nb <= self.nbytes, f"arena overflow {o}+{nb}>{self.nbytes}"
        self.off = o + nb
        a = self.t[:, o // 4:(o + nb) // 4]
        if dt != F32:
            a = a.bitcast(dt)
        a = a[:, 0:n]
        if len(shape) == 3:
            a = a.rearrange("p (a b) -> p a b", a=shape[1])
        elif len(shape) == 4:
            a = a.rearrange("p (a b c) -> p a b c", a=shape[1], b=shape[2])
        return a


def build(debug=False, stop_after=None, skip=(), cstage=99, var=0):
    nc = bass.Bass("TRN2", target_bir_lowering=False)

    def din(name, shape, dt=F32):
        return nc.dram_tensor(name, list(shape), dt, kind="ExternalInput").ap()

    x = din("x", [T, D])
    w_in = din("w_in", [D, DIN])
    p_a = din("p_a", [D, D])
    p_b = din("p_b", [D, D])
    w_o = din("w_o", [D, D])
    w_up = din("w_up", [D, 2 * DFF])
    w_down = din("w_down", [DFF, D])
    norm1 = din("norm1", [1, D])
    norm2 = din("norm2", [1, D])
    m_out_norm = din("m_out_norm", [1, D])
    gate_bias = din("gate_bias", [1, 16])
    q_norm = din("q_norm", [1, 64])
    k_norm = din("k_norm", [1, 64])
    lam4 = din("lam4", [1, 256])
    a_out_norm = din("a_out_norm", [1, 128])
    conv_wp = din("conv_wp", [128, 44 * 3])
    conv_bp = din("conv_bp", [128, 44])
    cst = din("cst", [128, 4 * 128])
    rope = din("rope", [128, 2 * 32 * 64])
    skind = "ExternalOutput" if debug else "Internal"
    haT = nc.dram_tensor("haT", [D, T], BF16, kind=skind).ap()
    hbT = nc.dram_tensor("hbT", [D, T], BF16, kind=skind).ap()
    yTs = nc.dram_tensor("yTs", [D, T], BF16, kind=skind).ap()
    x1s = nc.dram_tensor("x1s", [T, D], F32, kind=skind).ap()
    xn2T = nc.dram_tensor("xn2T", [D, T], BF16, kind=skind).ap()
    out = nc.dram_tensor("out", [T, D], F32, kind="ExternalOutput").ap()

    w_in_v = w_in.rearrange("(c p) n -> p c n", p=128)
    haT_v = haT.rearrange("(f p) t -> p f t", p=128)
    hbT_v = hbT.rearrange("(f p) t -> p f t", p=128)
    yT_v = yTs.rearrange("(f p) t -> p f t", p=128)
    xn2T_v = xn2T.rearrange("(f p) t -> p f t", p=128)

    S = Sched()
    ctx = ExitStack()
    arena_t = ctx.enter_context(nc.sbuf_tensor("arena", [128, ARENA_BYTES // 4], F32))
    psall = ctx.enter_context(nc.psum_tensor("psall", [128, 4096], F32))
    PS = [psall[:, b * 512:(b + 1) * 512] for b in range(8)]
    AR = Arena(arena_t, ARENA_BYTES)

    def fsz(ap):
        n = 1
        for d_ in ap.shape[1:]:
            n *= d_
        return n

    def mm(o, lhsT, rhs, start, stop, r=(), w=(), sgc=False, cost=None):
        n = fsz(rhs)
        c = max(64, n) / 3.2 + 15.0
        if lhsT.dtype == F32:
            c *= 4
        if cost is not None:
            c = cost
        if sgc:
            return S.add("pe", lambda e: e.matmul(o, lhsT=lhsT, rhs=rhs, start=start, stop=stop, skip_group_check=True),
                         r, w, cost=c)
        return S.add("pe", lambda e: e.matmul(o, lhsT=lhsT, rhs=rhs, start=start, stop=stop), r, w, cost=c)

    def tr(o, in_, ident, r=(), w=()):
        return S.add("pe", lambda e: e.transpose(o, in_, ident), r, w, cost=110.0)

    def act(o, in_, func, r=(), w=(), bias=None, scale=None, accum=None):
        kw = {}
        c = fsz(in_) / 1.6 + 80.0
        if bias is not None:
            kw["bias"] = bias
            if not isinstance(bias, float):
                c += 90.0
        if scale is not None:
            kw["scale"] = scale
            if not isinstance(scale, float):
                c += 90.0
        if accum is not None:
            kw["accum_out"] = accum
            c += 90.0
        return S.add("act", lambda e: e.activation(out=o, in_=in_, func=func, **kw), r, w, cost=c)

    def ecost(eng, n):
        return (2.2 * n + 120.0) if eng == "pool" else (n / 0.96 + 70.0)

    def ts(eng, o, in0, s1, op0, r=(), w=(), s2=None, op1=None):
        c = ecost(eng, fsz(o))
        if op1 is None:
            return S.add(eng, lambda e: e.tensor_scalar(out=o, in0=in0, scalar1=s1, scalar2=None, op0=op0), r, w, cost=c)
        return S.add(eng, lambda e: e.tensor_scalar(out=o, in0=in0, scalar1=s1, scalar2=s2, op0=op0, op1=op1), r, w, cost=c)

    def tt(eng, o, in0, in1, op, r=(), w=()):
        return S.add(eng, lambda e: e.tensor_tensor(out=o, in0=in0, in1=in1, op=op), r, w, cost=ecost(eng, fsz(o)))

    def stt(o, in0, sc, in1, op0, op1, r=(), w=()):
        return S.add("dve", lambda e: e.scalar_tensor_tensor(out=o, in0=in0, scalar=sc, in1=in1, op0=op0, op1=op1), r, w,
                     cost=ecost("dve", fsz(o)))

    def red(o, in_, op, r=(), w=(), negate=None):
        c = ecost("dve", fsz(in_))
        if negate:
            return S.add("dve", lambda e: e.tensor_reduce(out=o, in_=in_, axis=AX.X, op=op, negate=True), r, w, cost=c)
        return S.add("dve", lambda e: e.tensor_reduce(out=o, in_=in_, axis=AX.X, op=op), r, w, cost=c)

    def recip(o, in_, r=(), w=()):
        return S.add("dve", lambda e: e.reciprocal(out=o, in_=in_), r, w, cost=2.0 * fsz(o) + 70.0)

    def cp(eng, o, in_, r=(), w=()):
        if eng == "act":
            return S.add(eng, lambda e: e.activation(out=o, in_=in_, func=AF.Copy), r, w, cost=fsz(o) / 1.15 + 110.0)
        return S.add(eng, lambda e: e.tensor_copy(out=o, in_=in_), r, w, cost=ecost(eng, fsz(o)))

    def mset(eng, o, val, r=(), w=()):
        return S.add(eng, lambda e: e.memset(o, val), r, w, cost=ecost(eng, fsz(o)))

    def dma(o, in_, dk, r=(), w=()):
        nb = fsz(o) * o.shape[0] * (4 if o.dtype == F32 else 2)
        return S.add("sp", lambda e: e.dma_start(out=o, in_=in_), r, w, dk=dk, nbytes=nb)

    def psb(b, n0, n1):
        return PS[b][:, :].bitcast(BF16)[:, n0:n1]

    cstt = AR.alloc([128, 4, 128], F32)
    identf = cstt[:, 0, :]
    onesf = cstt[:, 1, :]
    U = [cstt[:, 2, :], cstt[:, 3, :]]
    identb = AR.alloc([128, 128], BF16)
    smallv = AR.alloc([128, 64], F32)
    base_consts = AR.off
    xnT = AR.alloc([128, KC, T], BF16)
    base_persist = AR.off

    dma(cstt, cst.rearrange("p (a b) -> p a b", a=4), "cst", w=["cst"])
    cp("dve", identb, identf, r=["cst"], w=["identb"])

    xt = [AR.alloc([128, D], F32) for _ in range(2)]
    g1bc = AR.alloc([128, D], F32)
    junk = AR.alloc([128, D], BF16)
    xs = [AR.alloc([128, D], BF16) for _ in range(2)]
    ssq = AR.alloc([128, 3 * NT], F32)
    dma(g1bc, norm1.partition_broadcast(128), "g1bc", w=["g1bc"])
    for i in range(NT):
        b = i % 2
        dma(xt[b], x[i * 128:(i + 1) * 128, :], f"xt{b}", w=[f"xt{b}"])
        act(junk, xt[b], AF.Square, r=[f"xt{b}"], w=["junk", f"ss{i}"], accum=ssq[:, i:i + 1])
        act(ssq[:, NT + i:NT + i + 1], ssq[:, i:i + 1], AF.Ln, r=[f"ss{i}"], w=[f"ln{i}"], bias=EPS, scale=1.0 / D)
        act(ssq[:, 2 * NT + i:2 * NT + i + 1], ssq[:, NT + i:NT + i + 1], AF.Exp, r=[f"ln{i}"], w=[f"rs{i}"], scale=-0.5)
        stt(xs[b], xt[b], ssq[:, 2 * NT + i:2 * NT + i + 1], g1bc, ALU.mult, ALU.mult,
            r=[f"xt{b}", f"rs{i}", "g1bc"], w=[f"xs{b}"])
        for c in range(KC):
            tr(psb(b, c * 128, (c + 1) * 128), xs[b][:, c * 128:(c + 1) * 128], identb,
               r=[f"xs{b}", "identb"], w=[f"ps{b}"])
        cp("dve", xnT[:, :, i * 128:(i + 1) * 128],
           psb(b, 0, 1024).rearrange("p (c t) -> p c t", c=KC), r=[f"ps{b}"], w=[])
    S.barrier()
    if stop_after == "A":
        return finish(nc, S, ctx, out, debug_dump=(xnT, haT_v))

    AR.off = base_persist
    cs2 = AR.alloc([128, 2, 32, 64], F32)
    wst = AR.alloc([128, KC, 384], F32)
    wbf = [AR.alloc([128, KC, 384], BF16) for _ in range(2)]
    qkT = [AR.alloc([128, 2, T], BF16) for _ in range(2)]
    vaug = [AR.alloc([128, NT, 130], BF16) for _ in range(2)]
    pjs = [AR.alloc([128, 384], F32) for _ in range(3)]
    sqb = [AR.alloc([128, 256], F32) for _ in range(3)]
    xnq = [AR.alloc([128, 256], F32) for _ in range(3)]
    rA = [AR.alloc([128, 256], F32) for _ in range(3)]
    rB = [AR.alloc([128, 256], F32) for _ in range(3)]
    qkb = [AR.alloc([128, 256], BF16) for _ in range(3)]
    st4 = [AR.alloc([128, 12], F32) for _ in range(3)]
    pT = [AR.alloc([128, 1024], BF16) for _ in range(3)]
    gqk = AR.alloc([128, 256], F32)
    aon = AR.alloc([128, 128], F32)
    lamt = AR.alloc([128, 256], F32)
    lamp = AR.alloc([128, 128], F32)
    accs = [AR.alloc([128, 9, 129], F32) for _ in range(2)]
    od4 = [AR.alloc([128, 4, 128], F32) for _ in range(2)]
    t24 = AR.alloc([128, 4, 128], F32)
    sq4 = AR.alloc([128, 4, 128], F32)
    ob4 = [AR.alloc([128, 4, 128], BF16) for _ in range(2)]
    e8 = [AR.alloc([128, 24], F32) for _ in range(2)]
    hbst = [AR.alloc([128, 512], BF16) for _ in range(2)]

    dma(cs2, rope.rearrange("p (a i d) -> p a i d", a=2, i=32), "cs2", w=["cs2"])
    for g in range(4):
        dma(gqk[:, g * 64:(g + 1) * 64], (q_norm if g < 2 else k_norm).partition_broadcast(128), "gqk", w=[f"gqk{g}"])
    ts("dve", gqk[:, 0:128], gqk[:, 0:128], 0.125, ALU.mult, r=["gqk0", "gqk1"], w=["gqk0", "gqk1"])
    dma(aon, a_out_norm.partition_broadcast(128), "aon", w=["aon"])
    ts("dve", aon, aon, 1.0 - LAM_INIT, ALU.mult, r=["aon"], w=["aon"])
    dma(lamt, lam4.partition_broadcast(128), "lamt", w=["lamt"])
    tt("dve", lamp[:, 0:64], lamt[:, 0:64], lamt[:, 64:128], ALU.mult, r=["lamt"], w=["lamp"])
    tt("dve", lamp[:, 64:128], lamt[:, 128:192], lamt[:, 192:256], ALU.mult, r=["lamt"], w=["lamp"])
    red(smallv[:, 0:2], lamp.rearrange("p (a b) -> p a b", a=2), ALU.add, r=["lamp"], w=["lam0"])
    act(smallv[:, 4:6], smallv[:, 0:2], AF.Exp, r=["lam0"], w=["lam1"])
    tt("dve", smallv[:, 2:3], smallv[:, 4:5], smallv[:, 5:6], ALU.subtract, r=["lam1"], w=["lam2"])
    ts("dve", smallv[:, 3:4], smallv[:, 2:3], LAM_INIT, ALU.add, r=["lam2"], w=["neglam"], s2=-1.0, op1=ALU.mult)
    neglam = smallv[:, 3:4]
    for sl in range(2):
        mset("dve", vaug[sl][:, :, 128:130], 1.0, w=[f"vaug1_{sl}"])

    def load_head_w(h):
        sl = h % 2
        for n, off in enumerate((OFF_AQ, OFF_AK, OFF_AV)):
            dma(wst[:, :, n * 128:(n + 1) * 128], w_in_v[:, :, off + h * 128:off + (h + 1) * 128],
                "wst", w=[f"wst_{n}"])
        cp("pool", wbf[sl], wst, r=[f"wst_{n}" for n in range(3)], w=[f"wbf{sl}"])

    def b2_tile(h, i, pb=7):
        sl = h % 2
        b = i % 3
        pj = PS[pb]
        for c in range(KC):
            mm(pj[:, 0:384], xnT[:, c, i * 128:(i + 1) * 128], wbf[sl][:, c, :], c == 0, c == KC - 1,
               r=[f"wbf{sl}"], w=[f"ps{pb}"])
        cp("act", pjs[b], pj[:, 0:384], r=[f"ps{pb}"], w=[f"pjs{b}"])
        cp("pool", vaug[sl][:, i, 0:128], pjs[b][:, 256:384], r=[f"pjs{b}"], w=[f"vaug{sl}_{i}"])
        tt("pool", sqb[b], pjs[b][:, 0:256], pjs[b][:, 0:256], ALU.mult, r=[f"pjs{b}"], w=[f"sqb{b}"])
        red(st4[b][:, 0:4], sqb[b].rearrange("p (g d) -> p g d", g=4), ALU.add, r=[f"sqb{b}"], w=[f"st4a{b}"])
        act(st4[b][:, 4:8], st4[b][:, 0:4], AF.Ln, r=[f"st4a{b}"], w=[f"st4b{b}"], bias=EPS, scale=1.0 / 64)
        act(st4[b][:, 8:12], st4[b][:, 4:8], AF.Exp, r=[f"st4b{b}"], w=[f"st4c{b}"], scale=-0.5)
        for g in range(4):
            stt(xnq[b][:, g * 64:(g + 1) * 64], pjs[b][:, g * 64:(g + 1) * 64], st4[b][:, 8 + g:9 + g],
                gqk[:, g * 64:(g + 1) * 64], ALU.mult, ALU.mult,
                r=[f"pjs{b}", f"st4c{b}", f"gqk{g}"], w=[f"xnq{b}"])
        x3 = xnq[b].rearrange("p (g d) -> p g d", g=4)
        a3 = rA[b].rearrange("p (g d) -> p g d", g=4)
        b3 = rB[b].rearrange("p (g d) -> p g d", g=4)
        cosb = cs2[:, 0, i, :].unsqueeze(1).broadcast_to([128, 4, 64])
        sin_lo = cs2[:, 1, i, 0:32].unsqueeze(1).broadcast_to([128, 4, 32])
        sin_hi = cs2[:, 1, i, 32:64].unsqueeze(1).broadcast_to([128, 4, 32])
        tt("dve", a3, x3, cosb, ALU.mult, r=[f"xnq{b}", "cs2"], w=[f"rA{b}"])
        tt("pool", b3[:, :, 0:32], x3[:, :, 32:64], sin_lo, ALU.mult, r=[f"xnq{b}", "cs2"], w=[f"rB{b}"])
        tt("pool", b3[:, :, 32:64], x3[:, :, 0:32], sin_hi, ALU.mult, r=[f"xnq{b}", "cs2"], w=[f"rB{b}"])
        tt("dve", qkb[b], rA[b], rB[b], ALU.add, r=[f"rA{b}", f"rB{b}"], w=[f"qkb{b}"])

    def b2_tile_p2(h, i, pb=7):
        sl = h % 2
        b = i % 3
        tr(psb(pb, 768, 896), qkb[b][:, 0:128], identb, r=[f"qkb{b}"], w=[f"ps{pb}"])
        tr(psb(pb, 896, 1024), qkb[b][:, 128:256], identb, r=[f"qkb{b}"], w=[f"ps{pb}"])
        cp("act", qkT[sl][:, :, i * 128:(i + 1) * 128], psb(pb, 768, 1024).rearrange("p (a t) -> p a t", a=2),
           r=[f"ps{pb}"], w=[f"qkT{sl}"])

    acc3 = psall[:, 4 * 512:7 * 512].rearrange("p (b n) -> p b n", b=3)

    def b3_epilogue(h, g):
        k2 = g % 2
        a9 = accs[k2]
        a3 = a9.rearrange("p a n -> p (a n)").rearrange("p (b n) -> p b n", b=3)
        e_ = e8[k2]
        cp("act", a3, acc3[:, :, 0:387], r=["ps4", "ps5", "ps6"], w=[f"accs{k2}"])
        recip(e_[:, 0:8], a9[:, 0:8, 128], r=[f"accs{k2}"], w=[f"e8a{k2}"])
        ts("dve", e_[:, 8:12], e_[:, 4:8], neglam, ALU.mult, r=[f"e8a{k2}", "neglam"], w=[f"e8b{k2}"])
        o_ = od4[k2]
        tt("pool", o_, a9[:, 0:4, 0:128], e_[:, 0:4].unsqueeze(2).broadcast_to([128, 4, 128]), ALU.mult,
           r=[f"accs{k2}", f"e8a{k2}"], w=[f"od4{k2}"])
        tt("pool", t24, a9[:, 4:8, 0:128], e_[:, 8:12].unsqueeze(2).broadcast_to([128, 4, 128]), ALU.mult,
           r=[f"accs{k2}", f"e8b{k2}"], w=["t24"])
        tt("dve", o_, o_, t24, ALU.add, r=[f"od4{k2}", "t24"], w=[f"od4{k2}"])
        tt("pool", sq4, o_, o_, ALU.mult, r=[f"od4{k2}"], w=["sq4"])
        red(e_[:, 12:16], sq4, ALU.add, r=["sq4"], w=[f"e8c{k2}"])
        act(e_[:, 12:16], e_[:, 12:16], AF.Ln, r=[f"e8c{k2}"], w=[f"e8c{k2}"], bias=EPS, scale=1.0 / 128)
        act(e_[:, 16:20], e_[:, 12:16], AF.Exp, r=[f"e8c{k2}"], w=[f"e8d{k2}"], scale=-0.5)
        tt("dve", o_, o_, e_[:, 16:20].unsqueeze(2).broadcast_to([128, 4, 128]), ALU.mult,
           r=[f"od4{k2}", f"e8d{k2}"], w=[f"od4{k2}"])
        tt("pool", ob4[k2], o_, aon.unsqueeze(1).broadcast_to([128, 4, 128]), ALU.mult,
           r=[f"od4{k2}", "aon"], w=[f"ob4{k2}"])

    def b3_epilogue_p2(h, g):
        k2 = g % 2
        for qt in range(4):
            tr(psb(7, qt * 128, (qt + 1) * 128), ob4[k2][:, qt, :], identb, r=[f"ob4{k2}"], w=["ps7"])
        cp("act", hbst[k2], psb(7, 0, 512), r=["ps7"], w=[f"hbst{k2}"])
        dma(hbT_v[:, h, g * 512:(g + 1) * 512], hbst[k2], f"hbst{k2}", r=[f"hbst{k2}"], w=[f"hbT{h}_{g}"])

    def b3_head(h, inter):
        sl = h % 2
        steps = [(g, j) for g in range(8) for j in range(NT)]
        ns = len(steps)

        def qk(s):
            g, j = steps[s]
            pb_ = 2 * (s % 2)
            for c in range(2):
                mm(PS[pb_ + c], qkT[sl][c * 64:(c + 1) * 64, 1, j * 128:(j + 1) * 128],
                   qkT[sl][c * 64:(c + 1) * 64, 0, g * 512:(g + 1) * 512], True, True,
                   r=[f"qkT{sl}"], w=[f"ps{pb_ + c}"], cost=170.0)
            act(pT[s % 3], psall[:, pb_ * 512:(pb_ + 2) * 512], AF.Exp, r=[f"ps{pb_}", f"ps{pb_ + 1}"],
                w=[f"pT{s % 3}"])

        pending = []
        qk(0)
        for s in range(ns):
            if s + 1 < ns:
                qk(s + 1)
            g, j = steps[s]
            for a in range(8):
                c, qt = divmod(a, 4)
                bank = 4 + a // 3
                col = (a % 3) * 129
                mm(PS[bank][:, col:col + 129], pT[s % 3][:, c * 512 + qt * 128:c * 512 + (qt + 1) * 128],
                   vaug[sl][:, j, 0:129], j == 0 and a % 3 == 0, j == NT - 1,
                   r=[f"pT{s % 3}", f"vaug{sl}_{j}", f"vaug1_{sl}"], w=[f"ps{bank}"], sgc=True)
            for _dm in range(NDUM):
                mm(PS[6][:, 258:387], pT[s % 3][:, 896:1024], vaug[sl][:, j, 0:129], False, j == NT - 1,
                   r=[f"pT{s % 3}", f"vaug{sl}_{j}", f"vaug1_{sl}"], w=["ps6"], sgc=True)
            for it in list(pending):
                if it[0] <= s:
                    pending.remove(it)
                    it[1]()
            if j == NT - 1:
                b3_epilogue(h, g)
                pending.append([s + 7, (lambda gg: (lambda: b3_epilogue_p2(h, gg)))(g)])
            if s % 8 == 2 and inter:
                p1, p2 = inter.pop(0)
                p1()
                if p2 is not None:
                    pending.append([s + 5, p2])
        for it in pending:
            it[1]()
        while inter:
            p1, p2 = inter.pop(0)
            p1()
            if p2 is not None:
                p2()

    nheads = 0 if "B" in skip else 8
    if nheads:
        load_head_w(0)
        load_head_w(1)
        for i in range(NT):
            b2_tile(0, i, pb=i % 8)
            if i >= 2:
                b2_tile_p2(0, i - 2, pb=(i - 2) % 8)
        for i in range(NT - 2, NT):
            b2_tile_p2(0, i, pb=i % 8)
    for h in range(nheads):
        inter = []
        if h + 1 < nheads:
            inter = [((lambda hh, ii: (lambda: b2_tile(hh, ii)))(h + 1, i),
                      (lambda hh, ii: (lambda: b2_tile_p2(hh, ii)))(h + 1, i)) for i in range(NT)]
            if h + 2 < nheads:
                inter.insert(NT // 2, ((lambda hh: (lambda: load_head_w(hh)))(h + 2), None))
        b3_head(h, inter)
    S.barrier()
    if stop_after == "B":
        return finish(nc, S, ctx, out)

    AR.off = base_persist
    wgst = AR.alloc([128, KC, 16], F32)
    wgb = AR.alloc([128, KC, 16], BF16)
    G = AR.alloc([128, NT, 16], F32)
    gb16 = AR.alloc([128, 16], F32)

    def arr():
        return AR.alloc([128, 2, 128], F32)

    SPl, NB, GTn, E_, EMAX, MN, MP, A1, A2, W1, W2, UP, IW, ED, T1, T2 = [arr() for _ in range(16)]
    emT = AR.alloc([128, 2], F32)
    dg1 = AR.alloc([128, 128], F32)
    dg = [dg1, dg1]
    zero4 = AR.alloc([128, 4], F32)
    mst = AR.alloc([128, KC, 128], F32)
    wmD = [AR.alloc([128, KC, 768], BF16) for _ in range(2)]
    qT = AR.alloc([128, T], BF16)
    kT = AR.alloc([128, T], BF16)
    ktok = AR.alloc([128, NT, 128], BF16)
    vaug2 = AR.alloc([128, NT, 258], BF16)
    hfwd = AR.alloc([128, NT, 256], F32)
    CnD = [AR.alloc([128, 264], F32) for _ in range(2)]
    CbD = [[AR.alloc([128, 264], BF16) for _ in range(2)] for _ in range(2)]
    STD = [[AR.alloc([128, 128], BF16) for _ in range(3)] for _ in range(2)]
    kwD = [[AR.alloc([128, 128], BF16) for _ in range(2)] for _ in range(2)]
    hm = [AR.alloc([128, 256], F32) for _ in range(2)]
    sg = [AR.alloc([128, 256], F32) for _ in range(2)]
    hn = [AR.alloc([128, 256], F32) for _ in range(2)]
    hab = [AR.alloc([128, 256], BF16) for _ in range(2)]
    hast = [AR.alloc([128, 256], BF16) for _ in range(2)]
    mnbc = AR.alloc([128, D], F32)
    junk2 = AR.alloc([128, 256], BF16)
    dnn = [AR.alloc([128, 8], F32) for _ in range(4)]

    def v4(a):
        return a.rearrange("p d (c h) -> p d c h", c=NT)

    Gv = G.rearrange("p i (t h) -> p i t h", t=4)
    dma(wgst, w_in_v[:, :, OFF_MG:OFF_MG + 16], "wgst", w=["wgst"])
    cp("pool", wgb, wgst, r=["wgst"], w=["wgb"])
    dma(gb16, gate_bias.partition_broadcast(128), "gb16", w=["gb16"])
    dma(mnbc, m_out_norm.partition_broadcast(128), "mnbc", w=["mnbc"])
    mset("pool", vaug2[:, :, 256:258], 1.0, w=["vaug2one"])
    mset("pool", zero4, 0.0, w=["zero4"])
    for i in range(NT):
        for c in range(KC):
            mm(PS[0][:, i * 16:(i + 1) * 16], xnT[:, c, i * 128:(i + 1) * 128], wgb[:, c, :], c == 0, c == KC - 1,
               r=["wgb"], w=["ps0"])
    tt("dve", G, PS[0][:, :].rearrange("p (i k) -> p i k", i=NT), gb16.unsqueeze(1).broadcast_to([128, NT, 16]),
       ALU.add, r=["ps0", "gb16"], w=["G"])
    for d in range(2):
        act(v4(SPl)[:, d], Gv[:, :, 2 * d + 1, :], AF.Exp, r=["G"], w=[f"SPl{d}"], scale=-1.0)
        act(SPl[:, d], SPl[:, d], AF.Ln, r=[f"SPl{d}"], w=[f"SPl{d}"], bias=1.0, scale=1.0)
        mm(PS[1][:, d * 128:(d + 1) * 128], U[d], SPl[:, d], True, True, r=[f"SPl{d}"], w=["ps1"])
        mm(PS[2][:, d * 128:(d + 1) * 128], onesf, SPl[:, d], True, True, r=[f"SPl{d}"], w=["ps2"])
    cp("dve", NB, PS[1][:, 0:256].rearrange("p (d k) -> p d k", d=2), r=["ps1"], w=["NB"])
    cp("dve", GTn, PS[2][:, 0:256].rearrange("p (d k) -> p d k", d=2), r=["ps2"], w=["GTn"])
    for d in range(2):
        tt("dve", v4(E_)[:, d], v4(NB)[:, d], Gv[:, :, 2 * d, :], ALU.add, r=["NB", "G"], w=[f"E{d}"])
        tr(PS[3][:, d * 128:(d + 1) * 128], E_[:, d], identf, r=[f"E{d}"], w=["ps3"])
        S.add("dve", (lambda dd: (lambda e: e.tensor_reduce(out=emT[:, dd:dd + 1], in_=PS[3][:, dd * 128:(dd + 1) * 128],
                                                            axis=AX.X, op=ALU.max)))(d), ["ps3"], [f"emT{d}"])
        ts("dve", dg[d], identf, emT[:, d:d + 1], ALU.mult, r=[f"emT{d}"], w=["dg"])
        mm(PS[4][:, d * 128:(d + 1) * 128], onesf, dg[d], True, True, r=["dg"], w=["ps4"])
    cp("dve", EMAX, PS[4][:, 0:256].rearrange("p (d k) -> p d k", d=2), r=["ps4"], w=["EMAX"])
    if cstage == 0:
        return finish(nc, S, ctx, out)
    for d in range(2):
        order = list(range(NT)) if d == 0 else list(range(NT - 1, -1, -1))
        prev = zero4
        pk = "zero4"
        for c in order:
            sl4 = slice(c * 4, (c + 1) * 4)
            tt("dve", T1[:, d, sl4], prev, EMAX[:, d, sl4], ALU.max, r=[pk, "EMAX"], w=[f"T1s{d}_{c}"])
            tt("dve", MN[:, d, sl4], T1[:, d, sl4], GTn[:, d, sl4], ALU.subtract, r=[f"T1s{d}_{c}", "GTn"], w=[f"MN{d}_{c}"])
            prev = MN[:, d, sl4]
            pk = f"MN{d}_{c}"
    mnk = [f"MN{d}_{c}" for d in range(2) for c in range(NT)]
    cp("dve", MP[:, 0, 4:128], MN[:, 0, 0:124], r=mnk, w=["MP"])
    mset("dve", MP[:, 0, 0:4], 0.0, w=["MP"])
    cp("dve", MP[:, 1, 0:124], MN[:, 1, 4:128], r=mnk, w=["MP"])
    mset("dve", MP[:, 1, 124:128], 0.0, w=["MP"])
    tt("dve", T1, MP, MN, ALU.subtract, r=["MP"] + mnk, w=["T1"])
    tt("dve", T1, T1, GTn, ALU.subtract, r=["T1", "GTn"], w=["T1"])
    act(A1, T1, AF.Exp, r=["T1"], w=["A1"])
    tt("dve", T2, EMAX, MN, ALU.subtract, r=["EMAX"] + mnk, w=["T2"])
    tt("dve", T2, T2, GTn, ALU.subtract, r=["T2", "GTn"], w=["T2"])
    act(A2, T2, AF.Exp, r=["T2"], w=["A2"])
    tt("dve", T1, E_, EMAX, ALU.subtract, r=["E0", "E1", "EMAX", "A1"], w=["T1"])
    act(W1, T1, AF.Exp, r=["T1"], w=["W1"])
    tt("dve", UP, EMAX, MP, ALU.max, r=["EMAX", "MP"], w=["UP"])
    tt("dve", T2, E_, UP, ALU.subtract, r=["E0", "E1", "UP", "A2"], w=["T2"])
    act(W2, T2, AF.Exp, r=["T2"], w=["W2"])
    tt("dve", T1, MP, UP, ALU.subtract, r=["MP", "UP", "W1"], w=["T1"])
    act(IW, T1, AF.Exp, r=["T1"], w=["IW"])
    tt("dve", T2, NB, UP, ALU.subtract, r=["NB", "UP", "W2"], w=["T2"])
    act(ED, T2, AF.Exp, r=["T2"], w=["ED"])

    if cstage == 1:
        return finish(nc, S, ctx, out)
    KSC = 128.0 ** -0.5
    def load_mw(h):
        wm_ = wmD[h % 2]
        q_ = h % 2
        subs = ((OFF_MQ + h * 128, 0, 0), (OFF_MK + h * 128, 128, 0),
                (OFF_MV + h * 256, 256, 1), (OFF_MV + h * 256 + 128, 384, 1),
                (OFF_MO + h * 256, 512, 2), (OFF_MO + h * 256 + 128, 640, 2))
        for off, dc, part in subs:
            dma(mst, w_in_v[:, :, off:off + 128], "mst", w=["mst"])
            cp("pool", wm_[:, :, dc:dc + 128], mst, r=["mst"], w=[f"wm{part}_{q_}"])

    nh_c = 4 if "C" not in skip else 0
    if nh_c:
        load_mw(0)
    for h in range(nh_c):
        wm = wmD[h % 2]
        wq = h % 2
        if h + 1 < nh_c:
            load_mw(h + 1)
        if cstage == 10:
            return finish(nc, S, ctx, out)
        n = 0
        for tb in range(8):
            for which, dst, scl in ((0, qT, 1.0), (1, kT, KSC)):
                bk = n % 2
                n += 1
                for c in range(KC):
                    mm(PS[bk][:, :], wm[:, c, which * 128:(which + 1) * 128], xnT[:, c, tb * 512:(tb + 1) * 512],
                       c == 0, c == KC - 1, r=[f"wm0_{wq}"], w=[f"ps{bk}"])
                act(dst[:, tb * 512:(tb + 1) * 512], PS[bk][:, :], AF.Identity, r=[f"ps{bk}"],
                    w=[("qT" if which == 0 else "kT")], scale=scl)
        if cstage == 11:
            return finish(nc, S, ctx, out)
        for i in range(NT):
            bk = 2 + i % 2
            for c in range(KC):
                mm(PS[bk][:, 0:384], xnT[:, c, i * 128:(i + 1) * 128], wm[:, c, 128:512], c == 0, c == KC - 1,
                   r=[f"wm0_{wq}", f"wm1_{wq}"], w=[f"ps{bk}"])
            cp("act", vaug2[:, i, 0:256], PS[bk][:, 128:384], r=[f"ps{bk}"], w=[f"v2_{i}"])
            ts("dve", ktok[:, i, :], PS[bk][:, 0:128], KSC, ALU.mult, r=[f"ps{bk}"], w=[f"ktok{i}"])
        if cstage == 2:
            return finish(nc, S, ctx, out)
        orders = [list(range(NT)), list(range(NT - 1, -1, -1))]
        for d in range(2):
            mset("pool", CnD[d], 0.0, w=[f"Cn{d}"])
            mset("pool", CbD[d][0], 0.0, w=[f"Cb{d}_0"])

        def pre(d, idx):
            c = orders[d][idx]
            bk = 4 + d
            mm(PS[bk][:, 0:128], kT[:, c * 128:(c + 1) * 128], qT[:, c * 128:(c + 1) * 128], True, True,
               r=["qT", "kT"], w=[f"ps{bk}"])
            stt(STD[d][idx % 3], PS[bk][:, 0:128], W2[:, d, c * 4 + h:c * 4 + h + 1], U[d], ALU.mult, ALU.mult,
                r=[f"ps{bk}", "W2"], w=[f"ST{d}_{idx % 3}"])

        def chunk(d, idx, ncomb):
            c = orders[d][idx]
            hb_ = 6 + d
            k2 = idx % 2
            dk4 = (2 * idx + d) % 4
            dn = dnn[dk4]
            col = c * 4 + h
            Cn_ = CnD[d]
            mm(PS[hb_][:, 0:257], STD[d][idx % 3], vaug2[:, c, 0:257], True, False,
               r=[f"ST{d}_{idx % 3}", f"v2_{c}", "vaug2one"], w=[f"ps{hb_}"])
            mm(PS[hb_][:, 0:257], qT[:, c * 128:(c + 1) * 128], CbD[d][k2][:, 0:257], False, True,
               r=["qT", f"Cb{d}_{k2}"], w=[f"ps{hb_}"])
            if idx + 1 < NT:
                cn_ = orders[d][idx + 1]
                kw_ = kwD[d][idx % 2]
                ts("pool", kw_, ktok[:, c, :], W1[:, d, col:col + 1], ALU.mult,
                   r=[f"ktok{c}", "W1"], w=[f"kw{d}_{idx % 2}"], s2=1.0, op1=ALU.mult)
                mm(PS[1][:, 0:257], kw_, vaug2[:, c, 0:257], True, True,
                   r=[f"kw{d}_{idx % 2}", f"v2_{c}", "vaug2one"], w=["ps1"])
                ts("dve", Cn_[:, 0:257], Cn_[:, 0:257], A1[:, d, col:col + 1], ALU.mult, r=[f"Cn{d}", "A1"], w=[f"Cn{d}"])
                stt(Cn_[:, 0:257], PS[1][:, 0:257], A2[:, d, col:col + 1], Cn_[:, 0:257], ALU.mult, ALU.add,
                    r=["ps1", "A2", f"Cn{d}"], w=[f"Cn{d}"])
                k3 = (idx + 1) % 2
                ts("pool", CbD[d][k3][:, 0:257], Cn_[:, 0:257], IW[:, d, cn_ * 4 + h:cn_ * 4 + h + 1], ALU.mult,
                   r=[f"Cn{d}", "IW"], w=[f"Cb{d}_{k3}"], s2=1.0, op1=ALU.mult)
            ts("dve", dn[:, 5:6], PS[hb_][:, 256:257], -1.0, ALU.mult, r=[f"ps{hb_}", "ED"], w=[f"dn{dk4}z"],
               s2=ED[:, d, col:col + 1], op1=ALU.max)
            stt(dn[:, 0:1], PS[hb_][:, 256:257], 1.0, dn[:, 5:6], ALU.mult, ALU.max,
                r=[f"ps{hb_}", f"dn{dk4}z"], w=[f"dn{dk4}a"])
            recip(dn[:, 1:2], dn[:, 0:1], r=[f"dn{dk4}a"], w=[f"dn{dk4}b"])
            if idx < NT // 2:
                ts("dve", hfwd[:, c, :], PS[hb_][:, 0:256], dn[:, 1:2], ALU.mult, r=[f"ps{hb_}", f"dn{dk4}b"], w=[f"hf{c}"])
                return
            k2 = ncomb % 2
            stt(hm[k2], PS[hb_][:, 0:256], dn[:, 1:2], hfwd[:, c, :], ALU.mult, ALU.add,
                r=[f"ps{hb_}", f"dn{dk4}b", f"hf{c}"], w=[f"hm{k2}"])
            act(junk2, hm[k2], AF.Square, r=[f"hm{k2}"], w=["junk2", f"dn{dk4}c"], accum=dn[:, 2:3])
            act(dn[:, 3:4], dn[:, 2:3], AF.Ln, r=[f"dn{dk4}c"], w=[f"dn{dk4}d"], bias=EPS, scale=1.0 / 256)
            act(dn[:, 4:5], dn[:, 3:4], AF.Exp, r=[f"dn{dk4}d"], w=[f"dn{dk4}e"], scale=-0.5)
            mb = 2 + ncomb % 2
            for cc in range(KC):
                mm(PS[mb][:, 0:256], xnT[:, cc, c * 128:(c + 1) * 128], wm[:, cc, 512:768], cc == 0, cc == KC - 1,
                   r=[f"wm2_{wq}"], w=[f"ps{mb}"])
            cp("dve", sg[k2], PS[mb][:, 0:256], r=[f"ps{mb}"], w=[f"sg{k2}"])
            act(sg[k2], sg[k2], AF.Exp, r=[f"sg{k2}"], w=[f"sg{k2}"], scale=-1.0)
            act(sg[k2], sg[k2], AF.Ln, r=[f"sg{k2}"], w=[f"sg{k2}"], bias=1.0, scale=1.0)
            act(sg[k2], sg[k2], AF.Exp, r=[f"sg{k2}"], w=[f"sg{k2}"], scale=-1.0)
            stt(hn[k2], hm[k2], dn[:, 4:5], mnbc[:, h * 256:(h + 1) * 256], ALU.mult, ALU.mult,
                r=[f"hm{k2}", f"dn{dk4}e", "mnbc"], w=[f"hn{k2}"])
            tt("pool", hab[k2], hn[k2], sg[k2], ALU.mult, r=[f"hn{k2}", f"sg{k2}"], w=[f"hab{k2}"])
            tr(psb(0, 0, 128), hab[k2][:, 0:128], identb, r=[f"hab{k2}"], w=["ps0"])
            tr(psb(0, 128, 256), hab[k2][:, 128:256], identb, r=[f"hab{k2}"], w=["ps0"])
            cp("dve", hast[k2], psb(0, 0, 256), r=["ps0"], w=[f"hast{k2}"])
            dma(haT_v[:, 2 * h:2 * h + 2, c * 128:(c + 1) * 128], hast[k2].rearrange("p (a t) -> p a t", a=2),
                f"hast{k2}", r=[f"hast{k2}"], w=[f"haT{h}_{c}"])

        pre(0, 0)
        pre(1, 0)
        ncomb = 0
        for idx in range(NT):
            for d in range(2):
                if idx + 1 < NT:
                    pre(d, idx + 1)
                chunk(d, idx, ncomb)
                if idx >= NT // 2:
                    ncomb += 1
    S.barrier()
    if stop_after == "C":
        return finish(nc, S, ctx, out)

    AR.off = base_persist
    wst1 = [AR.alloc([128, KC, 256], F32) for _ in range(2)]
    wmat = [AR.alloc([128, KC, D], BF16) for _ in range(4)]
    hAB = [[AR.alloc([128, KC, 256], BF16) for _ in range(2)] for _ in range(2)]
    eg = [AR.alloc([128, 512], F32) for _ in range(2)]
    y1 = [AR.alloc([128, 512], F32) for _ in range(2)]
    yTb = [AR.alloc([128, KC, 256], BF16) for _ in range(2)]
    srcs = [p_a.rearrange("(c p) n -> p c n", p=128), p_b.rearrange("(c p) n -> p c n", p=128),
            w_in_v[:, :, OFF_GA:OFF_GA + D], w_in_v[:, :, OFF_GB:OFF_GB + D]]
    npc = 0
    for pc in range(4):
        for wi in range(4):
            k = npc % 2
            npc += 1
            dma(wst1[k], srcs[wi][:, :, pc * 256:(pc + 1) * 256], f"wst1_{k}", w=[f"wst1_{k}"])
            cp("pool", wmat[wi][:, :, pc * 256:(pc + 1) * 256], wst1[k], r=[f"wst1_{k}"], w=[f"wmat{wi}_{pc}"])
    for b in range(16):
        sl = b % 2
        t0 = b * 256
        dma(hAB[0][sl], haT_v[:, :, t0:t0 + 256], f"hA{sl}", w=[f"hA{sl}"])
        dma(hAB[1][sl], hbT_v[:, :, t0:t0 + 256], f"hB{sl}", w=[f"hB{sl}"])
        for m in range(8):
            yb_ = m % 2
            gk = 2 + m % 2
            for half, (wi, rk) in enumerate(((0, f"hA{sl}"), (1, f"hB{sl}"))):
                for c in range(KC):
                    mm(PS[yb_][:, half * 256:(half + 1) * 256], wmat[wi][:, c, m * 128:(m + 1) * 128],
                       hAB[half][sl][:, c, :], c == 0, c == KC - 1, r=[f"wmat{wi}_{m // 2}", rk], w=[f"ps{yb_}"])
            for half, wi in enumerate((2, 3)):
                for c in range(KC):
                    mm(PS[gk][:, half * 256:(half + 1) * 256], wmat[wi][:, c, m * 128:(m + 1) * 128],
                       xnT[:, c, t0:t0 + 256], c == 0, c == KC - 1, r=[f"wmat{wi}_{m // 2}"], w=[f"ps{gk}"])
            act(eg[m % 2], PS[gk][:, :], AF.Exp, r=[f"ps{gk}"], w=[f"eg{m % 2}"], scale=-1.0)
            act(eg[m % 2], eg[m % 2], AF.Ln, r=[f"eg{m % 2}"], w=[f"eg{m % 2}"], bias=1.0, scale=1.0)
            act(eg[m % 2], eg[m % 2], AF.Exp, r=[f"eg{m % 2}"], w=[f"eg{m % 2}"], scale=-1.0)
            tt("dve", y1[m % 2], PS[yb_][:, :], eg[m % 2], ALU.mult, r=[f"ps{yb_}", f"eg{m % 2}"], w=[f"y1{m % 2}"])
            tt("pool", yTb[sl][:, m, :], y1[m % 2][:, 0:256], y1[m % 2][:, 256:512], ALU.add,
               r=[f"y1{m % 2}"], w=[f"yTb{sl}_{m}"])
        dma(yT_v[:, :, t0:t0 + 256], yTb[sl], f"yTb{sl}", r=[f"yTb{sl}_{m}" for m in range(8)], w=[f"yTs{b}"])
    S.barrier()
    if stop_after == "D1":
        return finish(nc, S, ctx, out)

    AR.off = base_consts
    wup = AR.alloc([128, KC, 2 * DFF], BF16)
    wdn = AR.alloc([128, NJ, D], BF16)
    base_e = AR.off
    wst2 = [AR.alloc([128, KC, 256], F32) for _ in range(2)]
    wo = AR.alloc([128, KC, D], BF16)
    yTt = [AR.alloc([128, KC, 128], BF16) for _ in range(2)]
    xt2 = [AR.alloc([128, D], F32) for _ in range(2)]
    x1t = [AR.alloc([128, D], F32) for _ in range(2)]
    g2bc = AR.alloc([128, D], F32)
    xn2b = [AR.alloc([128, D], BF16) for _ in range(2)]
    xn2st = [AR.alloc([128, D], BF16) for _ in range(2)]
    junk3 = AR.alloc([128, D], BF16)
    st2 = [AR.alloc([128, 4], F32) for _ in range(2)]
    end_d2 = AR.off
    wo_src = w_o.rearrange("(c p) n -> p c n", p=128)
    wup_src = w_up.rearrange("(c p) n -> p c n", p=128)
    wdn_src = w_down.rearrange("(j p) n -> p j n", p=128)
    npc = 0
    for pc in range(4):
        k = npc % 2
        npc += 1
        dma(wst2[k], wo_src[:, :, pc * 256:(pc + 1) * 256], f"wst2_{k}", w=[f"wst2_{k}"])
        cp("pool", wo[:, :, pc * 256:(pc + 1) * 256], wst2[k], r=[f"wst2_{k}"], w=["wo"])
    dma(g2bc, norm2.partition_broadcast(128), "g2bc", w=["g2bc"])
    ffn_pieces = [("up", pc) for pc in range(22)] + [("dn", pc) for pc in range(11)]

    def load_ffn_piece(kind, pc):
        nonlocal npc
        k = npc % 2
        npc += 1
        if kind == "up":
            dma(wst2[k], wup_src[:, :, pc * 256:(pc + 1) * 256], f"wst2_{k}", w=[f"wst2_{k}"])
            cp("pool", wup[:, :, pc * 256:(pc + 1) * 256], wst2[k], r=[f"wst2_{k}"], w=["wup"])
        else:
            dma(wst2[k].rearrange("p (a b) n -> p a (b n)", a=2), wdn_src[:, 2 * pc:2 * pc + 2, :], f"wst2_{k}",
                w=[f"wst2_{k}"])
            cp("pool", wdn[:, 2 * pc:2 * pc + 2, :], wst2[k].rearrange("p (a b) n -> p a (b n)", a=2),
               r=[f"wst2_{k}"], w=["wdn"])

    for i in range(NT):
        b = i % 2
        dma(yTt[b], yT_v[:, :, i * 128:(i + 1) * 128], f"yTt{b}", w=[f"yTt{b}"])
        dma(xt2[b], x[i * 128:(i + 1) * 128, :], f"xt2{b}", w=[f"xt2{b}"])
        if i < len(ffn_pieces):
            load_ffn_piece(*ffn_pieces[i])
        for n in range(2):
            bk = 2 * b + n
            for m in range(KC):
                mm(PS[bk][:, :], yTt[b][:, m, :], wo[:, m, n * 512:(n + 1) * 512], m == 0, m == KC - 1,
                   r=[f"yTt{b}", "wo"], w=[f"ps{bk}"])
            tt("dve", x1t[b][:, n * 512:(n + 1) * 512], PS[bk][:, :], xt2[b][:, n * 512:(n + 1) * 512], ALU.add,
               r=[f"ps{bk}", f"xt2{b}"], w=[f"x1t{b}_{n}"])
        dma(x1s[i * 128:(i + 1) * 128, :], x1t[b], f"x1t{b}", r=[f"x1t{b}_0", f"x1t{b}_1"], w=[f"x1s{i}"])
        act(junk3, x1t[b], AF.Square, r=[f"x1t{b}_0", f"x1t{b}_1"], w=["junk3", f"st2a{b}"], accum=st2[b][:, 0:1])
        act(st2[b][:, 1:2], st2[b][:, 0:1], AF.Ln, r=[f"st2a{b}"], w=[f"st2b{b}"], bias=EPS, scale=1.0 / D)
        act(st2[b][:, 2:3], st2[b][:, 1:2], AF.Exp, r=[f"st2b{b}"], w=[f"st2c{b}"], scale=-0.5)
        stt(xn2b[b], x1t[b], st2[b][:, 2:3], g2bc, ALU.mult, ALU.mult,
            r=[f"x1t{b}_0", f"x1t{b}_1", f"st2c{b}", "g2bc"], w=[f"xn2b{b}"])
        for c in range(KC):
            tr(psb(4 + b, c * 128, (c + 1) * 128), xn2b[b][:, c * 128:(c + 1) * 128], identb,
               r=[f"xn2b{b}"], w=[f"ps{4 + b}"])
        cp("dve", xn2st[b], psb(4 + b, 0, 1024), r=[f"ps{4 + b}"], w=[f"xn2st{b}"])
        dma(xn2T_v[:, :, i * 128:(i + 1) * 128], xn2st[b].rearrange("p (c t) -> p c t", c=KC), f"xn2st{b}",
            r=[f"xn2st{b}"], w=[f"xn2T{i}"])
    for kind, pc in ffn_pieces[NT:]:
        load_ffn_piece(kind, pc)
    S.barrier()
    if stop_after == "D2":
        return finish(nc, S, ctx, out)

    AR.off = base_e
    cw = AR.alloc([128, 44, 3], F32)
    cb = AR.alloc([128, 44], F32)
    xw = [AR.alloc([128, KC, 258], BF16) for _ in range(2)]
    actT = [AR.alloc([128, NJ, 256], BF16) for _ in range(2)]
    NB_E = 4
    ca = [AR.alloc([128, 256], F32) for _ in range(NB_E)]
    cg = [AR.alloc([128, 256], F32) for _ in range(NB_E)]
    x2 = [AR.alloc([128, 256], F32) for _ in range(NB_E)]
    zz = [AR.alloc([128, 256], F32) for _ in range(NB_E)]
    ez = [AR.alloc([128, 256], F32) for _ in range(NB_E)]
    x1e = [AR.alloc([128, D], F32) for _ in range(2)]
    ucp = [AR.alloc([128, 260], F32) for _ in range(2 * NB_E)]
    dma(cw, conv_wp.rearrange("p (j k) -> p j k", k=3), "cw", w=["cw"])
    dma(cb, conv_bp, "cb", w=["cb"])
    pend_down = []
    for b in range(16):
        sl = b % 2
        t0 = b * 256
        if b == 0:
            mset("pool", xw[sl][:, :, 0:2], 0.0, w=[f"xw{sl}", f"xw{sl}h"])
            dma(xw[sl][:, :, 1:258], xn2T_v[:, :, 0:257], f"xw{sl}", w=[f"xw{sl}"])
        elif b == 15:
            mset("pool", xw[sl][:, :, 256:258], 0.0, w=[f"xw{sl}", f"xw{sl}h"])
            dma(xw[sl][:, :, 0:257], xn2T_v[:, :, t0 - 1:T], f"xw{sl}", w=[f"xw{sl}"])
        else:
            dma(xw[sl], xn2T_v[:, :, t0 - 1:t0 + 257], f"xw{sl}", w=[f"xw{sl}", f"xw{sl}h"])
        for j in range(NJ):
            if j == 8 and pend_down:
                pend_down.pop(0)()
            s = j % NB_E
            for bank, ch, dst, dk_ in ((2 * (j % 2), j, ca[s], f"ca{s}"), (2 * (j % 2) + 1, NJ + j, cg[s], f"cg{s}")):
                for c in range(KC):
                    mm(PS[bank][:, 0:258], wup[:, c, ch * 128:(ch + 1) * 128], xw[sl][:, c, :], c == 0, c == KC - 1,
                       r=[f"xw{sl}", f"xw{sl}h"], w=[f"ps{bank}"])
                ui = 2 * s + (bank % 2)
                uc = ucp[ui][:, 0:258]
                act(uc, PS[bank][:, 0:258], AF.Copy, r=[f"ps{bank}"], w=[f"uc{ui}"])
                act(dst, uc[:, 1:257], AF.Identity, r=[f"uc{ui}", "cw", "cb"], w=[dk_],
                    scale=cw[:, ch, 1:2], bias=cb[:, ch:ch + 1])
                stt(dst, uc[:, 0:256], cw[:, ch, 0:1], dst, ALU.mult, ALU.add, r=[f"uc{ui}", dk_], w=[dk_])
                stt(dst, uc[:, 2:258], cw[:, ch, 2:3], dst, ALU.mult, ALU.add, r=[f"uc{ui}", dk_], w=[dk_])
            act(ez[s], cg[s], AF.Gelu_apprx_tanh, r=[f"cg{s}"], w=[f"ez{s}"])
            tt("dve", actT[sl][:, j, :], ez[s], ca[s], ALU.mult, r=[f"ez{s}", f"ca{s}"], w=[f"actT{sl}"])
        def down_proj(b=b, sl=sl):
            for t2 in range(2):
                tile_i = b * 2 + t2
                dma(x1e[t2], x1s[tile_i * 128:(tile_i + 1) * 128, :], f"x1e{t2}", w=[f"x1e{t2}"])
                for n in range(2):
                    bk = 4 + 2 * t2 + n
                    for j in range(NJ):
                        mm(PS[bk][:, :], actT[sl][:, j, t2 * 128:(t2 + 1) * 128], wdn[:, j, n * 512:(n + 1) * 512],
                           j == 0, j == NJ - 1, r=[f"actT{sl}"], w=[f"ps{bk}"])
                    tt("dve", x1e[t2][:, n * 512:(n + 1) * 512], PS[bk][:, :], x1e[t2][:, n * 512:(n + 1) * 512],
                       ALU.add, r=[f"ps{bk}", f"x1e{t2}"], w=[f"x1e{t2}"])
                dma(out[tile_i * 128:(tile_i + 1) * 128, :], x1e[t2], f"x1e{t2}", r=[f"x1e{t2}"], w=[f"out{tile_i}"])

        pend_down.append(down_proj)
    for f_ in pend_down:
        f_()
    return finish(nc, S, ctx, out)


def finish(nc, S, ctx, out, debug_dump=None):
    if debug_dump is not None:
        S.barrier()
        src, dst = debug_dump
        S.add("sp", lambda e: e.dma_start(out=dst, in_=src), (), (), dk="dbg", nbytes=1 << 23)
    S.barrier()
    for e_ in ENGS:
        S.add(e_, None, (), ())
    S.schedule()
    esem = {e: ctx.enter_context(nc.semaphore(f"se_{e}")) for e in ENGS if e != "sp"}
    dsem = {k: ctx.enter_context(nc.semaphore(f"sd_{k}")) for k in S.dma_n}
    with nc.Block() as block:
        @block.sync
        def _(e):
            S.emit("sp", e, esem, dsem)

        @block.scalar
        def _(e):
            S.emit("act", e, esem, dsem)

        @block.vector
        def _(e):
            S.emit("dve", e, esem, dsem)

        @block.gpsimd
        def _(e):
            S.emit("pool", e, esem, dsem)

        @block.tensor
        def _(e):
            S.emit("pe", e, esem, dsem)
    ctx.close()
    nc._est_ns = S.est_ns
    return nc


def host_consts():
    ident = np.eye(128, dtype=np.float32)
    ones = np.ones((128, 128), np.float32)
    s = np.arange(128)
    ufwd = (s[:, None] <= s[None, :]).astype(np.float32)
    ubwd = (s[:, None] >= s[None, :]).astype(np.float32)
    cst = np.concatenate([ident, ones, ufwd, ubwd], axis=1)
    pos = np.arange(T, dtype=np.float32)
    inv = (10000.0 ** (-np.arange(0, 64, 2, dtype=np.float32) / 64)).astype(np.float32)
    ang = pos[:, None] * inv[None, :]
    cos = np.cos(ang).astype(np.float32)
    sin = np.sin(ang).astype(np.float32)
    cos2 = np.concatenate([cos, cos], axis=1).reshape(32, 128, 64).transpose(1, 0, 2)
    sin2 = np.concatenate([-sin, sin], axis=1).reshape(32, 128, 64).transpose(1, 0, 2)
    rope = np.stack([cos2, sin2], axis=1).reshape(128, -1)
    return np.ascontiguousarray(cst), np.ascontiguousarray(rope.astype(np.float32))


def make_in_maps(inp, cores):
    cst, rope = host_consts()
    f = lambda a: np.ascontiguousarray(np.asarray(a, dtype=np.float32))
    shared = {
        "w_in": f(inp["w_in"][0]), "p_a": f(inp["p_a"][0]), "p_b": f(inp["p_b"][0]), "w_o": f(inp["w_o"][0]),
        "w_up": f(inp["w_up"][0]), "w_down": f(inp["w_down"][0]),
        "norm1": f(inp["norm1"]), "norm2": f(inp["norm2"]), "m_out_norm": f(inp["m_out_norm"]),
        "gate_bias": f(inp["gate_bias"]), "q_norm": f(inp["q_norm"]), "k_norm": f(inp["k_norm"]),
        "lam4": f(np.concatenate([inp["lam_q1"], inp["lam_k1"], inp["lam_q2"], inp["lam_k2"]], axis=1)),
        "a_out_norm": f(inp["a_out_norm"]),
        "conv_wp": f(np.asarray(inp["conv_w"])[0, :, 0, :].reshape(3, 44, 128).transpose(2, 1, 0).reshape(128, 132)),
        "conv_bp": f(np.asarray(inp["conv_b"])[0].reshape(44, 128).T),
        "cst": cst, "rope": rope,
    }
    maps = []
    for b in cores:
        m = dict(shared)
        m["x"] = f(inp["x"][b])
        maps.append(m)
    return maps


_NC = None


def kernel(**inputs):
    global _NC
    if _NC is None:
        _NC = build()
    maps = make_in_maps(inputs, list(range(8)))
    res = run_bass_kernel_spmd(_NC, maps, core_ids=list(range(8)))
    return np.stack([np.asarray(r["out"]) for r in res.results], axis=0).astype(np.float32)
```

```python
import math
from contextlib import ExitStack

import numpy as np
import concourse.bass as bass
import concourse.mybir as mybir
from concourse.bass_utils import run_bass_kernel_spmd

F32 = mybir.dt.float32
BF16 = mybir.dt.bfloat16
AF = mybir.ActivationFunctionType
ALU = mybir.AluOpType
AX = mybir.AxisListType

T = 4096
D = 1024
NT = 32
KC = 8
DIN = 8208
OFF_MQ, OFF_MK, OFF_MV, OFF_MO, OFF_MG = 0, 512, 1024, 2048, 3072
OFF_AQ, OFF_AK, OFF_AV, OFF_GA, OFF_GB = 3088, 4112, 5136, 6160, 7184
DFF = 2816
NJ = 22
EPS = 1e-6
LAM_INIT = 0.8 - 0.6 * math.exp(-0.3 * 0)
ARENA_BYTES = 206 * 1024

ENGS = ("pe", "act", "dve", "pool", "sp")


class Op:
    __slots__ = ("eng", "fn", "pidx", "pos", "sig", "dk", "ordn", "waits", "cnt", "cost", "deps", "users",
                 "nd", "rt", "fin", "group", "lastpos", "dmacnt", "nbytes")

    def __init__(self, eng, fn, dk, cost):
        self.eng = eng
        self.fn = fn
        self.dk = dk
        self.cost = cost
        self.sig = False
        self.ordn = 0
        self.waits = []
        self.cnt = 0
        self.deps = []
        self.users = []
        self.nd = 0
        self.rt = 0.0
        self.fin = 0.0
        self.pos = -1
        self.group = None
        self.nbytes = 0


USE_DRAIN = False
MAX_DMA_INFLIGHT = 6
NDUM = 0
PSUM_EXCL = True
SEM_LAT = 300.0
DMA_LAT = 2000.0
DMA_BW = 160.0


class Sched:
    def __init__(self):
        self.all = []
        self.lastw = {}
        self.readers = {}
        self.dma_n = {}
        self.dma_last = {}
        self.cur_bar = None
        self.group = []
        self.order = None
        self.last_psr = {}
        self.dma_hist = []

    def add(self, eng, fn, r=(), w=(), dk=None, cost=100.0, nbytes=0):
        op = Op(eng, fn, dk, cost)
        op.pidx = len(self.all)
        op.nbytes = nbytes
        deps = {}
        psr = False

        def dep(d, raw):
            if d is op:
                return
            if id(d) in deps:
                if raw:
                    deps[id(d)] = (d, True)
            else:
                deps[id(d)] = (d, raw)

        for k in r:
            d = self.lastw.get(k)
            if d is not None:
                dep(d, True)
            if PSUM_EXCL and k[:2] == "ps" and eng in ("act", "dve"):
                psr = True
        if psr:
            other = "dve" if eng == "act" else "act"
            d = self.last_psr.get(other)
            if d is not None:
                dep(d, True)
            self.last_psr[eng] = op
        for k in w:
            d = self.lastw.get(k)
            if d is not None:
                dep(d, False)
            for d2 in self.readers.get(k, ()):
                dep(d2, False)
        if dk is not None:
            d = self.dma_last.get(dk)
            if d is not None:
                dep(d, True)
            self.dma_hist.append(op)
            if len(self.dma_hist) > MAX_DMA_INFLIGHT:
                dep(self.dma_hist[-1 - MAX_DMA_INFLIGHT], True)
            self.dma_n[dk] = self.dma_n.get(dk, 0) + 1
            op.ordn = self.dma_n[dk]
            self.dma_last[dk] = op
        if self.cur_bar is not None:
            dep(self.cur_bar, True)
        op.deps = list(deps.values())
        for d, _ in op.deps:
            d.users.append(op)
        self.all.append(op)
        self.group.append(op)
        for k in r:
            self.readers.setdefault(k, []).append(op)
        for k in w:
            self.lastw[k] = op
            self.readers[k] = []
        return op

    def barrier(self):
        b = Op("bar", None, None, 0.0)
        b.pidx = len(self.all)
        b.group = self.group
        b.deps = [(o, True) for o in self.group]
        if self.cur_bar is not None:
            b.deps.append((self.cur_bar, True))
        for d, _ in b.deps:
            d.users.append(b)
        self.all.append(b)
        self.group = []
        self.cur_bar = b
        self.lastw = {}
        self.readers = {}

    def schedule(self):
        import heapq
        engs = ENGS + ("bar",)
        order = {e: [] for e in engs}
        readyq = {e: [] for e in engs}
        busy = {e: False for e in engs}
        ev = []
        seq = 0
        dma_free = 0.0
        for o in self.all:
            o.nd = len(o.deps)
            o.rt = 0.0
            if o.nd == 0:
                heapq.heappush(ev, (0.0, seq, 0, o))
                seq += 1

        def start(e, t):
            nonlocal seq, dma_free
            o = heapq.heappop(readyq[e])[1]
            busy[e] = True
            o.rt = t
            o.pos = len(order[e])
            order[e].append(o)
            if o.dk is not None:
                tend = t + 60.0
                dma_free = max(dma_free, t) + o.nbytes / DMA_BW
                o.fin = dma_free + DMA_LAT
            else:
                tend = t + o.cost
                o.fin = tend
            heapq.heappush(ev, (tend, seq, 1, e))
            seq += 1
            for u in o.users:
                u.nd -= 1
                lat = 0.0 if (u.eng == o.eng and o.dk is None) else SEM_LAT
                if o.fin + lat > u.rt:
                    u.rt = o.fin + lat
                if u.nd == 0:
                    heapq.heappush(ev, (u.rt, seq, 0, u))
                    seq += 1

        while ev:
            t, _, kind, x = heapq.heappop(ev)
            if kind == 0:
                e = x.eng
                heapq.heappush(readyq[e], (x.pidx, x))
                if not busy[e]:
                    start(e, t)
            else:
                busy[x] = False
                if readyq[x]:
                    start(x, t)
        assert sum(len(v) for v in order.values()) == len(self.all), "scheduler: dependency cycle"
        self.order = order
        self.est_ns = max(o.fin for o in self.all)
        last = {}
        dcnt = {}
        for b in order["bar"]:
            for o in b.group:
                if o.dk is not None:
                    dcnt[o.dk] = max(dcnt.get(o.dk, 0), o.ordn)
                elif o.fn is not None:
                    p = last.get(o.eng)
                    if p is None or o.pos > p.pos:
                        last[o.eng] = o
            b.lastpos = dict(last)
            b.dmacnt = dict(dcnt)
        import bisect
        kn = {e: {} for e in ENGS}
        hist = {e: {} for e in ENGS}
        last_drain = {e: -1 for e in ENGS}

        def learn(e, p, key, val):
            if kn[e].get(key, 0) < val:
                kn[e][key] = val
                h = hist[e].setdefault(key, ([], []))
                h[0].append(p)
                h[1].append(val)

        def want(op, key, val, dop):
            e = op.eng
            if kn[e].get(key, 0) >= val:
                return
            if dop is not None:
                dop.sig = True
                op.waits.append(dop)
            else:
                op.waits.append((key[1], val))
            learn(e, op.pos, key, val)
            if dop is not None:
                F = dop.eng
                for k2, h in hist[F].items():
                    i = bisect.bisect_right(h[0], dop.pos) - 1
                    if i >= 0:
                        learn(e, op.pos, k2, h[1][i])

        seq_ops = sorted((o for o in self.all if o.eng != "bar"), key=lambda o: (o.rt, o.pidx))
        for op in seq_ops:
            e = op.eng
            for d, raw in op.deps:
                if d.eng == "bar":
                    for F, lo in d.lastpos.items():
                        if F != e:
                            want(op, ("eng", F), lo.pos + 1, lo)
                    for dk_, n_ in d.dmacnt.items():
                        want(op, ("dma", dk_), n_, None)
                elif d.dk is not None:
                    want(op, ("dma", d.dk), d.ordn, None)
                elif d.eng == e and op.dk is None:
                    if e != "pe" and raw:
                        if USE_DRAIN:
                            if d.pos > last_drain[e]:
                                op.waits.append("DRAIN")
                                last_drain[e] = op.pos
                        else:
                            want(op, ("eng", e), d.pos + 1, d)
                else:
                    want(op, ("eng", d.eng), d.pos + 1, d)
        for e in ENGS:
            c = 0
            for o in order[e]:
                if o.dk is None and o.sig:
                    c += 1
                o.cnt = c

    def emit(self, eng, e, esem, dsem):
        for o in self.order[eng]:
            for wt in o.waits:
                if wt == "DRAIN":
                    e.drain()
                elif isinstance(wt, tuple):
                    e.wait_ge(dsem[wt[0]], 16 * wt[1])
                else:
                    e.wait_ge(esem[wt.eng], wt.cnt)
            if o.fn is None:
                continue
            ins = o.fn(e)
            if o.dk is not None:
                ins.then_inc(dsem[o.dk], 16)
            elif o.sig:
                ins.then_inc(esem[eng], 1)


class Arena:
    def __init__(self, t, nbytes, base=0):
        self.t = t
        self.nbytes = nbytes
        self.off = base

    def alloc(self, shape, dt):
        n = 1
        for s in shape[1:]:
            n *= s
        nb = n * (4 if dt == F32 else 2)
        nb = (nb + 31) // 32 * 32
        o = self.off
        assert o + nb <= self.nbytes, f"arena overflow {o}+{nb}>{self.nbytes}"
        self.off = o + nb
        a = self.t[:, o // 4:(o + nb) // 4]
        if dt != F32:
            a = a.bitcast(dt)
        a = a[:, 0:n]
        if len(shape) == 3:
            a = a.rearrange("p (a b) -> p a b", a=shape[1])
        elif len(shape) == 4:
            a = a.rearrange("p (a b c) -> p a b c", a=shape[1], b=shape[2])
        return a


def build(debug=False, stop_after=None, skip=(), cstage=99, var=0):
    nc = bass.Bass("TRN2", target_bir_lowering=False)

    def din(name, shape, dt=F32):
        return nc.dram_tensor(name, list(shape), dt, kind="ExternalInput").ap()

    x = din("x", [T, D])
    w_in = din("w_in", [D, DIN])
    p_a = din("p_a", [D, D])
    p_b = din("p_b", [D, D])
    w_o = din("w_o", [D, D])
    w_up = din("w_up", [D, 2 * DFF])
    w_down = din("w_down", [DFF, D])
    norm1 = din("norm1", [1, D])
    norm2 = din("norm2", [1, D])
    m_out_norm = din("m_out_norm", [1, D])
    gate_bias = din("gate_bias", [1, 16])
    q_norm = din("q_norm", [1, 64])
    k_norm = din("k_norm", [1, 64])
    lam4 = din("lam4", [1, 256])
    a_out_norm = din("a_out_norm", [1, 128])
    conv_wp = din("conv_wp", [128, 44 * 3])
    conv_bp = din("conv_bp", [128, 44])
    cst = din("cst", [128, 4 * 128])
    rope = din("rope", [128, 2 * 32 * 64])
    skind = "ExternalOutput" if debug else "Internal"
    haT = nc.dram_tensor("haT", [D, T], BF16, kind=skind).ap()
    hbT = nc.dram_tensor("hbT", [D, T], BF16, kind=skind).ap()
    yTs = nc.dram_tensor("yTs", [D, T], BF16, kind=skind).ap()
    x1s = nc.dram_tensor("x1s", [T, D], F32, kind=skind).ap()
    xn2T = nc.dram_tensor("xn2T", [D, T], BF16, kind=skind).ap()
    out = nc.dram_tensor("out", [T, D], F32, kind="ExternalOutput").ap()

    w_in_v = w_in.rearrange("(c p) n -> p c n", p=128)
    haT_v = haT.rearrange("(f p) t -> p f t", p=128)
    hbT_v = hbT.rearrange("(f p) t -> p f t", p=128)
    yT_v = yTs.rearrange("(f p) t -> p f t", p=128)
    xn2T_v = xn2T.rearrange("(f p) t -> p f t", p=128)

    S = Sched()
    ctx = ExitStack()
    arena_t = ctx.enter_context(nc.sbuf_tensor("arena", [128, ARENA_BYTES // 4], F32))
    psall = ctx.enter_context(nc.psum_tensor("psall", [128, 4096], F32))
    PS = [psall[:, b * 512:(b + 1) * 512] for b in range(8)]
    AR = Arena(arena_t, ARENA_BYTES)

    def fsz(ap):
        n = 1
        for d_ in ap.shape[1:]:
            n *= d_
        return n

    def mm(o, lhsT, rhs, start, stop, r=(), w=(), sgc=False, cost=None):
        n = fsz(rhs)
        c = max(64, n) / 3.2 + 15.0
        if lhsT.dtype == F32:
            c *= 4
        if cost is not None:
            c = cost
        if sgc:
            return S.add("pe", lambda e: e.matmul(o, lhsT=lhsT, rhs=rhs, start=start, stop=stop, skip_group_check=True),
                         r, w, cost=c)
        return S.add("pe", lambda e: e.matmul(o, lhsT=lhsT, rhs=rhs, start=start, stop=stop), r, w, cost=c)

    def tr(o, in_, ident, r=(), w=()):
        return S.add("pe", lambda e: e.transpose(o, in_, ident), r, w, cost=110.0)

    def act(o, in_, func, r=(), w=(), bias=None, scale=None, accum=None):
        kw = {}
        c = fsz(in_) / 1.6 + 80.0
        if bias is not None:
            kw["bias"] = bias
            if not isinstance(bias, float):
                c += 90.0
        if scale is not None:
            kw["scale"] = scale
            if not isinstance(scale, float):
                c += 90.0
        if accum is not None:
            kw["accum_out"] = accum
            c += 90.0
        return S.add("act", lambda e: e.activation(out=o, in_=in_, func=func, **kw), r, w, cost=c)

    def ecost(eng, n):
        return (1.5 * n + 100.0) if eng == "pool" else (n / 0.96 + 70.0)

    def ts(eng, o, in0, s1, op0, r=(), w=(), s2=None, op1=None):
        c = ecost(eng, fsz(o))
        if op1 is None:
            return S.add(eng, lambda e: e.tensor_scalar(out=o, in0=in0, scalar1=s1, scalar2=None, op0=op0), r, w, cost=c)
        return S.add(eng, lambda e: e.tensor_scalar(out=o, in0=in0, scalar1=s1, scalar2=s2, op0=op0, op1=op1), r, w, cost=c)

    def tt(eng, o, in0, in1, op, r=(), w=()):
        return S.add(eng, lambda e: e.tensor_tensor(out=o, in0=in0, in1=in1, op=op), r, w, cost=ecost(eng, fsz(o)))

    def stt(o, in0, sc, in1, op0, op1, r=(), w=()):
        return S.add("dve", lambda e: e.scalar_tensor_tensor(out=o, in0=in0, scalar=sc, in1=in1, op0=op0, op1=op1), r, w,
                     cost=ecost("dve", fsz(o)))

    def red(o, in_, op, r=(), w=(), negate=None):
        c = ecost("dve", fsz(in_))
        if negate:
            return S.add("dve", lambda e: e.tensor_reduce(out=o, in_=in_, axis=AX.X, op=op, negate=True), r, w, cost=c)
        return S.add("dve", lambda e: e.tensor_reduce(out=o, in_=in_, axis=AX.X, op=op), r, w, cost=c)

    def recip(o, in_, r=(), w=()):
        return S.add("dve", lambda e: e.reciprocal(out=o, in_=in_), r, w, cost=2.0 * fsz(o) + 70.0)

    def cp(eng, o, in_, r=(), w=()):
        if eng == "act":
            return S.add(eng, lambda e: e.activation(out=o, in_=in_, func=AF.Copy), r, w, cost=fsz(o) / 1.15 + 110.0)
        return S.add(eng, lambda e: e.tensor_copy(out=o, in_=in_), r, w, cost=ecost(eng, fsz(o)))

    def mset(eng, o, val, r=(), w=()):
        return S.add(eng, lambda e: e.memset(o, val), r, w, cost=ecost(eng, fsz(o)))

    def dma(o, in_, dk, r=(), w=()):
        nb = fsz(o) * o.shape[0] * (4 if o.dtype == F32 else 2)
        return S.add("sp", lambda e: e.dma_start(out=o, in_=in_), r, w, dk=dk, nbytes=nb)

    def psb(b, n0, n1):
        return PS[b][:, :].bitcast(BF16)[:, n0:n1]

    cstt = AR.alloc([128, 4, 128], F32)
    identf = cstt[:, 0, :]
    onesf = cstt[:, 1, :]
    U = [cstt[:, 2, :], cstt[:, 3, :]]
    identb = AR.alloc([128, 128], BF16)
    smallv = AR.alloc([128, 64], F32)
    base_consts = AR.off
    xnT = AR.alloc([128, KC, T], BF16)
    base_persist = AR.off

    dma(cstt, cst.rearrange("p (a b) -> p a b", a=4), "cst", w=["cst"])
    cp("dve", identb, identf, r=["cst"], w=["identb"])

    xt = [AR.alloc([128, D], F32) for _ in range(2)]
    g1bc = AR.alloc([128, D], F32)
    junk = AR.alloc([128, D], BF16)
    xs = [AR.alloc([128, D], BF16) for _ in range(2)]
    ssq = AR.alloc([128, 3 * NT], F32)
    dma(g1bc, norm1.partition_broadcast(128), "g1bc", w=["g1bc"])
    for i in range(NT):
        b = i % 2
        dma(xt[b], x[i * 128:(i + 1) * 128, :], f"xt{b}", w=[f"xt{b}"])
        act(junk, xt[b], AF.Square, r=[f"xt{b}"], w=["junk", f"ss{i}"], accum=ssq[:, i:i + 1])
        act(ssq[:, NT + i:NT + i + 1], ssq[:, i:i + 1], AF.Ln, r=[f"ss{i}"], w=[f"ln{i}"], bias=EPS, scale=1.0 / D)
        act(ssq[:, 2 * NT + i:2 * NT + i + 1], ssq[:, NT + i:NT + i + 1], AF.Exp, r=[f"ln{i}"], w=[f"rs{i}"], scale=-0.5)
        stt(xs[b], xt[b], ssq[:, 2 * NT + i:2 * NT + i + 1], g1bc, ALU.mult, ALU.mult,
            r=[f"xt{b}", f"rs{i}", "g1bc"], w=[f"xs{b}"])
        for c in range(KC):
            tr(psb(b, c * 128, (c + 1) * 128), xs[b][:, c * 128:(c + 1) * 128], identb,
               r=[f"xs{b}", "identb"], w=[f"ps{b}"])
        cp("dve", xnT[:, :, i * 128:(i + 1) * 128],
           psb(b, 0, 1024).rearrange("p (c t) -> p c t", c=KC), r=[f"ps{b}"], w=[])
    S.barrier()
    if stop_after == "A":
        return finish(nc, S, ctx, out, debug_dump=(xnT, haT_v))

    AR.off = base_persist
    cs2 = AR.alloc([128, 2, 32, 64], F32)
    wst = AR.alloc([128, KC, 384], F32)
    wbf = [AR.alloc([128, KC, 384], BF16) for _ in range(2)]
    qkT = [AR.alloc([128, 2, T], BF16) for _ in range(2)]
    vaug = [AR.alloc([128, NT, 130], BF16) for _ in range(2)]
    pjs = [AR.alloc([128, 384], F32) for _ in range(3)]
    sqb = [AR.alloc([128, 256], F32) for _ in range(3)]
    xnq = [AR.alloc([128, 256], F32) for _ in range(3)]
    rA = [AR.alloc([128, 256], F32) for _ in range(3)]
    rB = [AR.alloc([128, 256], F32) for _ in range(3)]
    qkb = [AR.alloc([128, 256], BF16) for _ in range(3)]
    st4 = [AR.alloc([128, 12], F32) for _ in range(3)]
    pT = [AR.alloc([128, 1024], BF16) for _ in range(3)]
    gqk = AR.alloc([128, 256], F32)
    aon = AR.alloc([128, 128], F32)
    lamt = AR.alloc([128, 256], F32)
    lamp = AR.alloc([128, 128], F32)
    accs = [AR.alloc([128, 9, 129], F32) for _ in range(2)]
    od4 = [AR.alloc([128, 4, 128], F32) for _ in range(2)]
    t24 = AR.alloc([128, 4, 128], F32)
    sq4 = AR.alloc([128, 4, 128], F32)
    ob4 = [AR.alloc([128, 4, 128], BF16) for _ in range(2)]
    e8 = [AR.alloc([128, 24], F32) for _ in range(2)]
    hbst = [AR.alloc([128, 512], BF16) for _ in range(2)]

    dma(cs2, rope.rearrange("p (a i d) -> p a i d", a=2, i=32), "cs2", w=["cs2"])
    for g in range(4):
        dma(gqk[:, g * 64:(g + 1) * 64], (q_norm if g < 2 else k_norm).partition_broadcast(128), "gqk", w=[f"gqk{g}"])
    ts("dve", gqk[:, 0:128], gqk[:, 0:128], 0.125, ALU.mult, r=["gqk0", "gqk1"], w=["gqk0", "gqk1"])
    dma(aon, a_out_norm.partition_broadcast(128), "aon", w=["aon"])
    ts("dve", aon, aon, 1.0 - LAM_INIT, ALU.mult, r=["aon"], w=["aon"])
    dma(lamt, lam4.partition_broadcast(128), "lamt", w=["lamt"])
    tt("dve", lamp[:, 0:64], lamt[:, 0:64], lamt[:, 64:128], ALU.mult, r=["lamt"], w=["lamp"])
    tt("dve", lamp[:, 64:128], lamt[:, 128:192], lamt[:, 192:256], ALU.mult, r=["lamt"], w=["lamp"])
    red(smallv[:, 0:2], lamp.rearrange("p (a b) -> p a b", a=2), ALU.add, r=["lamp"], w=["lam0"])
    act(smallv[:, 4:6], smallv[:, 0:2], AF.Exp, r=["lam0"], w=["lam1"])
    tt("dve", smallv[:, 2:3], smallv[:, 4:5], smallv[:, 5:6], ALU.subtract, r=["lam1"], w=["lam2"])
    ts("dve", smallv[:, 3:4], smallv[:, 2:3], LAM_INIT, ALU.add, r=["lam2"], w=["neglam"], s2=-1.0, op1=ALU.mult)
    neglam = smallv[:, 3:4]
    for sl in range(2):
        mset("dve", vaug[sl][:, :, 128:130], 1.0, w=[f"vaug1_{sl}"])

    def load_head_w(h):
        sl = h % 2
        for n, off in enumerate((OFF_AQ, OFF_AK, OFF_AV)):
            dma(wst[:, :, n * 128:(n + 1) * 128], w_in_v[:, :, off + h * 128:off + (h + 1) * 128],
                "wst", w=[f"wst_{n}"])
        cp("pool", wbf[sl], wst, r=[f"wst_{n}" for n in range(3)], w=[f"wbf{sl}"])

    def b2_tile(h, i, pb=7):
        sl = h % 2
        b = i % 3
        pj = PS[pb]
        for c in range(KC):
            mm(pj[:, 0:384], xnT[:, c, i * 128:(i + 1) * 128], wbf[sl][:, c, :], c == 0, c == KC - 1,
               r=[f"wbf{sl}"], w=[f"ps{pb}"])
        cp("act", pjs[b], pj[:, 0:384], r=[f"ps{pb}"], w=[f"pjs{b}"])
        cp("pool", vaug[sl][:, i, 0:128], pjs[b][:, 256:384], r=[f"pjs{b}"], w=[f"vaug{sl}_{i}"])
        tt("pool", sqb[b], pjs[b][:, 0:256], pjs[b][:, 0:256], ALU.mult, r=[f"pjs{b}"], w=[f"sqb{b}"])
        red(st4[b][:, 0:4], sqb[b].rearrange("p (g d) -> p g d", g=4), ALU.add, r=[f"sqb{b}"], w=[f"st4a{b}"])
        act(st4[b][:, 4:8], st4[b][:, 0:4], AF.Ln, r=[f"st4a{b}"], w=[f"st4b{b}"], bias=EPS, scale=1.0 / 64)
        act(st4[b][:, 8:12], st4[b][:, 4:8], AF.Exp, r=[f"st4b{b}"], w=[f"st4c{b}"], scale=-0.5)
        for g in range(4):
            stt(xnq[b][:, g * 64:(g + 1) * 64], pjs[b][:, g * 64:(g + 1) * 64], st4[b][:, 8 + g:9 + g],
                gqk[:, g * 64:(g + 1) * 64], ALU.mult, ALU.mult,
                r=[f"pjs{b}", f"st4c{b}", f"gqk{g}"], w=[f"xnq{b}"])
        x3 = xnq[b].rearrange("p (g d) -> p g d", g=4)
        a3 = rA[b].rearrange("p (g d) -> p g d", g=4)
        b3 = rB[b].rearrange("p (g d) -> p g d", g=4)
        cosb = cs2[:, 0, i, :].unsqueeze(1).broadcast_to([128, 4, 64])
        sin_lo = cs2[:, 1, i, 0:32].unsqueeze(1).broadcast_to([128, 4, 32])
        sin_hi = cs2[:, 1, i, 32:64].unsqueeze(1).broadcast_to([128, 4, 32])
        tt("dve", a3, x3, cosb, ALU.mult, r=[f"xnq{b}", "cs2"], w=[f"rA{b}"])
        tt("pool", b3[:, :, 0:32], x3[:, :, 32:64], sin_lo, ALU.mult, r=[f"xnq{b}", "cs2"], w=[f"rB{b}"])
        tt("pool", b3[:, :, 32:64], x3[:, :, 0:32], sin_hi, ALU.mult, r=[f"xnq{b}", "cs2"], w=[f"rB{b}"])
        tt("dve", qkb[b], rA[b], rB[b], ALU.add, r=[f"rA{b}", f"rB{b}"], w=[f"qkb{b}"])

    def b2_tile_p2(h, i, pb=7):
        sl = h % 2
        b = i % 3
        tr(psb(pb, 768, 896), qkb[b][:, 0:128], identb, r=[f"qkb{b}"], w=[f"ps{pb}"])
        tr(psb(pb, 896, 1024), qkb[b][:, 128:256], identb, r=[f"qkb{b}"], w=[f"ps{pb}"])
        cp("act", qkT[sl][:, :, i * 128:(i + 1) * 128], psb(pb, 768, 1024).rearrange("p (a t) -> p a t", a=2),
           r=[f"ps{pb}"], w=[f"qkT{sl}"])

    acc3 = psall[:, 4 * 512:7 * 512].rearrange("p (b n) -> p b n", b=3)

    def b3_epilogue(h, g):
        k2 = g % 2
        a9 = accs[k2]
        a3 = a9.rearrange("p a n -> p (a n)").rearrange("p (b n) -> p b n", b=3)
        e_ = e8[k2]
        cp("act", a3, acc3[:, :, 0:387], r=["ps4", "ps5", "ps6"], w=[f"accs{k2}"])
        recip(e_[:, 0:8], a9[:, 0:8, 128], r=[f"accs{k2}"], w=[f"e8a{k2}"])
        ts("dve", e_[:, 8:12], e_[:, 4:8], neglam, ALU.mult, r=[f"e8a{k2}", "neglam"], w=[f"e8b{k2}"])
        o_ = od4[k2]
        tt("pool", o_, a9[:, 0:4, 0:128], e_[:, 0:4].unsqueeze(2).broadcast_to([128, 4, 128]), ALU.mult,
           r=[f"accs{k2}", f"e8a{k2}"], w=[f"od4{k2}"])
        tt("pool", t24, a9[:, 4:8, 0:128], e_[:, 8:12].unsqueeze(2).broadcast_to([128, 4, 128]), ALU.mult,
           r=[f"accs{k2}", f"e8b{k2}"], w=["t24"])
        tt("dve", o_, o_, t24, ALU.add, r=[f"od4{k2}", "t24"], w=[f"od4{k2}"])
        tt("pool", sq4, o_, o_, ALU.mult, r=[f"od4{k2}"], w=["sq4"])
        red(e_[:, 12:16], sq4, ALU.add, r=["sq4"], w=[f"e8c{k2}"])
        act(e_[:, 12:16], e_[:, 12:16], AF.Ln, r=[f"e8c{k2}"], w=[f"e8c{k2}"], bias=EPS, scale=1.0 / 128)
        act(e_[:, 16:20], e_[:, 12:16], AF.Exp, r=[f"e8c{k2}"], w=[f"e8d{k2}"], scale=-0.5)
        tt("dve", o_, o_, e_[:, 16:20].unsqueeze(2).broadcast_to([128, 4, 128]), ALU.mult,
           r=[f"od4{k2}", f"e8d{k2}"], w=[f"od4{k2}"])
        tt("pool", ob4[k2], o_, aon.unsqueeze(1).broadcast_to([128, 4, 128]), ALU.mult,
           r=[f"od4{k2}", "aon"], w=[f"ob4{k2}"])

    def b3_epilogue_p2(h, g):
        k2 = g % 2
        for qt in range(4):
            tr(psb(7, qt * 128, (qt + 1) * 128), ob4[k2][:, qt, :], identb, r=[f"ob4{k2}"], w=["ps7"])
        cp("act", hbst[k2], psb(7, 0, 512), r=["ps7"], w=[f"hbst{k2}"])
        dma(hbT_v[:, h, g * 512:(g + 1) * 512], hbst[k2], f"hbst{k2}", r=[f"hbst{k2}"], w=[f"hbT{h}_{g}"])

    def b3_head(h, inter):
        sl = h % 2
        steps = [(g, j) for g in range(8) for j in range(NT)]
        ns = len(steps)

        def qk(s):
            g, j = steps[s]
            pb_ = 2 * (s % 2)
            for c in range(2):
                mm(PS[pb_ + c], qkT[sl][c * 64:(c + 1) * 64, 1, j * 128:(j + 1) * 128],
                   qkT[sl][c * 64:(c + 1) * 64, 0, g * 512:(g + 1) * 512], True, True,
                   r=[f"qkT{sl}"], w=[f"ps{pb_ + c}"], cost=170.0)
            act(pT[s % 3], psall[:, pb_ * 512:(pb_ + 2) * 512], AF.Exp, r=[f"ps{pb_}", f"ps{pb_ + 1}"],
                w=[f"pT{s % 3}"])

        pending = []
        qk(0)
        for s in range(ns):
            if s + 1 < ns:
                qk(s + 1)
            g, j = steps[s]
            for a in range(8):
                c, qt = divmod(a, 4)
                bank = 4 + a // 3
                col = (a % 3) * 129
                mm(PS[bank][:, col:col + 129], pT[s % 3][:, c * 512 + qt * 128:c * 512 + (qt + 1) * 128],
                   vaug[sl][:, j, 0:129], j == 0 and a % 3 == 0, j == NT - 1,
                   r=[f"pT{s % 3}", f"vaug{sl}_{j}", f"vaug1_{sl}"], w=[f"ps{bank}"], sgc=True)
            for _dm in range(NDUM):
                mm(PS[6][:, 258:387], pT[s % 3][:, 896:1024], vaug[sl][:, j, 0:129], False, j == NT - 1,
                   r=[f"pT{s % 3}", f"vaug{sl}_{j}", f"vaug1_{sl}"], w=["ps6"], sgc=True)
            for it in list(pending):
                if it[0] <= s:
                    pending.remove(it)
                    it[1]()
            if j == NT - 1:
                b3_epilogue(h, g)
                pending.append([s + 7, (lambda gg: (lambda: b3_epilogue_p2(h, gg)))(g)])
            if s % 8 == 2 and inter:
                p1, p2 = inter.pop(0)
                p1()
                if p2 is not None:
                    pending.append([s + 5, p2])
        for it in pending:
            it[1]()
        while inter:
            p1, p2 = inter.pop(0)
            p1()
            if p2 is not None:
                p2()

    nheads = 0 if "B" in skip else 8
    if nheads:
        load_head_w(0)
        load_head_w(1)
        for i in range(NT):
            b2_tile(0, i, pb=i % 8)
            if i >= 2:
                b2_tile_p2(0, i - 2, pb=(i - 2) % 8)
        for i in range(NT - 2, NT):
            b2_tile_p2(0, i, pb=i % 8)
    for h in range(nheads):
        inter = []
        if h + 1 < nheads:
            inter = [((lambda hh, ii: (lambda: b2_tile(hh, ii)))(h + 1, i),
                      (lambda hh, ii: (lambda: b2_tile_p2(hh, ii)))(h + 1, i)) for i in range(NT)]
            if h + 2 < nheads:
                inter.insert(NT // 2, ((lambda hh: (lambda: load_head_w(hh)))(h + 2), None))
        b3_head(h, inter)
    S.barrier()
    if stop_after == "B":
        return finish(nc, S, ctx, out)

    AR.off = base_persist
    wgst = AR.alloc([128, KC, 16], F32)
    wgb = AR.alloc([128, KC, 16], BF16)
    G = AR.alloc([128, NT, 16], F32)
    gb16 = AR.alloc([128, 16], F32)

    def arr():
        return AR.alloc([128, 2, 128], F32)

    SPl, NB, GTn, E_, EMAX, MN, MP, A1, A2, W1, W2, UP, IW, ED, T1, T2 = [arr() for _ in range(16)]
    emT = AR.alloc([128, 2], F32)
    dg1 = AR.alloc([128, 128], F32)
    dg = [dg1, dg1]
    zero4 = AR.alloc([128, 4], F32)
    mst = AR.alloc([128, KC, 128], F32)
    wmD = [AR.alloc([128, KC, 768], BF16) for _ in range(2)]
    qT = AR.alloc([128, T], BF16)
    kT = AR.alloc([128, T], BF16)
    ktok = AR.alloc([128, NT, 128], BF16)
    vaug2 = AR.alloc([128, NT, 258], BF16)
    hfwd = AR.alloc([128, NT, 256], F32)
    CnD = [AR.alloc([128, 264], F32) for _ in range(2)]
    CbD = [[AR.alloc([128, 264], BF16) for _ in range(2)] for _ in range(2)]
    STD = [[AR.alloc([128, 128], BF16) for _ in range(3)] for _ in range(2)]
    kwD = [[AR.alloc([128, 128], BF16) for _ in range(2)] for _ in range(2)]
    hm = [AR.alloc([128, 256], F32) for _ in range(2)]
    sg = [AR.alloc([128, 256], F32) for _ in range(2)]
    hn = [AR.alloc([128, 256], F32) for _ in range(2)]
    hab = [AR.alloc([128, 256], BF16) for _ in range(2)]
    hast = [AR.alloc([128, 256], BF16) for _ in range(2)]
    mnbc = AR.alloc([128, D], F32)
    junk2 = AR.alloc([128, 256], BF16)
    dnn = [AR.alloc([128, 8], F32) for _ in range(4)]

    def v4(a):
        return a.rearrange("p d (c h) -> p d c h", c=NT)

    Gv = G.rearrange("p i (t h) -> p i t h", t=4)
    dma(wgst, w_in_v[:, :, OFF_MG:OFF_MG + 16], "wgst", w=["wgst"])
    cp("pool", wgb, wgst, r=["wgst"], w=["wgb"])
    dma(gb16, gate_bias.partition_broadcast(128), "gb16", w=["gb16"])
    dma(mnbc, m_out_norm.partition_broadcast(128), "mnbc", w=["mnbc"])
    mset("pool", vaug2[:, :, 256:258], 1.0, w=["vaug2one"])
    mset("pool", zero4, 0.0, w=["zero4"])
    for i in range(NT):
        for c in range(KC):
            mm(PS[0][:, i * 16:(i + 1) * 16], xnT[:, c, i * 128:(i + 1) * 128], wgb[:, c, :], c == 0, c == KC - 1,
               r=["wgb"], w=["ps0"])
    tt("dve", G, PS[0][:, :].rearrange("p (i k) -> p i k", i=NT), gb16.unsqueeze(1).broadcast_to([128, NT, 16]),
       ALU.add, r=["ps0", "gb16"], w=["G"])
    for d in range(2):
        act(v4(SPl)[:, d], Gv[:, :, 2 * d + 1, :], AF.Exp, r=["G"], w=[f"SPl{d}"], scale=-1.0)
        act(SPl[:, d], SPl[:, d], AF.Ln, r=[f"SPl{d}"], w=[f"SPl{d}"], bias=1.0, scale=1.0)
        mm(PS[1][:, d * 128:(d + 1) * 128], U[d], SPl[:, d], True, True, r=[f"SPl{d}"], w=["ps1"])
        mm(PS[2][:, d * 128:(d + 1) * 128], onesf, SPl[:, d], True, True, r=[f"SPl{d}"], w=["ps2"])
    cp("dve", NB, PS[1][:, 0:256].rearrange("p (d k) -> p d k", d=2), r=["ps1"], w=["NB"])
    cp("dve", GTn, PS[2][:, 0:256].rearrange("p (d k) -> p d k", d=2), r=["ps2"], w=["GTn"])
    for d in range(2):
        tt("dve", v4(E_)[:, d], v4(NB)[:, d], Gv[:, :, 2 * d, :], ALU.add, r=["NB", "G"], w=[f"E{d}"])
        tr(PS[3][:, d * 128:(d + 1) * 128], E_[:, d], identf, r=[f"E{d}"], w=["ps3"])
        S.add("dve", (lambda dd: (lambda e: e.tensor_reduce(out=emT[:, dd:dd + 1], in_=PS[3][:, dd * 128:(dd + 1) * 128],
                                                            axis=AX.X, op=ALU.max)))(d), ["ps3"], [f"emT{d}"])
        ts("dve", dg[d], identf, emT[:, d:d + 1], ALU.mult, r=[f"emT{d}"], w=["dg"])
        mm(PS[4][:, d * 128:(d + 1) * 128], onesf, dg[d], True, True, r=["dg"], w=["ps4"])
    cp("dve", EMAX, PS[4][:, 0:256].rearrange("p (d k) -> p d k", d=2), r=["ps4"], w=["EMAX"])
    if cstage == 0:
        return finish(nc, S, ctx, out)
    for d in range(2):
        order = list(range(NT)) if d == 0 else list(range(NT - 1, -1, -1))
        prev = zero4
        pk = "zero4"
        for c in order:
            sl4 = slice(c * 4, (c + 1) * 4)
            tt("dve", T1[:, d, sl4], prev, EMAX[:, d, sl4], ALU.max, r=[pk, "EMAX"], w=[f"T1s{d}_{c}"])
            tt("dve", MN[:, d, sl4], T1[:, d, sl4], GTn[:, d, sl4], ALU.subtract, r=[f"T1s{d}_{c}", "GTn"], w=[f"MN{d}_{c}"])
            prev = MN[:, d, sl4]
            pk = f"MN{d}_{c}"
    mnk = [f"MN{d}_{c}" for d in range(2) for c in range(NT)]
    cp("dve", MP[:, 0, 4:128], MN[:, 0, 0:124], r=mnk, w=["MP"])
    mset("dve", MP[:, 0, 0:4], 0.0, w=["MP"])
    cp("dve", MP[:, 1, 0:124], MN[:, 1, 4:128], r=mnk, w=["MP"])
    mset("dve", MP[:, 1, 124:128], 0.0, w=["MP"])
    tt("dve", T1, MP, MN, ALU.subtract, r=["MP"] + mnk, w=["T1"])
    tt("dve", T1, T1, GTn, ALU.subtract, r=["T1", "GTn"], w=["T1"])
    act(A1, T1, AF.Exp, r=["T1"], w=["A1"])
    tt("dve", T2, EMAX, MN, ALU.subtract, r=["EMAX"] + mnk, w=["T2"])
    tt("dve", T2, T2, GTn, ALU.subtract, r=["T2", "GTn"], w=["T2"])
    act(A2, T2, AF.Exp, r=["T2"], w=["A2"])
    tt("dve", T1, E_, EMAX, ALU.subtract, r=["E0", "E1", "EMAX", "A1"], w=["T1"])
    act(W1, T1, AF.Exp, r=["T1"], w=["W1"])
    tt("dve", UP, EMAX, MP, ALU.max, r=["EMAX", "MP"], w=["UP"])
    tt("dve", T2, E_, UP, ALU.subtract, r=["E0", "E1", "UP", "A2"], w=["T2"])
    act(W2, T2, AF.Exp, r=["T2"], w=["W2"])
    tt("dve", T1, MP, UP, ALU.subtract, r=["MP", "UP", "W1"], w=["T1"])
    act(IW, T1, AF.Exp, r=["T1"], w=["IW"])
    tt("dve", T2, NB, UP, ALU.subtract, r=["NB", "UP", "W2"], w=["T2"])
    act(ED, T2, AF.Exp, r=["T2"], w=["ED"])

    if cstage == 1:
        return finish(nc, S, ctx, out)
    KSC = 128.0 ** -0.5
    def load_mw(h):
        wm_ = wmD[h % 2]
        q_ = h % 2
        subs = ((OFF_MQ + h * 128, 0, 0), (OFF_MK + h * 128, 128, 0),
                (OFF_MV + h * 256, 256, 1), (OFF_MV + h * 256 + 128, 384, 1),
                (OFF_MO + h * 256, 512, 2), (OFF_MO + h * 256 + 128, 640, 2))
        for off, dc, part in subs:
            dma(mst, w_in_v[:, :, off:off + 128], "mst", w=["mst"])
            cp("pool", wm_[:, :, dc:dc + 128], mst, r=["mst"], w=[f"wm{part}_{q_}"])

    nh_c = 4 if "C" not in skip else 0
    if nh_c:
        load_mw(0)
    for h in range(nh_c):
        wm = wmD[h % 2]
        wq = h % 2
        if h + 1 < nh_c:
            load_mw(h + 1)
        if cstage == 10:
            return finish(nc, S, ctx, out)
        n = 0
        for tb in range(8):
            for which, dst, scl in ((0, qT, 1.0), (1, kT, KSC)):
                bk = n % 2
                n += 1
                for c in range(KC):
                    mm(PS[bk][:, :], wm[:, c, which * 128:(which + 1) * 128], xnT[:, c, tb * 512:(tb + 1) * 512],
                       c == 0, c == KC - 1, r=[f"wm0_{wq}"], w=[f"ps{bk}"])
                act(dst[:, tb * 512:(tb + 1) * 512], PS[bk][:, :], AF.Identity, r=[f"ps{bk}"],
                    w=[("qT" if which == 0 else "kT")], scale=scl)
        if cstage == 11:
            return finish(nc, S, ctx, out)
        for i in range(NT):
            bk = 2 + i % 2
            for c in range(KC):
                mm(PS[bk][:, 0:384], xnT[:, c, i * 128:(i + 1) * 128], wm[:, c, 128:512], c == 0, c == KC - 1,
                   r=[f"wm0_{wq}", f"wm1_{wq}"], w=[f"ps{bk}"])
            cp("act", vaug2[:, i, 0:256], PS[bk][:, 128:384], r=[f"ps{bk}"], w=[f"v2_{i}"])
            ts("dve", ktok[:, i, :], PS[bk][:, 0:128], KSC, ALU.mult, r=[f"ps{bk}"], w=[f"ktok{i}"])
        if cstage == 2:
            return finish(nc, S, ctx, out)
        orders = [list(range(NT)), list(range(NT - 1, -1, -1))]
        for d in range(2):
            mset("pool", CnD[d], 0.0, w=[f"Cn{d}"])
            mset("pool", CbD[d][0], 0.0, w=[f"Cb{d}_0"])

        def pre(d, idx):
            c = orders[d][idx]
            bk = 4 + d
            mm(PS[bk][:, 0:128], kT[:, c * 128:(c + 1) * 128], qT[:, c * 128:(c + 1) * 128], True, True,
               r=["qT", "kT"], w=[f"ps{bk}"])
            stt(STD[d][idx % 3], PS[bk][:, 0:128], W2[:, d, c * 4 + h:c * 4 + h + 1], U[d], ALU.mult, ALU.mult,
                r=[f"ps{bk}", "W2"], w=[f"ST{d}_{idx % 3}"])

        def chunk(d, idx, ncomb):
            c = orders[d][idx]
            hb_ = 6 + d
            k2 = idx % 2
            dk4 = (2 * idx + d) % 4
            dn = dnn[dk4]
            col = c * 4 + h
            Cn_ = CnD[d]
            mm(PS[hb_][:, 0:257], STD[d][idx % 3], vaug2[:, c, 0:257], True, False,
               r=[f"ST{d}_{idx % 3}", f"v2_{c}", "vaug2one"], w=[f"ps{hb_}"])
            mm(PS[hb_][:, 0:257], qT[:, c * 128:(c + 1) * 128], CbD[d][k2][:, 0:257], False, True,
               r=["qT", f"Cb{d}_{k2}"], w=[f"ps{hb_}"])
            if idx + 1 < NT:
                cn_ = orders[d][idx + 1]
                kw_ = kwD[d][idx % 2]
                ts("pool", kw_, ktok[:, c, :], W1[:, d, col:col + 1], ALU.mult,
                   r=[f"ktok{c}", "W1"], w=[f"kw{d}_{idx % 2}"], s2=1.0, op1=ALU.mult)
                mm(PS[1][:, 0:257], kw_, vaug2[:, c, 0:257], True, True,
                   r=[f"kw{d}_{idx % 2}", f"v2_{c}", "vaug2one"], w=["ps1"])
                ts("dve", Cn_[:, 0:257], Cn_[:, 0:257], A1[:, d, col:col + 1], ALU.mult, r=[f"Cn{d}", "A1"], w=[f"Cn{d}"])
                stt(Cn_[:, 0:257], PS[1][:, 0:257], A2[:, d, col:col + 1], Cn_[:, 0:257], ALU.mult, ALU.add,
                    r=["ps1", "A2", f"Cn{d}"], w=[f"Cn{d}"])
                k3 = (idx + 1) % 2
                ts("pool", CbD[d][k3][:, 0:257], Cn_[:, 0:257], IW[:, d, cn_ * 4 + h:cn_ * 4 + h + 1], ALU.mult,
                   r=[f"Cn{d}", "IW"], w=[f"Cb{d}_{k3}"], s2=1.0, op1=ALU.mult)
            ts("dve", dn[:, 5:6], PS[hb_][:, 256:257], -1.0, ALU.mult, r=[f"ps{hb_}", "ED"], w=[f"dn{dk4}z"],
               s2=ED[:, d, col:col + 1], op1=ALU.max)
            stt(dn[:, 0:1], PS[hb_][:, 256:257], 1.0, dn[:, 5:6], ALU.mult, ALU.max,
                r=[f"ps{hb_}", f"dn{dk4}z"], w=[f"dn{dk4}a"])
            recip(dn[:, 1:2], dn[:, 0:1], r=[f"dn{dk4}a"], w=[f"dn{dk4}b"])
            if idx < NT // 2:
                ts("dve", hfwd[:, c, :], PS[hb_][:, 0:256], dn[:, 1:2], ALU.mult, r=[f"ps{hb_}", f"dn{dk4}b"], w=[f"hf{c}"])
                return
            k2 = ncomb % 2
            stt(hm[k2], PS[hb_][:, 0:256], dn[:, 1:2], hfwd[:, c, :], ALU.mult, ALU.add,
                r=[f"ps{hb_}", f"dn{dk4}b", f"hf{c}"], w=[f"hm{k2}"])
            act(junk2, hm[k2], AF.Square, r=[f"hm{k2}"], w=["junk2", f"dn{dk4}c"], accum=dn[:, 2:3])
            act(dn[:, 3:4], dn[:, 2:3], AF.Ln, r=[f"dn{dk4}c"], w=[f"dn{dk4}d"], bias=EPS, scale=1.0 / 256)
            act(dn[:, 4:5], dn[:, 3:4], AF.Exp, r=[f"dn{dk4}d"], w=[f"dn{dk4}e"], scale=-0.5)
            mb = 2 + ncomb % 2
            for cc in range(KC):
                mm(PS[mb][:, 0:256], xnT[:, cc, c * 128:(c + 1) * 128], wm[:, cc, 512:768], cc == 0, cc == KC - 1,
                   r=[f"wm2_{wq}"], w=[f"ps{mb}"])
            cp("dve", sg[k2], PS[mb][:, 0:256], r=[f"ps{mb}"], w=[f"sg{k2}"])
            act(sg[k2], sg[k2], AF.Exp, r=[f"sg{k2}"], w=[f"sg{k2}"], scale=-1.0)
            act(sg[k2], sg[k2], AF.Ln, r=[f"sg{k2}"], w=[f"sg{k2}"], bias=1.0, scale=1.0)
            act(sg[k2], sg[k2], AF.Exp, r=[f"sg{k2}"], w=[f"sg{k2}"], scale=-1.0)
            stt(hn[k2], hm[k2], dn[:, 4:5], mnbc[:, h * 256:(h + 1) * 256], ALU.mult, ALU.mult,
                r=[f"hm{k2}", f"dn{dk4}e", "mnbc"], w=[f"hn{k2}"])
            tt("pool", hab[k2], hn[k2], sg[k2], ALU.mult, r=[f"hn{k2}", f"sg{k2}"], w=[f"hab{k2}"])
            tr(psb(0, 0, 128), hab[k2][:, 0:128], identb, r=[f"hab{k2}"], w=["ps0"])
            tr(psb(0, 128, 256), hab[k2][:, 128:256], identb, r=[f"hab{k2}"], w=["ps0"])
            cp("dve", hast[k2], psb(0, 0, 256), r=["ps0"], w=[f"hast{k2}"])
            dma(haT_v[:, 2 * h:2 * h + 2, c * 128:(c + 1) * 128], hast[k2].rearrange("p (a t) -> p a t", a=2),
                f"hast{k2}", r=[f"hast{k2}"], w=[f"haT{h}_{c}"])

        pre(0, 0)
        pre(1, 0)
        ncomb = 0
        for idx in range(NT):
            for d in range(2):
                if idx + 1 < NT:
                    pre(d, idx + 1)
                chunk(d, idx, ncomb)
                if idx >= NT // 2:
                    ncomb += 1
    S.barrier()
    if stop_after == "C":
        return finish(nc, S, ctx, out)

    AR.off = base_persist
    wst1 = [AR.alloc([128, KC, 256], F32) for _ in range(2)]
    wmat = [AR.alloc([128, KC, D], BF16) for _ in range(4)]
    hAB = [[AR.alloc([128, KC, 256], BF16) for _ in range(2)] for _ in range(2)]
    eg = [AR.alloc([128, 512], F32) for _ in range(2)]
    y1 = [AR.alloc([128, 512], F32) for _ in range(2)]
    yTb = [AR.alloc([128, KC, 256], BF16) for _ in range(2)]
    srcs = [p_a.rearrange("(c p) n -> p c n", p=128), p_b.rearrange("(c p) n -> p c n", p=128),
            w_in_v[:, :, OFF_GA:OFF_GA + D], w_in_v[:, :, OFF_GB:OFF_GB + D]]
    npc = 0
    for pc in range(4):
        for wi in range(4):
            k = npc % 2
            npc += 1
            dma(wst1[k], srcs[wi][:, :, pc * 256:(pc + 1) * 256], f"wst1_{k}", w=[f"wst1_{k}"])
            cp("pool", wmat[wi][:, :, pc * 256:(pc + 1) * 256], wst1[k], r=[f"wst1_{k}"], w=[f"wmat{wi}_{pc}"])
    for b in range(16):
        sl = b % 2
        t0 = b * 256
        dma(hAB[0][sl], haT_v[:, :, t0:t0 + 256], f"hA{sl}", w=[f"hA{sl}"])
        dma(hAB[1][sl], hbT_v[:, :, t0:t0 + 256], f"hB{sl}", w=[f"hB{sl}"])
        for m in range(8):
            yb_ = m % 2
            gk = 2 + m % 2
            for half, (wi, rk) in enumerate(((0, f"hA{sl}"), (1, f"hB{sl}"))):
                for c in range(KC):
                    mm(PS[yb_][:, half * 256:(half + 1) * 256], wmat[wi][:, c, m * 128:(m + 1) * 128],
                       hAB[half][sl][:, c, :], c == 0, c == KC - 1, r=[f"wmat{wi}_{m // 2}", rk], w=[f"ps{yb_}"])
            for half, wi in enumerate((2, 3)):
                for c in range(KC):
                    mm(PS[gk][:, half * 256:(half + 1) * 256], wmat[wi][:, c, m * 128:(m + 1) * 128],
                       xnT[:, c, t0:t0 + 256], c == 0, c == KC - 1, r=[f"wmat{wi}_{m // 2}"], w=[f"ps{gk}"])
            act(eg[m % 2], PS[gk][:, :], AF.Exp, r=[f"ps{gk}"], w=[f"eg{m % 2}"], scale=-1.0)
            act(eg[m % 2], eg[m % 2], AF.Ln, r=[f"eg{m % 2}"], w=[f"eg{m % 2}"], bias=1.0, scale=1.0)
            act(eg[m % 2], eg[m % 2], AF.Exp, r=[f"eg{m % 2}"], w=[f"eg{m % 2}"], scale=-1.0)
            tt("dve", y1[m % 2], PS[yb_][:, :], eg[m % 2], ALU.mult, r=[f"ps{yb_}", f"eg{m % 2}"], w=[f"y1{m % 2}"])
            tt("pool", yTb[sl][:, m, :], y1[m % 2][:, 0:256], y1[m % 2][:, 256:512], ALU.add,
               r=[f"y1{m % 2}"], w=[f"yTb{sl}_{m}"])
        dma(yT_v[:, :, t0:t0 + 256], yTb[sl], f"yTb{sl}", r=[f"yTb{sl}_{m}" for m in range(8)], w=[f"yTs{b}"])
    S.barrier()
    if stop_after == "D1":
        return finish(nc, S, ctx, out)

    AR.off = base_consts
    wup = AR.alloc([128, KC, 2 * DFF], BF16)
    wdn = AR.alloc([128, NJ, D], BF16)
    base_e = AR.off
    wst2 = [AR.alloc([128, KC, 256], F32) for _ in range(2)]
    wo = AR.alloc([128, KC, D], BF16)
    yTt = [AR.alloc([128, KC, 128], BF16) for _ in range(2)]
    xt2 = [AR.alloc([128, D], F32) for _ in range(2)]
    x1t = [AR.alloc([128, D], F32) for _ in range(2)]
    g2bc = AR.alloc([128, D], F32)
    xn2b = [AR.alloc([128, D], BF16) for _ in range(2)]
    xn2st = [AR.alloc([128, D], BF16) for _ in range(2)]
    junk3 = AR.alloc([128, D], BF16)
    st2 = [AR.alloc([128, 4], F32) for _ in range(2)]
    end_d2 = AR.off
    wo_src = w_o.rearrange("(c p) n -> p c n", p=128)
    wup_src = w_up.rearrange("(c p) n -> p c n", p=128)
    wdn_src = w_down.rearrange("(j p) n -> p j n", p=128)
    npc = 0
    for pc in range(4):
        k = npc % 2
        npc += 1
        dma(wst2[k], wo_src[:, :, pc * 256:(pc + 1) * 256], f"wst2_{k}", w=[f"wst2_{k}"])
        cp("pool", wo[:, :, pc * 256:(pc + 1) * 256], wst2[k], r=[f"wst2_{k}"], w=["wo"])
    dma(g2bc, norm2.partition_broadcast(128), "g2bc", w=["g2bc"])
    ffn_pieces = [("up", pc) for pc in range(22)] + [("dn", pc) for pc in range(11)]

    def load_ffn_piece(kind, pc):
        nonlocal npc
        k = npc % 2
        npc += 1
        if kind == "up":
            dma(wst2[k], wup_src[:, :, pc * 256:(pc + 1) * 256], f"wst2_{k}", w=[f"wst2_{k}"])
            cp("pool", wup[:, :, pc * 256:(pc + 1) * 256], wst2[k], r=[f"wst2_{k}"], w=["wup"])
        else:
            dma(wst2[k].rearrange("p (a b) n -> p a (b n)", a=2), wdn_src[:, 2 * pc:2 * pc + 2, :], f"wst2_{k}",
                w=[f"wst2_{k}"])
            cp("pool", wdn[:, 2 * pc:2 * pc + 2, :], wst2[k].rearrange("p (a b) n -> p a (b n)", a=2),
               r=[f"wst2_{k}"], w=["wdn"])

    for i in range(NT):
        b = i % 2
        dma(yTt[b], yT_v[:, :, i * 128:(i + 1) * 128], f"yTt{b}", w=[f"yTt{b}"])
        dma(xt2[b], x[i * 128:(i + 1) * 128, :], f"xt2{b}", w=[f"xt2{b}"])
        if i < len(ffn_pieces):
            load_ffn_piece(*ffn_pieces[i])
        for n in range(2):
            bk = 2 * b + n
            for m in range(KC):
                mm(PS[bk][:, :], yTt[b][:, m, :], wo[:, m, n * 512:(n + 1) * 512], m == 0, m == KC - 1,
                   r=[f"yTt{b}", "wo"], w=[f"ps{bk}"])
            tt("dve", x1t[b][:, n * 512:(n + 1) * 512], PS[bk][:, :], xt2[b][:, n * 512:(n + 1) * 512], ALU.add,
               r=[f"ps{bk}", f"xt2{b}"], w=[f"x1t{b}_{n}"])
        dma(x1s[i * 128:(i + 1) * 128, :], x1t[b], f"x1t{b}", r=[f"x1t{b}_0", f"x1t{b}_1"], w=[f"x1s{i}"])
        act(junk3, x1t[b], AF.Square, r=[f"x1t{b}_0", f"x1t{b}_1"], w=["junk3", f"st2a{b}"], accum=st2[b][:, 0:1])
        act(st2[b][:, 1:2], st2[b][:, 0:1], AF.Ln, r=[f"st2a{b}"], w=[f"st2b{b}"], bias=EPS, scale=1.0 / D)
        act(st2[b][:, 2:3], st2[b][:, 1:2], AF.Exp, r=[f"st2b{b}"], w=[f"st2c{b}"], scale=-0.5)
        stt(xn2b[b], x1t[b], st2[b][:, 2:3], g2bc, ALU.mult, ALU.mult,
            r=[f"x1t{b}_0", f"x1t{b}_1", f"st2c{b}", "g2bc"], w=[f"xn2b{b}"])
        for c in range(KC):
            tr(psb(4 + b, c * 128, (c + 1) * 128), xn2b[b][:, c * 128:(c + 1) * 128], identb,
               r=[f"xn2b{b}"], w=[f"ps{4 + b}"])
        cp("dve", xn2st[b], psb(4 + b, 0, 1024), r=[f"ps{4 + b}"], w=[f"xn2st{b}"])
        dma(xn2T_v[:, :, i * 128:(i + 1) * 128], xn2st[b].rearrange("p (c t) -> p c t", c=KC), f"xn2st{b}",
            r=[f"xn2st{b}"], w=[f"xn2T{i}"])
    for kind, pc in ffn_pieces[NT:]:
        load_ffn_piece(kind, pc)
    S.barrier()
    if stop_after == "D2":
        return finish(nc, S, ctx, out)

    AR.off = base_e
    cw = AR.alloc([128, 44, 3], F32)
    cb = AR.alloc([128, 44], F32)
    xw = [AR.alloc([128, KC, 258], BF16) for _ in range(2)]
    actT = [AR.alloc([128, NJ, 256], BF16) for _ in range(2)]
    NB_E = 4
    ca = [AR.alloc([128, 256], F32) for _ in range(NB_E)]
    cg = [AR.alloc([128, 256], F32) for _ in range(NB_E)]
    x2 = [AR.alloc([128, 256], F32) for _ in range(NB_E)]
    zz = [AR.alloc([128, 256], F32) for _ in range(NB_E)]
    ez = [AR.alloc([128, 256], F32) for _ in range(NB_E)]
    x1e = [AR.alloc([128, D], F32) for _ in range(2)]
    ucp = [AR.alloc([128, 260], F32) for _ in range(2 * NB_E)]
    dma(cw, conv_wp.rearrange("p (j k) -> p j k", k=3), "cw", w=["cw"])
    dma(cb, conv_bp, "cb", w=["cb"])
    pend_down = []
    for b in range(16):
        sl = b % 2
        t0 = b * 256
        if b == 0:
            mset("pool", xw[sl][:, :, 0:2], 0.0, w=[f"xw{sl}", f"xw{sl}h"])
            dma(xw[sl][:, :, 1:258], xn2T_v[:, :, 0:257], f"xw{sl}", w=[f"xw{sl}"])
        elif b == 15:
            mset("pool", xw[sl][:, :, 256:258], 0.0, w=[f"xw{sl}", f"xw{sl}h"])
            dma(xw[sl][:, :, 0:257], xn2T_v[:, :, t0 - 1:T], f"xw{sl}", w=[f"xw{sl}"])
        else:
            dma(xw[sl], xn2T_v[:, :, t0 - 1:t0 + 257], f"xw{sl}", w=[f"xw{sl}", f"xw{sl}h"])
        for j in range(NJ):
            if j == 8 and pend_down:
                pend_down.pop(0)()
            s = j % NB_E
            for bank, ch, dst, dk_ in ((2 * (j % 2), j, ca[s], f"ca{s}"), (2 * (j % 2) + 1, NJ + j, cg[s], f"cg{s}")):
                for c in range(KC):
                    mm(PS[bank][:, 0:258], wup[:, c, ch * 128:(ch + 1) * 128], xw[sl][:, c, :], c == 0, c == KC - 1,
                       r=[f"xw{sl}", f"xw{sl}h"], w=[f"ps{bank}"])
                ui = 2 * s + (bank % 2)
                uc = ucp[ui][:, 0:258]
                act(uc, PS[bank][:, 0:258], AF.Copy, r=[f"ps{bank}"], w=[f"uc{ui}"])
                act(dst, uc[:, 1:257], AF.Identity, r=[f"uc{ui}", "cw", "cb"], w=[dk_],
                    scale=cw[:, ch, 1:2], bias=cb[:, ch:ch + 1])
                stt(dst, uc[:, 0:256], cw[:, ch, 0:1], dst, ALU.mult, ALU.add, r=[f"uc{ui}", dk_], w=[dk_])
                stt(dst, uc[:, 2:258], cw[:, ch, 2:3], dst, ALU.mult, ALU.add, r=[f"uc{ui}", dk_], w=[dk_])
            act(ez[s], cg[s], AF.Gelu_apprx_tanh, r=[f"cg{s}"], w=[f"ez{s}"])
            tt("dve", actT[sl][:, j, :], ez[s], ca[s], ALU.mult, r=[f"ez{s}", f"ca{s}"], w=[f"actT{sl}"])
        def down_proj(b=b, sl=sl):
            for t2 in range(2):
                tile_i = b * 2 + t2
                dma(x1e[t2], x1s[tile_i * 128:(tile_i + 1) * 128, :], f"x1e{t2}", w=[f"x1e{t2}"])
                for n in range(2):
                    bk = 4 + 2 * t2 + n
                    for j in range(NJ):
                        mm(PS[bk][:, :], actT[sl][:, j, t2 * 128:(t2 + 1) * 128], wdn[:, j, n * 512:(n + 1) * 512],
                           j == 0, j == NJ - 1, r=[f"actT{sl}"], w=[f"ps{bk}"])
                    tt("dve", x1e[t2][:, n * 512:(n + 1) * 512], PS[bk][:, :], x1e[t2][:, n * 512:(n + 1) * 512],
                       ALU.add, r=[f"ps{bk}", f"x1e{t2}"], w=[f"x1e{t2}"])
                dma(out[tile_i * 128:(tile_i + 1) * 128, :], x1e[t2], f"x1e{t2}", r=[f"x1e{t2}"], w=[f"out{tile_i}"])

        pend_down.append(down_proj)
    for f_ in pend_down:
        f_()
    return finish(nc, S, ctx, out)


def finish(nc, S, ctx, out, debug_dump=None):
    if debug_dump is not None:
        S.barrier()
        src, dst = debug_dump
        S.add("sp", lambda e: e.dma_start(out=dst, in_=src), (), (), dk="dbg", nbytes=1 << 23)
    S.barrier()
    for e_ in ENGS:
        S.add(e_, None, (), ())
    S.schedule()
    esem = {e: ctx.enter_context(nc.semaphore(f"se_{e}")) for e in ENGS if e != "sp"}
    dsem = {k: ctx.enter_context(nc.semaphore(f"sd_{k}")) for k in S.dma_n}
    with nc.Block() as block:
        @block.sync
        def _(e):
            S.emit("sp", e, esem, dsem)

        @block.scalar
        def _(e):
            S.emit("act", e, esem, dsem)

        @block.vector
        def _(e):
            S.emit("dve", e, esem, dsem)

        @block.gpsimd
        def _(e):
            S.emit("pool", e, esem, dsem)

        @block.tensor
        def _(e):
            S.emit("pe", e, esem, dsem)
    ctx.close()
    nc._est_ns = S.est_ns
    return nc


def host_consts():
    ident = np.eye(128, dtype=np.float32)
    ones = np.ones((128, 128), np.float32)
    s = np.arange(128)
    ufwd = (s[:, None] <= s[None, :]).astype(np.float32)
    ubwd = (s[:, None] >= s[None, :]).astype(np.float32)
    cst = np.concatenate([ident, ones, ufwd, ubwd], axis=1)
    pos = np.arange(T, dtype=np.float32)
    inv = (10000.0 ** (-np.arange(0, 64, 2, dtype=np.float32) / 64)).astype(np.float32)
    ang = pos[:, None] * inv[None, :]
    cos = np.cos(ang).astype(np.float32)
    sin = np.sin(ang).astype(np.float32)
    cos2 = np.concatenate([cos, cos], axis=1).reshape(32, 128, 64).transpose(1, 0, 2)
    sin2 = np.concatenate([-sin, sin], axis=1).reshape(32, 128, 64).transpose(1, 0, 2)
    rope = np.stack([cos2, sin2], axis=1).reshape(128, -1)
    return np.ascontiguousarray(cst), np.ascontiguousarray(rope.astype(np.float32))


def make_in_maps(inp, cores):
    cst, rope = host_consts()
    f = lambda a: np.ascontiguousarray(np.asarray(a, dtype=np.float32))
    shared = {
        "w_in": f(inp["w_in"][0]), "p_a": f(inp["p_a"][0]), "p_b": f(inp["p_b"][0]), "w_o": f(inp["w_o"][0]),
        "w_up": f(inp["w_up"][0]), "w_down": f(inp["w_down"][0]),
        "norm1": f(inp["norm1"]), "norm2": f(inp["norm2"]), "m_out_norm": f(inp["m_out_norm"]),
        "gate_bias": f(inp["gate_bias"]), "q_norm": f(inp["q_norm"]), "k_norm": f(inp["k_norm"]),
        "lam4": f(np.concatenate([inp["lam_q1"], inp["lam_k1"], inp["lam_q2"], inp["lam_k2"]], axis=1)),
        "a_out_norm": f(inp["a_out_norm"]),
        "conv_wp": f(np.asarray(inp["conv_w"])[0, :, 0, :].reshape(3, 44, 128).transpose(2, 1, 0).reshape(128, 132)),
        "conv_bp": f(np.asarray(inp["conv_b"])[0].reshape(44, 128).T),
        "cst": cst, "rope": rope,
    }
    maps = []
    for b in cores:
        m = dict(shared)
        m["x"] = f(inp["x"][b])
        maps.append(m)
    return maps


_NC = None


def kernel(**inputs):
    global _NC
    if _NC is None:
        _NC = build()
    maps = make_in_maps(inputs, list(range(8)))
    res = run_bass_kernel_spmd(_NC, maps, core_ids=list(range(8)))
    return np.stack([np.asarray(r["out"]) for r in res.results], axis=0).astype(np.float32)
```
